# Optimizing a Trainium2 kernel written in Bass

```python
import jax, jax.numpy as jnp
from jax import lax
import numpy as np

D_MODEL = 1024
BATCH = 8
SEQ = 4096
DEPTH = 2

MIX_WIDTH = 512
N_BRANCH = 4
MLA_HEADS = 8
QK_NOPE = 64
QK_ROPE = 32
V_HEAD = 64
Q_LORA = 384
KV_LORA = 256
ROPE_THETA = 10000.0
Q_BLOCK = 128
POOL_WINDOWS = (2, 4, 8, 16)
POOL_GROUP = MIX_WIDTH // 4
SSD_HEADS = 8
SSD_HEADDIM = 64
SSD_GROUPS = 2
SSD_STATE = 64
SSD_CHUNK = 128
CONV_WIDTH = 4
SSD_XBC = SSD_HEADS * SSD_HEADDIM + 2 * SSD_GROUPS * SSD_STATE
LRU_BLOCKS = 8
LRU_BLOCK = MIX_WIDTH // LRU_BLOCKS
LRU_C = 8.0
D_FF = 4 * D_MODEL
PLE_DIM = 256
EPS = 1e-6

SPLIT_SIZES = (Q_LORA, KV_LORA, QK_ROPE,
               MIX_WIDTH,
               MIX_WIDTH, SSD_XBC, SSD_HEADS,
               MIX_WIDTH, MIX_WIDTH,
               N_BRANCH * D_MODEL)
IN_COLS = sum(SPLIT_SIZES)

kernel_name = "hybrid_gated_mla_pool_ssd_rglru_block"


def _split_points():
    pts, acc = [], 0
    for s in SPLIT_SIZES[:-1]:
        acc += s
        pts.append(acc)
    return pts


def rmsnorm(x, g):
    x32 = x.astype(jnp.float32)
    y = x32 * lax.rsqrt(jnp.mean(x32 * x32, axis=-1, keepdims=True) + EPS)
    return (y * g.astype(jnp.float32)).astype(x.dtype)


def causal_dwconv(x, w, b):
    c = x.shape[-1]
    y = lax.conv_general_dilated(x, w[:, None, :].astype(x.dtype), window_strides=(1,),
                                 padding=[(CONV_WIDTH - 1, 0)],
                                 dimension_numbers=('NWC', 'WIO', 'NWC'),
                                 feature_group_count=c)
    return y + b.astype(x.dtype)


def rope_tables(positions):
    inv = 1.0 / (ROPE_THETA ** (jnp.arange(0, QK_ROPE, 2, dtype=jnp.float32) / QK_ROPE))
    ang = positions.astype(jnp.float32)[..., None] * inv
    return jnp.cos(ang), jnp.sin(ang)


def apply_rope(x, cos, sin):
    x32 = x.astype(jnp.float32)
    x1, x2 = jnp.split(x32, 2, axis=-1)
    out = jnp.concatenate([x1 * cos - x2 * sin, x2 * cos + x1 * sin], axis=-1)
    return out.astype(x.dtype)


def mla_mixer(c_q, c_kv, k_r, cos, sin, q_norm, w_uq, kv_norm, w_ukv):
    b, s, _ = c_q.shape
    q = (rmsnorm(c_q, q_norm) @ w_uq).reshape(b, s, MLA_HEADS, QK_NOPE + QK_ROPE)
    q_nope = q[..., :QK_NOPE]
    q_rope = apply_rope(q[..., QK_NOPE:], cos[:, :, None], sin[:, :, None])
    kv = (rmsnorm(c_kv, kv_norm) @ w_ukv).reshape(b, s, MLA_HEADS, QK_NOPE + V_HEAD)
    k_nope, v = kv[..., :QK_NOPE], kv[..., QK_NOPE:]
    k_rope = apply_rope(k_r, cos, sin)
    scale = (QK_NOPE + QK_ROPE) ** -0.5
    outs = []
    for blk in range(s // Q_BLOCK):
        q0, kend = blk * Q_BLOCK, (blk + 1) * Q_BLOCK
        sc = (jnp.einsum('bqhd,bkhd->bhqk', q_nope[:, q0:kend], k_nope[:, :kend])
              + jnp.einsum('bqhd,bkd->bhqk', q_rope[:, q0:kend], k_rope[:, :kend]))
        sc = sc.astype(jnp.float32) * scale
        qi = q0 + jnp.arange(Q_BLOCK)[:, None]
        ki = jnp.arange(kend)[None, :]
        sc = jnp.where(ki <= qi, sc, -jnp.inf)
        pr = jax.nn.softmax(sc, axis=-1).astype(v.dtype)
        outs.append(jnp.einsum('bhqk,bkhd->bqhd', pr, v[:, :kend]))
    o = jnp.concatenate(outs, axis=1)
    return o.reshape(b, s, MLA_HEADS * V_HEAD)


def pool_mixer(u, w_pool, pool_scale):
    b, s, _ = u.shape
    u32 = u.astype(jnp.float32)
    maxw = max(POOL_WINDOWS)
    cs = jnp.pad(jnp.cumsum(u32, axis=1), ((0, 0), (maxw, 0), (0, 0)))
    t = jnp.arange(s)
    groups = []
    for g, w in enumerate(POOL_WINDOWS):
        sl = slice(g * POOL_GROUP, (g + 1) * POOL_GROUP)
        win_sum = cs[:, maxw:, sl] - cs[:, maxw - w:maxw - w + s, sl]
        count = jnp.minimum(t + 1, w).astype(jnp.float32)[None, :, None]
        groups.append(win_sum / count - u32[..., sl])
    d = jnp.stack(groups, axis=2).astype(u.dtype)
    y = jnp.einsum('bsgc,gcd->bsgd', d, w_pool).reshape(b, s, MIX_WIDTH)
    return y * pool_scale


def segsum(a):
    t = a.shape[-1]
    cs = jnp.cumsum(a, axis=-1)
    diff = cs[..., :, None] - cs[..., None, :]
    mask = jnp.tril(jnp.ones((t, t), dtype=bool))
    return jnp.where(mask, diff, -jnp.inf)


def ssd_mixer(z, xbc, dt, conv_w, conv_b, dt_bias, a_log, d_skip, norm_g):
    b, s, _ = z.shape
    nc, lc, g, r, n, hp = s // SSD_CHUNK, SSD_CHUNK, SSD_GROUPS, SSD_HEADS // SSD_GROUPS, SSD_STATE, SSD_HEADDIM
    xbc = jax.nn.silu(causal_dwconv(xbc, conv_w, conv_b)).astype(jnp.float32)
    xs = xbc[..., :MIX_WIDTH]
    bm = xbc[..., MIX_WIDTH:MIX_WIDTH + g * n].reshape(b, nc, lc, g, n)
    cm = xbc[..., MIX_WIDTH + g * n:].reshape(b, nc, lc, g, n)
    dt = jax.nn.softplus(dt.astype(jnp.float32) + dt_bias.astype(jnp.float32))
    a_head = -jnp.exp(a_log.astype(jnp.float32))
    x = xs.reshape(b, nc, lc, g, r, hp)
    xdt = x * dt.reshape(b, nc, lc, g, r)[..., None]
    a = (dt * a_head).reshape(b, nc, lc, g, r).transpose(0, 3, 4, 1, 2)
    a_cs = jnp.cumsum(a, axis=-1)
    lmat = jnp.exp(segsum(a))
    cb = jnp.einsum('bclgn,bcsgn->bgcls', cm, bm)
    y_diag = jnp.einsum('bgrcls,bcsgrp->bclgrp', cb[:, :, None] * lmat, xdt)
    decay_states = jnp.exp(a_cs[..., -1:] - a_cs)
    states = jnp.einsum('bclgn,bgrcl,bclgrp->bcgrpn', bm, decay_states, xdt)
    states = jnp.concatenate([jnp.zeros_like(states[:, :1]), states], axis=1)
    chunk_a = jnp.pad(a_cs[..., -1], ((0, 0), (0, 0), (0, 0), (1, 0)))
    decay_chunk = jnp.exp(segsum(chunk_a))
    states = jnp.einsum('bgrzc,bcgrpn->bzgrpn', decay_chunk, states)[:, :-1]
    y_off = jnp.einsum('bclgn,bcgrpn,bgrcl->bclgrp', cm, states, jnp.exp(a_cs))
    y = (y_diag + y_off).reshape(b, s, SSD_HEADS, hp) \
        + xs.reshape(b, s, SSD_HEADS, hp) * d_skip.astype(jnp.float32)[:, None]
    y = y.reshape(b, s, MIX_WIDTH) * jax.nn.silu(z.astype(jnp.float32))
    return rmsnorm(y, norm_g).astype(z.dtype)


def rglru_mixer(gate_in, x_in, conv_w, conv_b, w_a, b_a, w_i, b_i, lam):
    b, s, _ = x_in.shape
    gate = jax.nn.gelu(gate_in)
    xc = causal_dwconv(x_in, conv_w, conv_b)
    xb = xc.reshape(b, s, LRU_BLOCKS, LRU_BLOCK)
    r_t = jax.nn.sigmoid((jnp.einsum('bshi,hij->bshj', xb, w_a).reshape(b, s, MIX_WIDTH) + b_a).astype(jnp.float32))
    i_t = jax.nn.sigmoid((jnp.einsum('bshi,hij->bshj', xb, w_i).reshape(b, s, MIX_WIDTH) + b_i).astype(jnp.float32))
    log_a = -LRU_C * r_t * jax.nn.softplus(-lam.astype(jnp.float32))
    a_t = jnp.exp(log_a)
    mult = jnp.sqrt(-jnp.expm1(2.0 * log_a))
    u = xc.astype(jnp.float32) * i_t * mult

    def combine(lhs, rhs):
        a1, b1 = lhs
        a2, b2 = rhs
        return a1 * a2, a2 * b1 + b2

    _, h = lax.associative_scan(combine, (a_t, u), axis=1)
    return h.astype(x_in.dtype) * gate


def setup_inputs(seed: int = 0) -> dict:
    key = jax.random.key(seed)
    ks = iter(jax.random.split(key, 48))
    f32 = jnp.float32

    def nrm(shape, fan_in):
        return jax.random.normal(next(ks), shape, f32) * (fan_in ** -0.5)

    def gain(shape):
        return 1.0 + 0.05 * jax.random.normal(next(ks), shape, f32)

    def small(shape):
        return 0.01 * jax.random.normal(next(ks), shape, f32)

    L = DEPTH
    x = jax.random.normal(next(ks), (BATCH, SEQ, D_MODEL), f32)
    p = jax.random.normal(next(ks), (DEPTH, BATCH, SEQ, PLE_DIM), f32)
    offs = jax.random.randint(next(ks), (BATCH, 1), 0, 1024, dtype=jnp.int32)
    positions = (offs + jnp.arange(SEQ, dtype=jnp.int32)[None, :]).astype(jnp.int32)
    dt0 = jnp.exp(jax.random.uniform(next(ks), (L, SSD_HEADS), f32, np.log(1e-3), np.log(1e-1)))
    dt_bias = dt0 + jnp.log(-jnp.expm1(-dt0))
    a_log = jnp.log(jax.random.uniform(next(ks), (L, SSD_HEADS), f32, 1.0, 16.0))
    a_pow = jax.random.uniform(next(ks), (L, MIX_WIDTH), f32, 0.9, 0.999) ** (1.0 / LRU_C)
    lam = jnp.log(a_pow) - jnp.log1p(-a_pow)
    return {
        "x": x,
        "p": p,
        "positions": positions,
        "g_mix": gain((L, D_MODEL)),
        "w_in": nrm((L, D_MODEL, IN_COLS), D_MODEL),
        "q_norm": gain((L, Q_LORA)),
        "w_uq": nrm((L, Q_LORA, MLA_HEADS * (QK_NOPE + QK_ROPE)), Q_LORA),
        "kv_norm": gain((L, KV_LORA)),
        "w_ukv": nrm((L, KV_LORA, MLA_HEADS * (QK_NOPE + V_HEAD)), KV_LORA),
        "w_pool": nrm((L, 4, POOL_GROUP, POOL_GROUP), POOL_GROUP),
        "pool_scale": 1.0 + 0.1 * jax.random.normal(next(ks), (L, MIX_WIDTH), f32),
        "ssd_conv_w": nrm((L, CONV_WIDTH, SSD_XBC), CONV_WIDTH),
        "ssd_conv_b": small((L, SSD_XBC)),
        "ssd_dt_bias": dt_bias,
        "ssd_a_log": a_log,
        "ssd_d": gain((L, SSD_HEADS)),
        "ssd_norm": gain((L, MIX_WIDTH)),
        "lru_conv_w": nrm((L, CONV_WIDTH, MIX_WIDTH), CONV_WIDTH),
        "lru_conv_b": small((L, MIX_WIDTH)),
        "lru_w_a": nrm((L, LRU_BLOCKS, LRU_BLOCK, LRU_BLOCK), LRU_BLOCK),
        "lru_b_a": small((L, MIX_WIDTH)),
        "lru_w_i": nrm((L, LRU_BLOCKS, LRU_BLOCK, LRU_BLOCK), LRU_BLOCK),
        "lru_b_i": small((L, MIX_WIDTH)),
        "lru_lambda": lam,
        "w_branch": nrm((L, N_BRANCH, MIX_WIDTH, D_MODEL), MIX_WIDTH),
        "w_out": nrm((L, D_MODEL, D_MODEL), D_MODEL),
        "g_mlp": gain((L, D_MODEL)),
        "w_ff1": nrm((L, D_MODEL, D_FF), D_MODEL),
        "w_ff2": nrm((L, D_FF, D_MODEL), D_FF),
        "g_ple": gain((L, D_MODEL)),
        "w_ple_gate": nrm((L, D_MODEL, D_MODEL), D_MODEL),
        "w_ple": nrm((L, PLE_DIM, D_MODEL), PLE_DIM),
        "g_final": gain((D_MODEL,)),
    }


def reference(x, p, positions, g_mix, w_in, q_norm, w_uq, kv_norm, w_ukv, w_pool, pool_scale,
              ssd_conv_w, ssd_conv_b, ssd_dt_bias, ssd_a_log, ssd_d, ssd_norm,
              lru_conv_w, lru_conv_b, lru_w_a, lru_b_a, lru_w_i, lru_b_i, lru_lambda,
              w_branch, w_out, g_mlp, w_ff1, w_ff2, g_ple, w_ple_gate, w_ple, g_final):
    b, s, _ = x.shape
    cos, sin = rope_tables(positions)
    pts = _split_points()
    for l in range(DEPTH):
        h = rmsnorm(x, g_mix[l])
        u = h @ w_in[l]
        c_q, c_kv, k_r, u_pool, z, xbc, dt, lru_g, lru_x, gates = jnp.split(u, pts, axis=-1)
        y_a = mla_mixer(c_q, c_kv, k_r, cos, sin, q_norm[l], w_uq[l], kv_norm[l], w_ukv[l])
        y_b = pool_mixer(u_pool, w_pool[l], pool_scale[l])
        y_c = ssd_mixer(z, xbc, dt, ssd_conv_w[l], ssd_conv_b[l], ssd_dt_bias[l], ssd_a_log[l],
                        ssd_d[l], ssd_norm[l])
        y_d = rglru_mixer(lru_g, lru_x, lru_conv_w[l], lru_conv_b[l], lru_w_a[l], lru_b_a[l],
                          lru_w_i[l], lru_b_i[l], lru_lambda[l])
        gates = jax.nn.sigmoid(gates.reshape(b, s, N_BRANCH, D_MODEL))
        merged = (gates[:, :, 0] * (y_a @ w_branch[l, 0])
                  + gates[:, :, 1] * (y_b @ w_branch[l, 1])
                  + gates[:, :, 2] * (y_c @ w_branch[l, 2])
                  + gates[:, :, 3] * (y_d @ w_branch[l, 3]))
        x = x + merged @ w_out[l]
        h2 = rmsnorm(x, g_mlp[l])
        x = x + jnp.square(jax.nn.relu(h2 @ w_ff1[l])) @ w_ff2[l]
        ple_gate = jax.nn.sigmoid(rmsnorm(x, g_ple[l]) @ w_ple_gate[l])
        x = x + (p[l] @ w_ple[l]) * ple_gate
    return rmsnorm(x, g_final)
```

```python
import contextlib
import numpy as np
import concourse.bass as bass
import concourse.mybir as mybir
from concourse.bass_utils import run_bass_kernel_spmd

F32 = mybir.dt.float32
BF16 = mybir.dt.bfloat16
I32 = mybir.dt.int32
AF = mybir.ActivationFunctionType
ALU = mybir.AluOpType

S_LEN = 4096
D = 1024
T = 512
NT = S_LEN // T
DEPTH = 2
CH = 2048
NCH = 92
RING = 6
PAGE = 1056
NPAGE = 30
NV = 96
NR = 536
EPS = 1e-6
SCALE = 96 ** -0.5
TWO_PI = 6.283185307179586
MAGIC = 12582912.0

DEBUG_TAPS = {}
LAST_SCHED = None


class Buf:
    __slots__ = ("name", "lastw", "readers")

    def __init__(self, name=""):
        self.name = name
        self.lastw = None
        self.readers = {}


class _Rec:
    def __init__(self):
        self.call = None

    def __getattr__(self, name):
        def f(*args, **kwargs):
            self.call = (name, args, kwargs)
            return self
        return f


LAST_CALL = [None]


def _bind(fn):
    r = _Rec()
    fn(r)
    name, args, kwargs = r.call
    LAST_CALL[0] = r.call
    return lambda e: getattr(e, name)(*args, **kwargs)


class Sched:
    ENG = ("pe", "act", "dve", "pool", "sp")

    def __init__(self, nc):
        self.nc = nc
        self.prog = {e: [] for e in self.ENG}
        self.sems = {}
        self.cnt = {}
        self.known = {e: {} for e in self.ENG}
        self._sem_ctx = []
        self.dma_total = set()
        self.phase = "init"
        self.pe_phase = []
        self.pe_entry_phase = []
        self.mm_count = 0
        self.on_mm = None
        for e in ("pe", "act", "dve", "pool"):
            self.new_sem(e)

    def new_sem(self, name):
        cm = self.nc.semaphore(name)
        s = cm.__enter__()
        self._sem_ctx.append(cm)
        self.sems[name] = s
        self.cnt[name] = 0
        return name

    def _waits(self, eng, reads, writes):
        need = {}
        for b in reads:
            if b.lastw is not None:
                s, v = b.lastw
                if need.get(s, 0) < v:
                    need[s] = v
        for b in writes:
            if b.lastw is not None:
                s, v = b.lastw
                if need.get(s, 0) < v:
                    need[s] = v
            for s, v in b.readers.items():
                if need.get(s, 0) < v:
                    need[s] = v
        kn = self.known[eng]
        for s in list(need):
            if s in self.dma_total:
                need[s] = self.cnt[s]
        for s, v in need.items():
            if s == "pe" and eng == "pe":
                continue
            if kn.get(s, 0) >= v:
                continue
            kn[s] = v
            sem = self.sems[s]
            self.prog[eng].append(lambda e, sem=sem, v=v: e.wait_ge(sem, v))
            if eng == "pe":
                self.pe_entry_phase.append(self.phase)

    def _mark(self, sname, val, reads, writes):
        for b in writes:
            b.lastw = (sname, val)
            b.readers = {}
        for b in reads:
            b.readers[sname] = val

    def op(self, eng, fn, reads=(), writes=()):
        self._waits(eng, reads, writes)
        self.cnt[eng] += 1
        val = self.cnt[eng]
        sem = self.sems[eng]
        fn = _bind(fn)
        self.prog[eng].append(lambda e, fn=fn, sem=sem: fn(e).then_inc(sem, 1))
        self._mark(eng, val, reads, writes)

    def mm(self, fns, reads=(), writes=()):
        self._waits("pe", reads, writes)
        self.cnt["pe"] += 1
        val = self.cnt["pe"]
        sem = self.sems["pe"]
        fns2 = []
        for f in fns:
            fns2.append(_bind(f))
            nm, a_, kw_ = LAST_CALL[0]
            nsl = 1
            if nm == "matmul":
                lh = kw_.get("lhsT")
                if lh is not None and lh.dtype == F32:
                    nsl = 2
            self.pe_phase.extend([self.phase] * nsl)
            self.pe_entry_phase.append(self.phase)
        fns = fns2
        for f in fns[:-1]:
            self.prog["pe"].append(lambda e, f=f: f(e))
        self.prog["pe"].append(lambda e, f=fns[-1], sem=sem: f(e).then_inc(sem, 1))
        self._mark("pe", val, reads, writes)
        self.mm_count += len(fns)
        if self.on_mm is not None:
            self.on_mm()

    def dma(self, q, sname, out, in_, reads=(), writes=()):
        self._waits(q, reads, writes)
        self.cnt[sname] += 16
        val = self.cnt[sname]
        sem = self.sems[sname]
        self.prog[q].append(lambda e, out=out, in_=in_, sem=sem: e.dma_start(out=out, in_=in_).then_inc(sem, 16))
        self._mark(sname, val, reads, writes)

    def run(self):
        nc = self.nc
        with nc.Block() as block:
            @block.tensor
            def _(e):
                for f in self.prog["pe"]:
                    f(e)

            @block.scalar
            def _(e):
                for f in self.prog["act"]:
                    f(e)

            @block.vector
            def _(e):
                for f in self.prog["dve"]:
                    f(e)

            @block.gpsimd
            def _(e):
                for f in self.prog["pool"]:
                    f(e)

            @block.sync
            def _(e):
                for f in self.prog["sp"]:
                    f(e)
        for cm in reversed(self._sem_ctx):
            cm.__exit__(None, None, None)


W_CQ, W_CKV, W_KR, W_POOL, W_Z, W_XBC, W_DT, W_LG, W_LX, W_G = 0, 384, 640, 672, 1184, 1696, 2464, 2472, 2984, 3496


def _kmajor(w):
    k, n = w.shape
    kc = k // 128
    a = w.reshape(kc, 128, n).transpose(1, 0, 2).reshape(128, kc * n)
    out = np.zeros((128, CH), np.float32)
    out[:, : kc * n] = a
    return out


def chunk_names():
    names = ["A0", "A1", "A2", "UQn", "UQr", "UKV", "P0", "P1", "SMP", "LX0", "LX1", "CVL", "SML", "LG0", "LG1",
             "XB0", "XB1", "XB2", "CVS0", "CVS1", "Z0", "Z1", "DT"]
    for dg in range(4):
        names += [f"G0_{dg}", f"B0_{dg}", f"G1_{dg}", f"G2_{dg}", f"B12_{dg}", f"G3_{dg}", f"B3_{dg}"]
    names += [f"WO{i}" for i in range(4)]
    names += [f"F1_{i}" for i in range(16)]
    names += [f"F2_{i}" for i in range(16)]
    names += [f"PG{i}" for i in range(4)]
    names += ["PL"]
    assert len(names) == NCH, len(names)
    return names


def pack_layer(inp, l):
    win = np.asarray(inp["w_in"][l], np.float32)
    ch = {}
    ch["A0"] = _kmajor(win[:, 0:256])
    ch["A1"] = _kmajor(win[:, 256:512])
    a2 = np.zeros((1024, 256), np.float32)
    a2[:, 0:128] = win[:, 512:640]
    a2[:, 128:160] = win[:, 640:672]
    a2[:, 160:176] = win[:, 656:672]
    a2[:, 176:192] = win[:, 640:656]
    ch["A2"] = _kmajor(a2)
    wq = np.asarray(inp["w_uq"][l], np.float32)
    uqn = np.zeros((384, 512), np.float32)
    uqr = np.zeros((384, 512), np.float32)
    for h in range(8):
        uqn[:, h * 64:(h + 1) * 64] = wq[:, h * 96: h * 96 + 64]
        uqr[:, h * 32:(h + 1) * 32] = wq[:, h * 96 + 64: h * 96 + 96]
        uqr[:, 256 + h * 32: 256 + h * 32 + 16] = wq[:, h * 96 + 80: h * 96 + 96]
        uqr[:, 256 + h * 32 + 16: 256 + h * 32 + 32] = wq[:, h * 96 + 64: h * 96 + 80]
    ch["UQn"] = _kmajor(uqn)
    ch["UQr"] = _kmajor(uqr)
    wkv = np.asarray(inp["w_ukv"][l], np.float32)
    ukv = np.zeros((256, 1024), np.float32)
    for h in range(8):
        ukv[:, h * 64:(h + 1) * 64] = wkv[:, h * 128: h * 128 + 64]
        ukv[:, 512 + h * 64: 512 + (h + 1) * 64] = wkv[:, h * 128 + 64: h * 128 + 128]
    ch["UKV"] = _kmajor(ukv)
    ch["P0"] = _kmajor(win[:, W_POOL:W_POOL + 256])
    ch["P1"] = _kmajor(win[:, W_POOL + 256:W_POOL + 512])
    wp = np.asarray(inp["w_pool"][l], np.float32)
    smp = np.zeros((128, CH), np.float32)
    smp[:, 0:512] = wp.transpose(1, 0, 2).reshape(128, 512)
    ch["SMP"] = smp
    ch["LX0"] = _kmajor(win[:, W_LX:W_LX + 256])
    ch["LX1"] = _kmajor(win[:, W_LX + 256:W_LX + 512])
    sml = np.zeros((128, CH), np.float32)
    for wi, key in enumerate(("lru_w_a", "lru_w_i")):
        w = np.asarray(inp[key][l], np.float32)
        for c4 in range(4):
            bd = np.zeros((128, 128), np.float32)
            bd[0:64, 0:64] = w[2 * c4]
            bd[64:128, 64:128] = w[2 * c4 + 1]
            sml[:, wi * 512 + c4 * 128: wi * 512 + (c4 + 1) * 128] = bd
    ch["SML"] = sml
    lcw = np.asarray(inp["lru_conv_w"][l], np.float32)
    cvl = np.zeros((128, CH), np.float32)
    for k in range(4):
        for c in range(4):
            m = k * 4 + c
            cvl[np.arange(128), m * 128 + np.arange(128)] = lcw[k, c * 128:(c + 1) * 128]
    ch["CVL"] = cvl
    scw = np.asarray(inp["ssd_conv_w"][l], np.float32)
    cvs = [np.zeros((128, CH), np.float32), np.zeros((128, CH), np.float32)]
    for k in range(4):
        for c in range(6):
            m = k * 6 + c
            cvs[m // 16][np.arange(128), (m % 16) * 128 + np.arange(128)] = scw[k, c * 128:(c + 1) * 128]
    ch["CVS0"], ch["CVS1"] = cvs
    ch["LG0"] = _kmajor(win[:, W_LG:W_LG + 256])
    ch["LG1"] = _kmajor(win[:, W_LG + 256:W_LG + 512])
    for i in range(3):
        ch[f"XB{i}"] = _kmajor(win[:, W_XBC + i * 256: W_XBC + (i + 1) * 256])
    ch["Z0"] = _kmajor(win[:, W_Z:W_Z + 256])
    ch["Z1"] = _kmajor(win[:, W_Z + 256:W_Z + 512])
    ch["DT"] = _kmajor(win[:, W_DT:W_DT + 8])
    wb = np.asarray(inp["w_branch"][l], np.float32)
    for dg in range(4):
        cs = slice(dg * 256, (dg + 1) * 256)
        for n in range(4):
            ch[f"G{n}_{dg}"] = _kmajor(win[:, W_G + n * 1024 + dg * 256: W_G + n * 1024 + (dg + 1) * 256])
        b0 = np.zeros((128, CH), np.float32)
        b0[0:64, :] = wb[0][:, cs].reshape(8, 64, 256).transpose(1, 0, 2).reshape(64, 2048)
        ch[f"B0_{dg}"] = b0
        b12 = np.zeros((128, CH), np.float32)
        b12[:, 0:1024] = _kmajor(wb[1][:, cs])[:, 0:1024]
        b12[:, 1024:2048] = _kmajor(wb[2][:, cs])[:, 0:1024]
        ch[f"B12_{dg}"] = b12
        ch[f"B3_{dg}"] = _kmajor(wb[3][:, cs])
    wo = np.asarray(inp["w_out"][l], np.float32)
    for i in range(4):
        ch[f"WO{i}"] = _kmajor(wo[:, i * 256:(i + 1) * 256])
    w1 = np.asarray(inp["w_ff1"][l], np.float32)
    for i in range(16):
        ch[f"F1_{i}"] = _kmajor(w1[:, i * 256:(i + 1) * 256])
    w2 = np.asarray(inp["w_ff2"][l], np.float32)
    for dc in range(8):
        for half in range(2):
            ch[f"F2_{dc * 2 + half}"] = _kmajor(w2[half * 2048:(half + 1) * 2048, dc * 128:(dc + 1) * 128])
    wg = np.asarray(inp["w_ple_gate"][l], np.float32)
    for i in range(4):
        ch[f"PG{i}"] = _kmajor(wg[:, i * 256:(i + 1) * 256])
    ch["PL"] = _kmajor(np.asarray(inp["w_ple"][l], np.float32))
    wflat = np.stack([ch[n] for n in chunk_names()], 0)

    def col(v):
        v = np.asarray(v, np.float32)
        return v.reshape(-1, 128).T

    vecs = np.zeros((128, NV), np.float32)
    vecs[:, 0:8] = col(inp["g_mix"][l]); vecs[:, 8:16] = col(inp["g_mlp"][l]); vecs[:, 16:24] = col(inp["g_ple"][l])
    vecs[:, 24:27] = col(inp["q_norm"][l]); vecs[:, 27:29] = col(inp["kv_norm"][l])
    vecs[:, 29:33] = col(inp["pool_scale"][l]); vecs[:, 33:37] = col(inp["lru_conv_b"][l])
    vecs[:, 37:41] = col(inp["lru_b_a"][l]); vecs[:, 41:45] = col(inp["lru_b_i"][l]); vecs[:, 45:49] = col(inp["lru_lambda"][l])
    vecs[:, 49:55] = col(inp["ssd_conv_b"][l])
    for k in range(4):
        vecs[:, 55 + k * 4: 55 + k * 4 + 4] = col(inp["lru_conv_w"][l][k])
        vecs[:, 71 + k * 6: 71 + k * 6 + 6] = col(inp["ssd_conv_w"][l][k])
    rows = np.zeros((1, NR), np.float32)
    rows[0, 0:8] = inp["ssd_dt_bias"][l]; rows[0, 8:16] = inp["ssd_a_log"][l]; rows[0, 16:24] = inp["ssd_d"][l]
    rows[0, 24:536] = inp["ssd_norm"][l]
    return wflat, vecs, rows


def build_nc(taps=None, nlayers=DEPTH, ntiles=NT, stop_after=None):
    taps = taps or []
    nc = bass.Bass("TRN2", target_bir_lowering=False)
    x_d = nc.dram_tensor("x", [S_LEN, D], F32, kind="ExternalInput").ap()
    p_d = nc.dram_tensor("p", [DEPTH, S_LEN, 256], F32, kind="ExternalInput").ap()
    pos_d = nc.dram_tensor("pos", [1, S_LEN], I32, kind="ExternalInput").ap()
    wf_d = nc.dram_tensor("wflat", [DEPTH * NCH, 128, CH], F32, kind="ExternalInput").ap()
    vecs_d = nc.dram_tensor("vecs", [DEPTH, 128, NV], F32, kind="ExternalInput").ap()
    rows_d = nc.dram_tensor("rows", [1, DEPTH * NR], F32, kind="ExternalInput").ap()
    gfin_d = nc.dram_tensor("gfin", [1, D], F32, kind="ExternalInput").ap()
    ropec_d = nc.dram_tensor("ropec", [32, 2], F32, kind="ExternalInput").ap()
    invc_d = nc.dram_tensor("invcnt", [128, 64], F32, kind="ExternalInput").ap()
    out_d = nc.dram_tensor("out", [S_LEN, D], F32, kind="ExternalOutput").ap()
    wbf_d = nc.dram_tensor("wbf", [DEPTH * NCH, 128, CH], BF16, kind="Internal").ap()
    xs_d = nc.dram_tensor("xs", [NT, 128, 8 * T], F32, kind="Internal").ap()
    tap_d = {}

    S = Sched(nc)
    for s in ("st", "pst"):
        S.new_sem(s)
        S.dma_total.add(s)
    for s in ("ld0", "ld1", "ld2", "ld3", "xl0", "xl1", "pl0", "pl1", "pl2", "pl3", "pst0", "pst1", "c_ropec", "c_invc", "c_gfin", "c_vecs", "c_rows", "c_pos"):
        S.new_sem(s)
    xs_b = [Buf(f"xs{i}") for i in range(NT)]
    ring_sems = [S.new_sem(f"r{i}") for i in range(RING)]
    es = contextlib.ExitStack()
    _n = [0]

    def sb(shape, dt, name=None):
        _n[0] += 1
        return es.enter_context(nc.sbuf_tensor("s_" + (name or f"sb{_n[0]}"), shape, dt))

    def psb(shape, dt, name=None):
        _n[0] += 1
        return es.enter_context(nc.psum_tensor(name or f"ps{_n[0]}", shape, dt))

    arena = sb([128, NPAGE * PAGE], BF16, "arena")
    pbuf = [Buf(f"pg{i}") for i in range(NPAGE)]
    free_pages = list(range(NPAGE))

    hw = [NPAGE]

    def palloc(n=1):
        assert len(free_pages) >= n, "out of pages"
        r = [free_pages.pop(0) for _ in range(n)]
        hw[0] = min(hw[0], len(free_pages))
        return r

    def pfree(pgs):
        free_pages.extend(pgs)

    def palloc_c(n):
        fs = sorted(free_pages)
        for i in range(len(fs) - n + 1):
            if fs[i + n - 1] - fs[i] == n - 1:
                r = fs[i:i + n]
                for x_ in r:
                    free_pages.remove(x_)
                return r
        raise AssertionError("no contiguous pages")

    def pb(pg, half):
        return arena[:, pg * PAGE + half * 512: pg * PAGE + (half + 1) * 512]

    def pf(pg):
        return arena[:, pg * PAGE: (pg + 1) * PAGE].bitcast(F32)

    kn_c = sb([128, 4, S_LEN], BF16, "kn_c"); kn_b = [Buf() for _ in range(NT)]
    kr_c = sb([128, S_LEN], BF16, "kr_c"); kr_b = [Buf() for _ in range(NT)]
    v_cf = sb([128, (S_LEN // 128) * 8 * 65 + 64], BF16, "v_c")
    v_c = v_cf[:, 0:(S_LEN // 128) * 8 * 65].rearrange("p (b h d) -> p b h d", b=S_LEN // 128, h=8); v_b = [Buf() for _ in range(NT)]
    xT = sb([128, 8, T], F32, "xT"); xT_b = [Buf() for _ in range(8)]
    hT = sb([128, 8, T], BF16, "hT"); hT_b = [Buf() for _ in range(8)]
    mT = sb([128, 8, T], BF16, "mT"); mT_b = [Buf() for _ in range(8)]
    ring = sb([128, RING, CH], BF16, "ring"); ring_b = [Buf() for _ in range(RING)]
    ps_t = [psb([128, 512], F32, f"bank{i}") for i in range(8)]
    ps_b = [Buf(f"bank{i}") for i in range(8)]
    rot = [0]

    rotn = [5]

    def bank():
        i = rot[0] % rotn[0]
        rot[0] += 1
        return ps_t[i], ps_b[i]

    identf = sb([128, 128], F32, "identf"); identf_b = Buf()
    identb = sb([128, 128], BF16, "identb"); identb_b = Buf()
    onesb = sb([128, 128], BF16, "onesb"); onesb_b = Buf()
    onesf = sb([128, 128], F32, "onesf"); onesf_b = Buf()
    U_le = sb([128, 128], F32, "U_le"); U_b = Buf()
    SU_gt = sb([128, 128], F32, "SU_gt"); SU_b = Buf()
    cmask = sb([128, 128], BF16, "cmask"); cmask_b = Buf()
    neghalf = sb([128, 1], F32, "neghalf"); neghalf_b = Buf()
    vecs = sb([128, NV], F32, "vecs"); vecs_b = Buf()
    rows = sb([128, NR], F32, "rows"); rows_b = Buf()
    gfin = sb([128, D], F32, "gfin"); gfin_b = Buf()
    ropec = sb([32, 2], F32, "ropec"); ropec_b = Buf()
    invc = sb([128, 64], F32, "invc"); invc_b = Buf()
    lcv = sb([128, 4], F32, "lcv"); lcv_b = Buf()
    vhalf = sb([128, 48], F32, "vhalf"); vhalf_b = Buf()
    Abc = sb([128, 8], F32, "Abc"); Abc_b = Buf()
    Sst = sb([128, 4, 64], F32, "Sst"); Sst_b = Buf()
    Sbf = sb([128, 4, 64], BF16, "Sbf"); Sbf_b = Buf()
    hcar = sb([128, 4], F32, "hcar"); hcar_b = Buf()
    hal_pool = sb([128, 4, 16], F32, "hal_pool"); hal_pool_b = Buf()
    hal_lx = sb([128, 4, 3], F32, "hal_lx"); hal_lx_b = Buf()
    hal_xb = sb([128, 6, 3], F32, "hal_xb"); hal_xb_b = Buf()
    sm = sb([128, 256], F32, "sm"); sm_b = Buf(); sm_bs = [Buf(), Buf()]
    sm2 = sb([128, 64], F32, "sm2"); sm2_b = Buf()

    def tap(name, ap, bufs, shape, dt=F32):
        if name not in taps:
            return
        if name not in tap_d:
            tap_d[name] = nc.dram_tensor("tap_" + name, list(shape), dt, kind="ExternalOutput").ap()
        S.dma("sp", "st", tap_d[name], ap, reads=bufs)

    S.dma("sp", "c_ropec", ropec[:], ropec_d, writes=[ropec_b])
    S.dma("sp", "c_invc", invc[:], invc_d, writes=[invc_b])
    S.dma("sp", "c_gfin", gfin[:], gfin_d.partition_broadcast(128), writes=[gfin_b])
    S.op("pool", lambda e: e.memset(identf[:], 0.0), writes=[identf_b])
    S.op("pool", lambda e: e.affine_select(out=identf[:], in_=identf[:], compare_op=ALU.not_equal, fill=1.0, base=0,
                                           pattern=[[-1, 128]], channel_multiplier=1), reads=[identf_b], writes=[identf_b])
    S.op("pool", lambda e: e.tensor_copy(out=identb[:], in_=identf[:]), reads=[identf_b], writes=[identb_b])
    S.op("pool", lambda e: e.memset(onesb[:], 1.0), writes=[onesb_b])
    S.op("pool", lambda e: e.memset(onesf[:], 1.0), writes=[onesf_b])
    S.op("pool", lambda e: e.memset(neghalf[:], -0.5), writes=[neghalf_b])
    S.op("pool", lambda e: e.memset(U_le[:], 1.0), writes=[U_b])
    S.op("pool", lambda e: e.affine_select(out=U_le[:], in_=U_le[:], compare_op=ALU.is_ge, fill=0.0, base=0,
                                           pattern=[[1, 128]], channel_multiplier=-1), reads=[U_b], writes=[U_b])
    S.op("pool", lambda e: e.memset(SU_gt[:], 1.0), writes=[SU_b])
    S.op("pool", lambda e: e.affine_select(out=SU_gt[:], in_=SU_gt[:], compare_op=ALU.is_gt, fill=0.0, base=0,
                                           pattern=[[-1, 128]], channel_multiplier=1), reads=[SU_b], writes=[SU_b])
    S.op("pool", lambda e: e.tensor_copy(out=cmask[:], in_=U_le[:]), reads=[U_b], writes=[cmask_b])
    S.op("pool", lambda e: e.memset(v_cf[:], 0.0), writes=v_b)
    S.op("pool", lambda e: e.memset(v_c[:, :, :, 64:65], 1.0), writes=v_b)
    S.op("pool", lambda e: e.memset(kr_c[:], 0.0), writes=kr_b)

    nchunks_total = nlayers * NCH
    wbf_b = [Buf(f"wbf{i}") for i in range(nchunks_total)]
    CS = 4
    HEAD_CAST = 8
    cast_sems = [S.new_sem(f"cs{i}") for i in range(CS)]
    cst = {"next": 0}

    def cast_emit(n=1):
        for _ in range(n):
            i = cst["next"]
            if i >= nchunks_total:
                return
            S.dma("pool", cast_sems[i % CS], wbf_d[i], wf_d[i], reads=[wbf_b[i - CS]] if i >= CS else [], writes=[wbf_b[i]])
            cst["next"] += 1

    cast_emit(HEAD_CAST)
    pace = {"last": 0}

    def _pace():
        if cst["next"] < nchunks_total and S.mm_count - pace["last"] >= 12:
            pace["last"] = S.mm_count
            cast_emit(1)
    S.on_mm = _pace

    names = chunk_names()
    seq = [(l, j, c) for l in range(nlayers) for j in range(ntiles) for c in range(NCH)]
    ws = {"next_dma": 0, "next_get": 0, "consumed": 0}

    def ws_pump():
        while ws["next_dma"] < len(seq) and ws["next_dma"] < ws["consumed"] + RING:
            i = ws["next_dma"]
            l, j, c = seq[i]
            slot = i % RING
            gi = l * NCH + c
            while cst["next"] <= gi:
                cast_emit()
            S.dma("sp", ring_sems[slot], ring[:, slot, :], wbf_d[gi], reads=[wbf_b[gi]], writes=[ring_b[slot]])
            ws["next_dma"] += 1

    def ws_get(name):
        i = ws["next_get"]
        l, j, c = seq[i]
        assert names[c] == name, (names[c], name)
        assert i < ws["consumed"] + RING
        ws_pump()
        ws["next_get"] += 1
        slot = i % RING
        return ring[:, slot, :], ring_b[slot]

    def ws_done():
        ws["consumed"] = ws["next_get"]
        ws_pump()

    def km(view, kc, ncol):
        return view[:, 0:kc * ncol].rearrange("p (k c) -> p k c", k=kc)

    def rms_bc(srcs, src_bufs, nfeat):
        bk, bkb = bank()
        sq_pg = palloc(1)
        n = len(srcs)
        for c, (s_, b_) in enumerate(zip(srcs, src_bufs)):
            sq = pb(sq_pg[0], c % 2)
            S.op("act", lambda e, sq=sq, s_=s_: e.activation(out=sq, in_=s_, func=AF.Square), reads=[b_], writes=[pbuf[sq_pg[0]]])
            S.mm([lambda e, sq=sq, c=c: e.matmul(bk[:], lhsT=onesb[:], rhs=sq, start=(c == 0), stop=(c == n - 1))],
                 reads=[pbuf[sq_pg[0]], onesb_b], writes=[bkb])
        pfree(sq_pg)
        rp = palloc(1)
        rv = pf(rp[0])[:, 0:512]
        S.op("act", lambda e: e.activation(out=rv, in_=bk[:], func=AF.Sqrt, scale=1.0 / nfeat, bias=EPS), reads=[bkb], writes=[pbuf[rp[0]]])
        S.op("dve", lambda e: e.reciprocal(out=rv, in_=rv), reads=[pbuf[rp[0]]], writes=[pbuf[rp[0]]])
        pfree(rp)
        return rv, pbuf[rp[0]]

    def norm_x_to_h(gcol0):
        rv, rvb = rms_bc([xT[:, c, :] for c in range(8)], xT_b, D)
        for c in range(8):
            S.op("dve", lambda e, c=c: e.scalar_tensor_tensor(out=hT[:, c, :], in0=xT[:, c, :], scalar=vecs[:, gcol0 + c: gcol0 + c + 1],
                                                              in1=rv, op0=ALU.mult, op1=ALU.mult),
                 reads=[xT_b[c], rvb, vecs_b], writes=[hT_b[c]])

    def proj_fm(wview, wbuf, kc, ncol, col0, m, rhs_fn, rhs_bufs):
        bk, bkb = bank()
        w3 = km(wview, kc, ncol)
        S.mm([lambda e, k=k: e.matmul(bk[0:m, :], lhsT=w3[:, k, col0:col0 + m], rhs=rhs_fn(k), start=(k == 0), stop=(k == kc - 1))
              for k in range(kc)], reads=[wbuf] + list(rhs_bufs), writes=[bkb])
        return bk, bkb

    hrhs = lambda k: hT[:, k, :]

    import os as _os2
    _dbgstop = _os2.environ.get('DBG_STOP', '')
    for l in range(nlayers):
        S.dma("sp", "c_vecs", vecs[:], vecs_d[l], writes=[vecs_b])
        S.dma("sp", "c_rows", rows[:], rows_d[:, l * NR:(l + 1) * NR].partition_broadcast(128), writes=[rows_b])
        S.op("act", lambda e: e.activation(out=lcv[:], in_=vecs[:, 45:49], func=AF.Exp, scale=-1.0), reads=[vecs_b], writes=[lcv_b])
        S.op("act", lambda e: e.activation(out=lcv[:], in_=lcv[:], func=AF.Ln, bias=1.0), reads=[lcv_b], writes=[lcv_b])
        S.op("dve", lambda e: e.tensor_scalar(out=lcv[:], in0=lcv[:], scalar1=-8.0, scalar2=None, op0=ALU.mult), reads=[lcv_b], writes=[lcv_b])
        S.op("dve", lambda e: e.tensor_scalar(out=vhalf[:, 0:8], in0=vecs[:, 37:45], scalar1=0.5, scalar2=None, op0=ALU.mult), reads=[vecs_b], writes=[vhalf_b])
        S.op("dve", lambda e: e.tensor_scalar(out=vhalf[:, 8:12], in0=lcv[:], scalar1=0.5, scalar2=None, op0=ALU.mult), reads=[lcv_b, vhalf_b], writes=[vhalf_b])
        S.op("dve", lambda e: e.tensor_scalar(out=vhalf[:, 12:18], in0=vecs[:, 49:55], scalar1=0.5, scalar2=None, op0=ALU.mult), reads=[vecs_b, vhalf_b], writes=[vhalf_b])
        S.op("dve", lambda e: e.tensor_scalar(out=vhalf[:, 18:42], in0=vecs[:, 71:95], scalar1=0.5, scalar2=None, op0=ALU.mult), reads=[vecs_b, vhalf_b], writes=[vhalf_b])
        S.op("act", lambda e: e.activation(out=Abc[:], in_=rows[:, 8:16], func=AF.Exp), reads=[rows_b], writes=[Abc_b])
        S.op("dve", lambda e: e.tensor_scalar(out=Abc[:], in0=Abc[:], scalar1=-1.0, scalar2=None, op0=ALU.mult), reads=[Abc_b], writes=[Abc_b])
        S.op("pool", lambda e: e.memset(Sst[:], 0.0), writes=[Sst_b])
        S.op("pool", lambda e: e.memset(Sbf[:], 0.0), writes=[Sbf_b])
        S.op("pool", lambda e: e.memset(hcar[:], 0.0), writes=[hcar_b])
        S.op("pool", lambda e: e.memset(hal_pool[:], 0.0), writes=[hal_pool_b])
        S.op("pool", lambda e: e.memset(hal_lx[:], 0.0), writes=[hal_lx_b])
        S.op("pool", lambda e: e.memset(hal_xb[:], 0.0), writes=[hal_xb_b])

        for j in range(ntiles):
            t0 = j * T
            rotn[0] = 8
            S.phase = f"L{l}T{j}_p0_x"

            if l == 0:
                xpg = [palloc_c(2) for _ in range(2)]
                for q in range(4):
                    pg4 = xpg[q % 2]
                    xv = arena[:, pg4[0] * PAGE: pg4[0] * PAGE + 2048].bitcast(F32)
                    bufs4 = [pbuf[i] for i in pg4]
                    S.dma("sp", f"xl{q % 2}", xv, x_d[t0 + q * 128: t0 + (q + 1) * 128, :], writes=bufs4)
                    for c in range(8):
                        pass
                    for g2 in range(2):
                        bk, bkb = bank()
                        S.mm([lambda e, c=c, bk=bk, xv=xv: e.transpose(out=bk[:, (c % 4) * 128:(c % 4 + 1) * 128], in_=xv[:, c * 128:(c + 1) * 128], identity=identf[:])
                              for c in range(g2 * 4, g2 * 4 + 4)], reads=bufs4 + [identf_b], writes=[bkb])
                        eng = "act" if g2 == 0 else "dve"
                        outv = xT[:, g2 * 4:(g2 + 1) * 4, q * 128:(q + 1) * 128]
                        inv_ = bk[:].rearrange("p (c t) -> p c t", c=4)
                        if eng == "act":
                            S.op("act", lambda e, o=outv, i_=inv_: e.activation(out=o, in_=i_, func=AF.Copy), reads=[bkb], writes=xT_b[g2 * 4:(g2 + 1) * 4])
                        else:
                            S.op("dve", lambda e, o=outv, i_=inv_: e.tensor_copy(out=o, in_=i_), reads=[bkb], writes=xT_b[g2 * 4:(g2 + 1) * 4])
                for pg4 in xpg:
                    pfree(pg4)
            else:
                S.dma("sp", "xl0", xT[:].rearrange("p c t -> p (c t)"), xs_d[j], reads=[xs_b[j]], writes=xT_b)
            if l == 0 and j == 0:
                tap("xT0", xT[:].rearrange("p c t -> p (c t)"), xT_b, [128, 8 * T])

            S.phase = f"L{l}T{j}_p1_norm"

            norm_x_to_h(0)
            if l == 0 and j == 0:
                tap("hT0", hT[:].rearrange("p c t -> p (c t)"), hT_b, [128, 8 * T], BF16)

            S.phase = f"L{l}T{j}_p2_mla_proj"

            cq_pg = palloc(3); ckv_pg = palloc(2); kr_pg = palloc(2)
            wA0, bA0 = ws_get("A0")
            for cb in range(2):
                bk, bkb = proj_fm(wA0, bA0, 8, 256, cb * 128, 128, hrhs, hT_b)
                S.op("act", lambda e, bk=bk, cb=cb: e.activation(out=pf(cq_pg[cb])[:, 0:512], in_=bk[:], func=AF.Copy), reads=[bkb], writes=[pbuf[cq_pg[cb]]])
            ws_done()
            wA1, bA1 = ws_get("A1")
            bk, bkb = proj_fm(wA1, bA1, 8, 256, 0, 128, hrhs, hT_b)
            S.op("act", lambda e, bk=bk: e.activation(out=pf(cq_pg[2])[:, 0:512], in_=bk[:], func=AF.Copy), reads=[bkb], writes=[pbuf[cq_pg[2]]])
            bk, bkb = proj_fm(wA1, bA1, 8, 256, 128, 128, hrhs, hT_b)
            S.op("dve", lambda e, bk=bk: e.tensor_copy(out=pf(ckv_pg[0])[:, 0:512], in_=bk[:]), reads=[bkb], writes=[pbuf[ckv_pg[0]]])
            ws_done()
            wA2, bA2 = ws_get("A2")
            bk, bkb = proj_fm(wA2, bA2, 8, 256, 0, 128, hrhs, hT_b)
            S.op("dve", lambda e, bk=bk: e.tensor_copy(out=pf(ckv_pg[1])[:, 0:512], in_=bk[:]), reads=[bkb], writes=[pbuf[ckv_pg[1]]])
            for i2 in range(2):
                bk, bkb = proj_fm(wA2, bA2, 8, 256, 128 + 32 * i2, 32, hrhs, hT_b)
                S.op("act", lambda e, bk=bk, i2=i2: e.activation(out=pf(kr_pg[i2])[0:32, 0:512], in_=bk[0:32, :], func=AF.Copy), reads=[bkb], writes=[pbuf[kr_pg[i2]]])
            ws_done()
            cqn_pg = palloc(2); ckvn_pg = palloc(1)
            cqn = [pb(cqn_pg[c // 2], c % 2) for c in range(3)]
            cqn_bufs = [pbuf[cqn_pg[c // 2]] for c in range(3)]
            ckvn = [pb(ckvn_pg[0], c) for c in range(2)]
            rv, rvb = rms_bc([pf(cq_pg[c])[:, 0:512] for c in range(3)], [pbuf[i] for i in cq_pg], 384)
            for c in range(3):
                S.op("dve", lambda e, c=c: e.scalar_tensor_tensor(out=cqn[c], in0=pf(cq_pg[c])[:, 0:512], scalar=vecs[:, 24 + c:25 + c], in1=rv,
                                                                  op0=ALU.mult, op1=ALU.mult), reads=[pbuf[cq_pg[c]], rvb, vecs_b], writes=[cqn_bufs[c]])
            rv, rvb = rms_bc([pf(ckv_pg[c])[:, 0:512] for c in range(2)], [pbuf[i] for i in ckv_pg], 256)
            for c in range(2):
                S.op("dve", lambda e, c=c: e.scalar_tensor_tensor(out=ckvn[c], in0=pf(ckv_pg[c])[:, 0:512], scalar=vecs[:, 27 + c:28 + c], in1=rv,
                                                                  op0=ALU.mult, op1=ALU.mult), reads=[pbuf[ckv_pg[c]], rvb, vecs_b], writes=[pbuf[ckvn_pg[0]]])
            pfree(cq_pg); pfree(ckv_pg)
            tb_pg = palloc(4)
            cosv = pf(tb_pg[0])[0:32, 0:512]; sinv = pf(tb_pg[1])[0:32, 0:512]
            t1 = pf(tb_pg[2])[0:32, 0:512]; t2 = pf(tb_pg[3])[0:32, 0:512]
            tb = [pbuf[i] for i in tb_pg]
            posi = arena[0:32, tb_pg[3] * PAGE: tb_pg[3] * PAGE + 1024].bitcast(I32)
            S.dma("sp", "c_pos", posi, pos_d[:, t0:t0 + T].partition_broadcast(32), writes=[tb[3]])
            S.op("dve", lambda e: e.tensor_scalar(out=t1, in0=posi, scalar1=ropec[:, 0:1], scalar2=None, op0=ALU.mult),
                 reads=[tb[3], ropec_b], writes=[tb[2]])
            for which, dst, shift in ((0, sinv, 0.0), (1, cosv, 1.5707963267948966)):
                db = tb[1] if which == 0 else tb[0]
                S.op("dve", lambda e, shift=shift: e.tensor_scalar(out=t2, in0=t1, scalar1=shift, scalar2=1.0 / TWO_PI, op0=ALU.add, op1=ALU.mult),
                     reads=[tb[2]], writes=[tb[3]])
                S.op("dve", lambda e: e.tensor_scalar(out=t2, in0=t2, scalar1=MAGIC, scalar2=None, op0=ALU.add), reads=[tb[3]], writes=[tb[3]])
                S.op("dve", lambda e: e.tensor_scalar(out=t2, in0=t2, scalar1=-MAGIC, scalar2=None, op0=ALU.add), reads=[tb[3]], writes=[tb[3]])
                S.op("dve", lambda e, dst=dst: e.scalar_tensor_tensor(out=dst, in0=t2, scalar=-6.28125, in1=t1, op0=ALU.mult, op1=ALU.add),
                     reads=[tb[3], tb[2]], writes=[db])
                S.op("dve", lambda e, dst=dst: e.scalar_tensor_tensor(out=dst, in0=t2, scalar=-(TWO_PI - 6.28125), in1=dst, op0=ALU.mult, op1=ALU.add),
                     reads=[tb[3], db], writes=[db])
                S.op("dve", lambda e, dst=dst, shift=shift: e.tensor_scalar(out=dst, in0=dst, scalar1=shift, scalar2=3.14159, op0=ALU.add, op1=ALU.min),
                     reads=[db], writes=[db])
                S.op("dve", lambda e, dst=dst: e.tensor_scalar(out=dst, in0=dst, scalar1=-3.14159, scalar2=None, op0=ALU.max), reads=[db], writes=[db])
                S.op("act", lambda e, dst=dst: e.activation(out=dst, in_=dst, func=AF.Sin), reads=[db], writes=[db])
            S.op("dve", lambda e: e.tensor_scalar(out=sinv, in0=sinv, scalar1=ropec[:, 1:2], scalar2=None, op0=ALU.mult), reads=[tb[1], ropec_b], writes=[tb[1]])
            if l == 0 and j == 0:
                tap("cos0", cosv, [tb[0]], [32, 512]); tap("sin0", sinv, [tb[1]], [32, 512])
            S.op("dve", lambda e: e.tensor_tensor(out=t1, in0=pf(kr_pg[0])[0:32, 0:512], in1=cosv, op=ALU.mult), reads=[pbuf[kr_pg[0]], tb[0]], writes=[tb[2]])
            S.op("dve", lambda e: e.tensor_tensor(out=t2, in0=pf(kr_pg[1])[0:32, 0:512], in1=sinv, op=ALU.mult), reads=[pbuf[kr_pg[1]], tb[1]], writes=[tb[3]])
            S.op("dve", lambda e: e.tensor_tensor(out=kr_c[0:32, t0:t0 + T], in0=t1, in1=t2, op=ALU.add), reads=[tb[2], tb[3]], writes=[kr_b[j]])
            pfree(kr_pg)
            qz_pg = palloc(4); qr_pg = palloc(4)
            for pg_ in qz_pg + qr_pg:
                S.op("pool", lambda e, pg_=pg_: e.memset(arena[:, pg_ * PAGE: pg_ * PAGE + 1024], 0.0), writes=[pbuf[pg_]])
            wq, bq = ws_get("UQn")
            for pr in range(4):
                bk, bkb = proj_fm(wq, bq, 3, 512, pr * 128, 128, lambda k: cqn[k], cqn_bufs)
                hA, hB = 2 * pr, 2 * pr + 1
                S.op("act", lambda e, bk=bk, hA=hA: e.activation(out=pb(qz_pg[hA // 2], hA % 2)[0:64, :], in_=bk[0:64, :], func=AF.Copy), reads=[bkb], writes=[pbuf[qz_pg[hA // 2]]])
                S.op("dve", lambda e, bk=bk, hB=hB: e.tensor_copy(out=pb(qz_pg[hB // 2], hB % 2)[64:128, :], in_=bk[64:128, :]), reads=[bkb], writes=[pbuf[qz_pg[hB // 2]]])
            ws_done()
            wq, bq = ws_get("UQr")
            for h in range(8):
                bk1, bkb1 = proj_fm(wq, bq, 3, 512, h * 32, 32, lambda k: cqn[k], cqn_bufs)
                bk2, bkb2 = proj_fm(wq, bq, 3, 512, 256 + h * 32, 32, lambda k: cqn[k], cqn_bufs)
                S.op("dve", lambda e, bk1=bk1: e.tensor_tensor(out=t1, in0=bk1[0:32, :], in1=cosv, op=ALU.mult), reads=[bkb1, tb[0]], writes=[tb[2]])
                S.op("dve", lambda e, bk2=bk2: e.tensor_tensor(out=t2, in0=bk2[0:32, :], in1=sinv, op=ALU.mult), reads=[bkb2, tb[1]], writes=[tb[3]])
                dst = pb(qr_pg[h // 2], h % 2)[0:32, :]
                S.op("pool", lambda e, dst=dst: e.tensor_tensor(out=dst, in0=t1, in1=t2, op=ALU.add), reads=[tb[2], tb[3]], writes=[pbuf[qr_pg[h // 2]]])
            ws_done()
            pfree(tb_pg)
            wkv, bkv = ws_get("UKV")
            for pr in range(4):
                bk, bkb = proj_fm(wkv, bkv, 2, 1024, pr * 128, 128, lambda k: ckvn[k], [pbuf[ckvn_pg[0]]])
                if pr % 2 == 0:
                    S.op("act", lambda e, bk=bk, pr=pr: e.activation(out=kn_c[:, pr, t0:t0 + T], in_=bk[:], func=AF.Copy), reads=[bkb], writes=[kn_b[j]])
                else:
                    S.op("dve", lambda e, bk=bk, pr=pr: e.tensor_copy(out=kn_c[:, pr, t0:t0 + T], in_=bk[:]), reads=[bkb], writes=[kn_b[j]])
            w3 = km(wkv, 2, 1024)
            for q in range(4):
                bk, bkb = bank()
                S.mm([lambda e, k=k, q=q, bk=bk: e.matmul(bk[:], lhsT=ckvn[k][:, q * 128:(q + 1) * 128], rhs=w3[:, k, 512:1024], start=(k == 0), stop=(k == 1))
                      for k in range(2)], reads=[bkv, pbuf[ckvn_pg[0]]], writes=[bkb])
                dst = v_c[:, j * 4 + q, :, 0:64]
                src = bk[:].rearrange("p (h d) -> p h d", h=8)
                if q % 2 == 0:
                    S.op("act", lambda e, dst=dst, src=src: e.activation(out=dst, in_=src, func=AF.Copy), reads=[bkb], writes=[v_b[j]])
                else:
                    S.op("dve", lambda e, dst=dst, src=src: e.tensor_copy(out=dst, in_=src), reads=[bkb], writes=[v_b[j]])
            ws_done()
            pfree(cqn_pg); pfree(ckvn_pg)
            if l == 0 and j == 0:
                tap("qn0", pb(qz_pg[0], 0), [pbuf[qz_pg[0]]], [128, 512], BF16)
                tap("qr0", pb(qr_pg[0], 0)[0:32, :], [pbuf[qr_pg[0]]], [32, 512], BF16)
                tap("kn0", kn_c[:, 0, 0:512], [kn_b[0]], [128, 512], BF16)
                tap("krc", kr_c[0:32, 0:512], [kr_b[0]], [32, 512], BF16)
                tap("vc0", v_c[:, 0, 0, :], [v_b[0]], [128, 65], BF16)
            rotn[0] = 5
            S.phase = f"L{l}T{j}_p2_attn"
            ya_pg = palloc(4)
            pt_pg = palloc(2)
            nrm_pg = palloc(3)
            nkb = 4 * (j + 1)
            units = [(h, kb) for h in range(8) for kb in range(nkb)]
            st_bank = {}
            st_pt = {}

            def stage_qk(u):
                h, kb = units[u]
                q0 = 0 if kb < 4 * j else 128 * (kb - 4 * j)
                n = 512 - q0
                jt = kb // 4
                bk, bkb = bank()
                st_bank[u] = (bk, bkb)
                qz_v = pb(qz_pg[h // 2], h % 2)
                qr_v = pb(qr_pg[h // 2], h % 2)
                S.mm([lambda e: e.matmul(bk[:, 0:n], lhsT=kn_c[:, h // 2, kb * 128:(kb + 1) * 128], rhs=qz_v[:, q0:512], start=True, stop=False),
                      lambda e: e.matmul(bk[:, 0:n], lhsT=kr_c[:, kb * 128:(kb + 1) * 128], rhs=qr_v[:, q0:512], start=False, stop=True)],
                     reads=[kn_b[jt], kr_b[jt], pbuf[qz_pg[h // 2]], pbuf[qr_pg[h // 2]]], writes=[bkb])

            def stage_exp(u):
                h, kb = units[u]
                q0 = 0 if kb < 4 * j else 128 * (kb - 4 * j)
                n = 512 - q0
                bk, bkb = st_bank.pop(u)
                ptv = pb(pt_pg[(u % 4) // 2], u % 2)
                ptb = pbuf[pt_pg[(u % 4) // 2]]
                st_pt[u] = (ptv, ptb)
                S.op("act", lambda e: e.activation(out=ptv[:, 0:n], in_=bk[:, 0:n], func=AF.Exp, scale=SCALE), reads=[bkb], writes=[ptb])
                if l == 0 and j == 0 and h < 2 and kb == 0:
                    tap(f"pt{h}", ptv, [ptb], [128, 512], BF16)
                if kb >= 4 * j:
                    S.op("pool", lambda e: e.tensor_tensor(out=ptv[:, 0:128], in0=ptv[:, 0:128], in1=cmask[:], op=ALU.mult), reads=[ptb, cmask_b], writes=[ptb])

            def stage_pv(u):
                h, kb = units[u]
                q0 = 0 if kb < 4 * j else 128 * (kb - 4 * j)
                n = 512 - q0
                jt = kb // 4
                ob = 6 + (h % 2)
                O, Ob = ps_t[ob], ps_b[ob]
                ptv, ptb = st_pt.pop(u)
                off = (kb * 8 + h) * 65
                S.mm([lambda e: e.matmul(O[:, q0:512], lhsT=v_cf[:, off:off + 128], rhs=ptv[:, 0:n], start=(kb == 0), stop=(kb == nkb - 1))],
                     reads=[v_b[jt], ptb], writes=[Ob])
                if kb == nkb - 1:
                    if l == 0 and j == 0 and h < 2:
                        dbp = palloc(1)
                        S.op("dve", lambda e: e.tensor_copy(out=pf(dbp[0])[:, 0:512], in_=O[:]), reads=[Ob], writes=[pbuf[dbp[0]]])
                        tap(f"O{h}", pf(dbp[0])[:, 0:512], [pbuf[dbp[0]]], [128, 512])
                        pfree(dbp)
                    rs = pf(nrm_pg[0])[64:65, (h % 2) * 512 // 2 * 0:512]
                    rs = pf(nrm_pg[h % 2])[64:65, 0:512]
                    S.op("dve", lambda e: e.reciprocal(out=rs, in_=O[64:65, :]), reads=[Ob], writes=[pbuf[nrm_pg[h % 2]]])
                    pend_norm.append((t_now[0] + 2, h, O, Ob, rs))

            def norm_b(h, O, Ob, rs):
                bk, bkb = bank()
                S.mm([lambda e: e.matmul(bk[0:64, :], lhsT=onesf[64:65, 0:64], rhs=rs, start=True, stop=True)], reads=[onesf_b, pbuf[nrm_pg[h % 2]]], writes=[bkb])
                bcs = pf(nrm_pg[2])[0:64, 0:512]
                S.op("act", lambda e: e.activation(out=bcs, in_=bk[0:64, :], func=AF.Copy), reads=[bkb], writes=[pbuf[nrm_pg[2]]])
                dst = pb(ya_pg[h // 2], h % 2)[0:64, :]
                S.op("dve", lambda e: e.tensor_tensor(out=dst, in0=O[0:64, :], in1=bcs, op=ALU.mult), reads=[Ob, pbuf[nrm_pg[2]]], writes=[pbuf[ya_pg[h // 2]]])

            pend_norm = []
            t_now = [0]
            NU = len(units)
            for t in range(NU + 3):
                t_now[0] = t
                if t < NU:
                    stage_qk(t)
                if 0 <= t - 1 < NU:
                    stage_exp(t - 1)
                while pend_norm and pend_norm[0][0] <= t:
                    _, h_, O_, Ob_, rs_ = pend_norm.pop(0)
                    norm_b(h_, O_, Ob_, rs_)
                if 0 <= t - 3 < NU:
                    stage_pv(t - 3)
            while pend_norm:
                _, h_, O_, Ob_, rs_ = pend_norm.pop(0)
                norm_b(h_, O_, Ob_, rs_)
            pfree(pt_pg); pfree(nrm_pg); pfree(qz_pg); pfree(qr_pg)
            if l == 0 and j == 0:
                for h in range(8):
                    tap(f"ya{h}", pb(ya_pg[h // 2], h % 2)[0:64, :], [pbuf[ya_pg[h // 2]]], [64, 512], BF16)

            rotn[0] = 8
            S.phase = f"L{l}T{j}_p3_pool"

            yb_pg = palloc(2)
            up_pg = palloc(4)
            for ci in range(2):
                wP, bP = ws_get(f"P{ci}")
                for cb in range(2):
                    g = ci * 2 + cb
                    bk, bkb = proj_fm(wP, bP, 8, 256, cb * 128, 128, hrhs, hT_b)
                    uv = pf(up_pg[g])
                    S.op("act", lambda e, bk=bk, uv=uv: e.activation(out=uv[:, 16:528], in_=bk[:], func=AF.Copy), reads=[bkb], writes=[pbuf[up_pg[g]]])
                    S.op("pool", lambda e, uv=uv, g=g: e.tensor_copy(out=uv[:, 0:16], in_=hal_pool[:, g, :]), reads=[hal_pool_b], writes=[pbuf[up_pg[g]]])
                ws_done()
            wS, bS = ws_get("SMP")
            wp3 = km(wS, 4, 128)
            tmp_pg = palloc(2)
            d_pg = palloc(2)
            for g in range(4):
                uv = pf(up_pg[g]); ub = pbuf[up_pg[g]]
                a = pf(tmp_pg[0]); b_ = pf(tmp_pg[1]); ab = pbuf[tmp_pg[0]]; bb = pbuf[tmp_pg[1]]
                cur, curb = uv, ub
                sh = 1
                lo = 0
                for step in range(g + 1):
                    nxt, nxtb = (a, ab) if step % 2 == 0 else (b_, bb)
                    lo2 = lo + sh
                    S.op("pool", lambda e, cur=cur, nxt=nxt, lo2=lo2, sh=sh: e.tensor_tensor(out=nxt[:, lo2:528], in0=cur[:, lo2:528], in1=cur[:, lo2 - sh:528 - sh], op=ALU.add),
                         reads=[curb], writes=[nxtb])
                    cur, curb = nxt, nxtb
                    lo = lo2
                    sh *= 2
                w_ = float(2 ** (g + 1))
                dv = pb(d_pg[g // 2], g % 2); db_ = pbuf[d_pg[g // 2]]
                S.op("dve", lambda e, cur=cur, uv=uv, dv=dv, w_=w_: e.scalar_tensor_tensor(out=dv, in0=cur[:, 16:528], scalar=1.0 / w_, in1=uv[:, 16:528], op0=ALU.mult, op1=ALU.subtract),
                     reads=[curb, ub], writes=[db_])
                if j == 0:
                    S.op("dve", lambda e, cur=cur, g=g: e.tensor_tensor(out=sm2[:, 0:16], in0=cur[:, 16:32], in1=invc[:, g * 16:(g + 1) * 16], op=ALU.mult), reads=[curb, invc_b], writes=[sm2_b])
                    S.op("dve", lambda e, uv=uv, dv=dv: e.tensor_tensor(out=dv[:, 0:16], in0=sm2[:, 0:16], in1=uv[:, 16:32], op=ALU.subtract), reads=[sm2_b, ub], writes=[db_])
                S.op("pool", lambda e, uv=uv, g=g: e.tensor_copy(out=hal_pool[:, g, :], in_=uv[:, 512:528]), reads=[ub], writes=[hal_pool_b])
                bk, bkb = bank()
                S.mm([lambda e, bk=bk, g=g, dv=dv: e.matmul(bk[:], lhsT=wp3[:, g, :], rhs=dv, start=True, stop=True)], reads=[bS, db_], writes=[bkb])
                S.op("act", lambda e, bk=bk, g=g: e.activation(out=pb(yb_pg[g // 2], g % 2), in_=bk[:], func=AF.Copy, scale=vecs[:, 29 + g:30 + g]), reads=[bkb, vecs_b], writes=[pbuf[yb_pg[g // 2]]])
            ws_done()
            pfree(tmp_pg); pfree(d_pg); pfree(up_pg)
            if l == 0 and j == 0:
                for g in range(4):
                    tap(f"yb{g}", pb(yb_pg[g // 2], g % 2), [pbuf[yb_pg[g // 2]]], [128, 512], BF16)

            if _dbgstop == 'pool':
                break
            S.phase = f"L{l}T{j}_p4_lru"

            yd_pg = palloc(2)
            lx_pg = palloc(4)
            xpb = lambda pg: arena[:, pg * PAGE: pg * PAGE + 515]
            for ci in range(2):
                wX, bX = ws_get(f"LX{ci}")
                for cb in range(2):
                    c = ci * 2 + cb
                    bk, bkb = proj_fm(wX, bX, 8, 256, cb * 128, 128, hrhs, hT_b)
                    xv = xpb(lx_pg[c])
                    S.op("act", lambda e, bk=bk, xv=xv: e.activation(out=xv[:, 3:515], in_=bk[:], func=AF.Copy), reads=[bkb], writes=[pbuf[lx_pg[c]]])
                    S.op("pool", lambda e, xv=xv, c=c: e.tensor_copy(out=xv[:, 0:3], in_=hal_lx[:, c, :]), reads=[hal_lx_b], writes=[pbuf[lx_pg[c]]])
                ws_done()
            if _dbgstop == 'lx':
                break
            xc_pg = palloc(4); xcb_pg = palloc(2)
            wCV, bCV = ws_get("CVL")
            for c in range(4):
                xv = xpb(lx_pg[c]); xb_ = pbuf[lx_pg[c]]
                xc = pf(xc_pg[c])[:, 0:512]; xcb = pbuf[xc_pg[c]]
                bk, bkb = bank()
                S.mm([lambda e, k=k, c=c, xv=xv, bk=bk: e.matmul(bk[:], lhsT=wCV[:, (k * 4 + c) * 128:(k * 4 + c + 1) * 128], rhs=xv[:, k:k + 512], start=(k == 0), stop=(k == 3)) for k in range(4)],
                     reads=[bCV, xb_], writes=[bkb])
                S.op("act", lambda e, bk=bk, xc=xc, c=c: e.activation(out=xc, in_=bk[:], func=AF.Identity, bias=vecs[:, 33 + c:34 + c]), reads=[bkb, vecs_b], writes=[xcb])
                S.op("dve", lambda e, xc=xc, c=c: e.tensor_copy(out=pb(xcb_pg[c // 2], c % 2), in_=xc), reads=[xcb], writes=[pbuf[xcb_pg[c // 2]]])
                S.op("pool", lambda e, xv=xv, c=c: e.tensor_copy(out=hal_lx[:, c, :], in_=xv[:, 512:515]), reads=[xb_], writes=[hal_lx_b])
            ws_done()
            if _dbgstop == 'lruconv':
                break
            pfree(lx_pg)
            wL, bL = ws_get("SML")
            wl3 = km(wL, 8, 128)
            a_pg = palloc(4); u_pg = palloc(4); t_pg = palloc(4)
            AV = [pf(a_pg[c])[:, 0:512] for c in range(4)]; AB = [pbuf[a_pg[c]] for c in range(4)]
            UV = [pf(u_pg[c])[:, 0:512] for c in range(4)]; UB = [pbuf[u_pg[c]] for c in range(4)]
            TV = [pf(t_pg[c])[:, 0:512] for c in range(4)]; TB = [pbuf[t_pg[c]] for c in range(4)]
            XC = [pf(xc_pg[c])[:, 0:512] for c in range(4)]; XCB = [pbuf[xc_pg[c]] for c in range(4)]
            for c in range(4):
                xcbv = pb(xcb_pg[c // 2], c % 2)
                bk, bkb = bank()
                S.mm([lambda e, bk=bk, c=c, xcbv=xcbv: e.matmul(bk[:], lhsT=wl3[:, c, :], rhs=xcbv, start=True, stop=True)], reads=[bL, pbuf[xcb_pg[c // 2]]], writes=[bkb])
                S.op("act", lambda e, bk=bk, c=c: e.activation(out=AV[c], in_=bk[:], func=AF.Tanh, scale=0.5, bias=vhalf[:, c:c + 1]), reads=[bkb, vhalf_b], writes=[AB[c]])
                bk2, bkb2 = bank()
                S.mm([lambda e, bk2=bk2, c=c, xcbv=xcbv: e.matmul(bk2[:], lhsT=wl3[:, 4 + c, :], rhs=xcbv, start=True, stop=True)], reads=[bL, pbuf[xcb_pg[c // 2]]], writes=[bkb2])
                S.op("act", lambda e, bk2=bk2, c=c: e.activation(out=UV[c], in_=bk2[:], func=AF.Tanh, scale=0.5, bias=vhalf[:, 4 + c:5 + c]), reads=[bkb2, vhalf_b], writes=[UB[c]])
            ws_done()
            for c in range(4):
                S.op("act", lambda e, c=c: e.activation(out=AV[c], in_=AV[c], func=AF.Exp, scale=vhalf[:, 8 + c:9 + c], bias=vhalf[:, 8 + c:9 + c]), reads=[AB[c], vhalf_b], writes=[AB[c]])
                S.op("pool", lambda e, c=c: e.tensor_tensor(out=TV[c], in0=AV[c], in1=AV[c], op=ALU.mult), reads=[AB[c]], writes=[TB[c]])
                S.op("dve", lambda e, c=c: e.scalar_tensor_tensor(out=UV[c], in0=UV[c], scalar=1.0, in1=XC[c], op0=ALU.add, op1=ALU.mult), reads=[UB[c], XCB[c]], writes=[UB[c]])
            for c in range(4):
                S.op("act", lambda e, c=c: e.activation(out=TV[c], in_=TV[c], func=AF.Sqrt, scale=-0.25, bias=0.25), reads=[TB[c]], writes=[TB[c]])
            for c in range(4):
                S.op("pool", lambda e, c=c: e.tensor_tensor(out=UV[c], in0=UV[c], in1=TV[c], op=ALU.mult), reads=[UB[c], TB[c]], writes=[UB[c]])
                S.op("dve", lambda e, c=c: e.tensor_tensor_scan(out=XC[c], data0=AV[c], data1=UV[c], initial=hcar[:, c:c + 1], op0=ALU.mult, op1=ALU.add),
                     reads=[AB[c], UB[c], hcar_b, XCB[c]], writes=[XCB[c]])
                S.op("pool", lambda e, c=c: e.tensor_copy(out=hcar[:, c:c + 1], in_=XC[c][:, 511:512]), reads=[XCB[c]], writes=[hcar_b])
            pfree(a_pg); pfree(u_pg); pfree(t_pg); pfree(xcb_pg)
            g_pg = palloc(4)
            for ci in range(2):
                wG, bG = ws_get(f"LG{ci}")
                for cb in range(2):
                    c = ci * 2 + cb
                    bk, bkb = proj_fm(wG, bG, 8, 256, cb * 128, 128, hrhs, hT_b)
                    gv = pf(g_pg[(c % 2) * 2])[:, 0:512]; gb = pbuf[g_pg[(c % 2) * 2]]
                    tv = pf(g_pg[(c % 2) * 2 + 1])[:, 0:512]; tb_ = pbuf[g_pg[(c % 2) * 2 + 1]]
                    xc = XC[c]; xcb = XCB[c]
                    S.op("act", lambda e, bk=bk, gv=gv: e.activation(out=gv, in_=bk[:], func=AF.Copy), reads=[bkb], writes=[gb])
                    S.op("pool", lambda e, gv=gv, tv=tv: e.tensor_tensor(out=tv, in0=gv, in1=gv, op=ALU.mult), reads=[gb], writes=[tb_])
                    S.op("dve", lambda e, tv=tv: e.tensor_scalar(out=tv, in0=tv, scalar1=0.044715, scalar2=1.0, op0=ALU.mult, op1=ALU.add), reads=[tb_], writes=[tb_])
                    S.op("pool", lambda e, gv=gv, tv=tv: e.tensor_tensor(out=tv, in0=tv, in1=gv, op=ALU.mult), reads=[gb, tb_], writes=[tb_])
                    S.op("act", lambda e, tv=tv: e.activation(out=tv, in_=tv, func=AF.Tanh, scale=0.7978845608028654), reads=[tb_], writes=[tb_])
                    S.op("dve", lambda e, gv=gv, tv=tv: e.scalar_tensor_tensor(out=tv, in0=tv, scalar=1.0, in1=gv, op0=ALU.add, op1=ALU.mult), reads=[gb, tb_], writes=[tb_])
                    S.op("dve", lambda e, tv=tv, xc=xc, c=c: e.scalar_tensor_tensor(out=pb(yd_pg[c // 2], c % 2), in0=tv, scalar=0.5, in1=xc, op0=ALU.mult, op1=ALU.mult), reads=[tb_, xcb], writes=[pbuf[yd_pg[c // 2]]])
                ws_done()
            pfree(g_pg); pfree(xc_pg)
            if l == 0 and j == 0:
                for c in range(4):
                    tap(f"yd{c}", pb(yd_pg[c // 2], c % 2), [pbuf[yd_pg[c // 2]]], [128, 512], BF16)

            if _dbgstop == 'lru':
                break
            rotn[0] = 5
            S.phase = f"L{l}T{j}_p5_ssd"

            yc_pg = palloc(2)
            xb_pg = palloc(6)
            for ci in range(3):
                wX, bX = ws_get(f"XB{ci}")
                for cb in range(2):
                    c = ci * 2 + cb
                    bk, bkb = proj_fm(wX, bX, 8, 256, cb * 128, 128, hrhs, hT_b)
                    xv = xpb(xb_pg[c])
                    S.op("act", lambda e, bk=bk, xv=xv: e.activation(out=xv[:, 3:515], in_=bk[:], func=AF.Copy), reads=[bkb], writes=[pbuf[xb_pg[c]]])
                    S.op("pool", lambda e, xv=xv, c=c: e.tensor_copy(out=xv[:, 0:3], in_=hal_xb[:, c, :]), reads=[hal_xb_b], writes=[pbuf[xb_pg[c]]])
                ws_done()
            xa_pg = palloc(3)
            cv_pg = palloc(4)
            wC0, bC0 = ws_get("CVS0"); wC1, bC1 = ws_get("CVS1")
            for c in range(6):
                xv = xpb(xb_pg[c]); xb_ = pbuf[xb_pg[c]]
                vb = pf(cv_pg[c % 2])[:, 0:512]; vbb = pbuf[cv_pg[c % 2]]
                tt = pf(cv_pg[2 + c % 2])[:, 0:512]; ttb = pbuf[cv_pg[2 + c % 2]]
                bk, bkb = bank()

                def tapw(k, c=c):
                    m = k * 6 + c
                    w_ = wC0 if m < 16 else wC1
                    return w_[:, (m % 16) * 128:(m % 16 + 1) * 128]
                S.mm([lambda e, k=k, xv=xv, bk=bk: e.matmul(bk[:], lhsT=tapw(k), rhs=xv[:, k:k + 512], start=(k == 0), stop=(k == 3)) for k in range(4)],
                     reads=[bC0, bC1, xb_], writes=[bkb])
                S.op("pool", lambda e, xv=xv, c=c: e.tensor_copy(out=hal_xb[:, c, :], in_=xv[:, 512:515]), reads=[xb_], writes=[hal_xb_b])
                S.op("act", lambda e, bk=bk, tt=tt, c=c: e.activation(out=tt, in_=bk[:], func=AF.Tanh, scale=0.5, bias=vhalf[:, 12 + c:13 + c]), reads=[bkb, vhalf_b], writes=[ttb])
                S.op("act", lambda e, bk=bk, vb=vb, c=c: e.activation(out=vb, in_=bk[:], func=AF.Identity, scale=0.5, bias=vhalf[:, 12 + c:13 + c]), reads=[bkb, vhalf_b], writes=[vbb])
                S.op("dve", lambda e, vb=vb, tt=tt, c=c: e.scalar_tensor_tensor(out=pb(xa_pg[c // 2], c % 2), in0=tt, scalar=1.0, in1=vb, op0=ALU.add, op1=ALU.mult),
                     reads=[ttb, vbb], writes=[pbuf[xa_pg[c // 2]]])
            ws_done()
            if _dbgstop == 'ssdconv':
                break
            pfree(xb_pg); pfree(cv_pg)
            if l == 0 and j == 0:
                for c in range(6):
                    tap(f"xa{c}", pb(xa_pg[c // 2], c % 2), [pbuf[xa_pg[c // 2]]], [128, 512], BF16)
            wZ0, bZ0 = ws_get("Z0"); wZ1, bZ1 = ws_get("Z1"); wDT, bDT = ws_get("DT")
            z3 = [km(wZ0, 8, 256), km(wZ1, 8, 256)]
            dt3 = km(wDT, 8, 8)
            xa = [pb(xa_pg[c // 2], c % 2) for c in range(6)]
            xab = [pbuf[xa_pg[c // 2]] for c in range(6)]
            fs_pg = [palloc(4) for _ in range(2)]
            zs_pg = palloc(4)
            for c4 in range(4):
                ts_ = slice(c4 * 128, (c4 + 1) * 128)
                bk3, bkb3 = bank()
                for hf in range(2):
                    S.mm([lambda e, k=k, hf=hf: e.matmul(bk3[:, hf * 256:(hf + 1) * 256], lhsT=hT[:, k, ts_], rhs=z3[hf][:, k, :], start=(k == 0), stop=(k == 7)) for k in range(8)],
                         reads=hT_b + [bZ0, bZ1], writes=[bkb3])
                zs_ = pf(zs_pg[c4])[:, 0:512]
                S.op("act", lambda e: e.activation(out=zs_, in_=bk3[:], func=AF.Tanh, scale=0.5), reads=[bkb3], writes=[pbuf[zs_pg[c4]]])
                S.op("dve", lambda e: e.scalar_tensor_tensor(out=zs_, in0=zs_, scalar=1.0, in1=bk3[:], op0=ALU.add, op1=ALU.mult), reads=[bkb3, pbuf[zs_pg[c4]]], writes=[pbuf[zs_pg[c4]]])
            bk4, bkb4 = bank()
            for c4 in range(4):
                ts_ = slice(c4 * 128, (c4 + 1) * 128)
                S.mm([lambda e, k=k: e.matmul(bk4[:, c4 * 8:(c4 + 1) * 8], lhsT=hT[:, k, ts_], rhs=dt3[:, k, :], start=(k == 0), stop=(k == 7)) for k in range(8)],
                     reads=hT_b + [bDT], writes=[bkb4])
            v48 = lambda ap: ap.rearrange("p (c h) -> p c h", c=4)
            S.op("dve", lambda e: e.tensor_tensor(out=v48(sm[:, 64:96]), in0=v48(bk4[:, 0:32]), in1=rows[:, 0:8].unsqueeze(1).to_broadcast([128, 4, 8]), op=ALU.add), reads=[bkb4, rows_b], writes=[sm_b])
            S.op("act", lambda e: e.activation(out=sm[:, 64:96], in_=sm[:, 64:96], func=AF.Exp), reads=[sm_b], writes=[sm_b])
            S.op("act", lambda e: e.activation(out=sm[:, 0:32], in_=sm[:, 64:96], func=AF.Ln, bias=1.0), reads=[sm_b], writes=[sm_b])
            S.op("dve", lambda e: e.tensor_tensor(out=v48(sm[:, 32:64]), in0=v48(sm[:, 0:32]), in1=Abc[:].unsqueeze(1).to_broadcast([128, 4, 8]), op=ALU.mult), reads=[sm_b, Abc_b], writes=[sm_b])
            bk5, bkb5 = bank()
            mmsmall = []
            for c4 in range(4):
                a_ = sm[:, 32 + c4 * 8:40 + c4 * 8]
                mmsmall += [lambda e, c4=c4, a_=a_: e.matmul(bk5[:, c4 * 24:c4 * 24 + 8], lhsT=U_le[:], rhs=a_, start=True, stop=True),
                            lambda e, c4=c4, a_=a_: e.matmul(bk5[:, c4 * 24 + 8:c4 * 24 + 16], lhsT=SU_gt[:], rhs=a_, start=True, stop=True),
                            lambda e, c4=c4, a_=a_: e.matmul(bk5[:, c4 * 24 + 16:c4 * 24 + 24], lhsT=onesf[:], rhs=a_, start=True, stop=True)]
            S.mm(mmsmall, reads=[U_b, SU_b, onesf_b, sm_b], writes=[bkb5])
            S.op("act", lambda e: e.activation(out=sm[:, 96:192], in_=bk5[:, 0:96], func=AF.Exp), reads=[bkb5], writes=[sm_b])
            w_pg = palloc(4)
            Mdt_pg, xb_pg2, ycomb_pg, junk_pg = w_pg
            Yd, Yd_bf = ps_t[5], ps_b[5]
            Yo, Yo_bf = ps_t[6], ps_b[6]
            St, St_bf = ps_t[7], ps_b[7]
            fstate = {}

            import os as _os
            _flim = int(_os.environ.get('DBG_FLIM', '0'))

            def ssd_front(c4):
                st = c4 % 2
                fp = fs_pg[st]
                ts_ = slice(c4 * 128, (c4 + 1) * 128)
                smb = sm_b
                a_v = sm[:, 32 + c4 * 8:40 + c4 * 8]
                xtok = pb(fp[0], 0); btok = pb(fp[0], 1); p0b = pbuf[fp[0]]
                zs = pf(zs_pg[c4])[:, 0:512]; zs_b = pbuf[zs_pg[c4]]
                cbm = pf(fp[1])[:, 0:256]; cbm_b = pbuf[fp[1]]
                E = [pf(fp[2])[:, 0:512], pf(fp[3])[:, 0:512]]; E_b = [pbuf[fp[2]], pbuf[fp[3]]]
                bk, bkb = bank()
                S.mm([lambda e, c=c: e.matmul(bk[:, c * 128:(c + 1) * 128], lhsT=xa[c][:, ts_], rhs=identb[:], start=True, stop=True) for c in range(4)],
                     reads=xab[0:4] + [identb_b], writes=[bkb])
                S.op("act", lambda e: e.activation(out=xtok, in_=bk[:], func=AF.Copy), reads=[bkb], writes=[p0b])
                bk2, bkb2 = bank()
                S.mm([lambda e: e.matmul(bk2[:, 0:128], lhsT=xa[4][:, ts_], rhs=identb[:], start=True, stop=True)], reads=[xab[4], identb_b], writes=[bkb2])
                S.op("dve", lambda e: e.tensor_copy(out=btok[:, 0:128], in_=bk2[:, 0:128]), reads=[bkb2], writes=[p0b])
                if _flim == 1:
                    return
                if _flim == 2:
                    return
                for g in range(2):
                    bk6, bkb6 = bank()
                    S.mm([lambda e, g=g, bk6=bk6: e.matmul(bk6[:, 0:128], lhsT=xa[4][64 * g:64 * g + 64, ts_], rhs=xa[5][64 * g:64 * g + 64, ts_], start=True, stop=True)],
                         reads=[xab[4], xab[5]], writes=[bkb6])
                    S.op("dve", lambda e, g=g, bk6=bk6: e.tensor_tensor(out=cbm[:, g * 128:(g + 1) * 128], in0=bk6[:, 0:128], in1=U_le[:], op=ALU.mult), reads=[bkb6, U_b], writes=[cbm_b])
                if _flim == 3:
                    return
                for half in range(2):
                    lh = E[half].rearrange("p (h n) -> p h n", h=4)
                    S.op("pool", lambda e, lh=lh, half=half: e.tensor_tensor(out=lh, in0=SU_gt[:].unsqueeze(1).to_broadcast([128, 4, 128]),
                                                                            in1=a_v[:, 4 * half:4 + 4 * half].unsqueeze(2).to_broadcast([128, 4, 128]), op=ALU.mult),
                         reads=[SU_b, smb], writes=[E_b[half]])
                if _flim == 4:
                    return
                for half in range(2):
                    bk7, bkb7 = bank()
                    S.mm([lambda e, r=r, half=half, bk7=bk7: e.matmul(bk7[:, r * 128:(r + 1) * 128], lhsT=E[half][:, r * 128:(r + 1) * 128], rhs=U_le[:], start=True, stop=True) for r in range(4)],
                         reads=[E_b[half], U_b], writes=[bkb7])
                    S.op("act", lambda e, half=half, bk7=bk7: e.activation(out=E[half], in_=bk7[:], func=AF.Exp), reads=[bkb7], writes=[E_b[half]])
                fstate[c4] = (xtok, btok, p0b, zs, zs_b, cbm, cbm_b, E, E_b, None, smb)

            def ssd_back(c4):
                xtok, btok, p0b, zs, zs_b, cbm, cbm_b, E, E_b, smc, smb = fstate.pop(c4)
                dt_v = sm[:, c4 * 8:(c4 + 1) * 8]
                eacs_v = sm[:, 96 + c4 * 24:104 + c4 * 24]
                edec_v = sm[:, 104 + c4 * 24:112 + c4 * 24]
                etot_v = sm[:, 112 + c4 * 24:120 + c4 * 24]
                ts_ = slice(c4 * 128, (c4 + 1) * 128)
                Mdt = [pb(Mdt_pg, 0), pb(Mdt_pg, 1)]; Mdt_b = pbuf[Mdt_pg]
                xdt = pb(xb_pg2, 0); Bw = pb(xb_pg2, 1); xb_b = pbuf[xb_pg2]
                ycomb = pf(ycomb_pg)[:, 0:512]; yc_b = pbuf[ycomb_pg]
                junk = pf(junk_pg)[:, 0:512]; junk_b = pbuf[junk_pg]
                v3 = lambda ap: ap.rearrange("p (h n) -> p h n", h=8)
                S.op("dve", lambda e: e.tensor_tensor(out=v3(xdt), in0=v3(xtok), in1=dt_v.unsqueeze(2).to_broadcast([128, 8, 64]), op=ALU.mult),
                     reads=[p0b, smb], writes=[xb_b])
                for g in range(2):
                    S.op("pool", lambda e, g=g: e.tensor_tensor(out=Bw[:, g * 256:(g + 1) * 256].rearrange("p (r n) -> p r n", r=4),
                                                                in0=btok[:, g * 64:(g + 1) * 64].unsqueeze(1).to_broadcast([128, 4, 64]),
                                                                in1=edec_v[:, 4 * g:4 + 4 * g].unsqueeze(2).to_broadcast([128, 4, 64]), op=ALU.mult),
                         reads=[p0b, smb], writes=[xb_b])
                for g in range(2):
                    S.op("dve", lambda e, g=g: e.tensor_tensor(out=Mdt[g].rearrange("p (r n) -> p r n", r=4), in0=E[g].rearrange("p (r n) -> p r n", r=4),
                                                               in1=cbm[:, g * 128:(g + 1) * 128].unsqueeze(1).to_broadcast([128, 4, 128]), op=ALU.mult),
                         reads=[E_b[g], cbm_b], writes=[Mdt_b])
                def yo_mm(hs_):
                    S.mm([lambda e, h=h: e.matmul(Yo[:, h * 64:(h + 1) * 64], lhsT=xa[5][64 * (h // 4):64 * (h // 4) + 64, ts_], rhs=Sbf[64 * (h // 4):64 * (h // 4) + 64, h % 4, :], start=True, stop=True) for h in hs_],
                         reads=[xab[5], Sbf_b], writes=[Yo_bf])
                yo_mm(range(0, 4))
                S.mm([lambda e, h=h: e.matmul(Yd[:, h * 64:(h + 1) * 64], lhsT=Mdt[h // 4][:, (h % 4) * 128:(h % 4 + 1) * 128], rhs=xdt[:, h * 64:(h + 1) * 64], start=True, stop=True) for h in range(8)],
                     reads=[Mdt_b, xb_b], writes=[Yd_bf])
                yo_mm(range(4, 8))
                S.mm([lambda e, h=h: e.matmul(St[64 * (h // 4):64 * (h // 4) + 64, (h % 4) * 64:(h % 4 + 1) * 64], lhsT=Bw[:, h * 64:(h + 1) * 64], rhs=xdt[:, h * 64:(h + 1) * 64], start=True, stop=True) for h in range(8)],
                     reads=[xb_b], writes=[St_bf])
                for g in range(2):
                    ps_ = slice(64 * g, 64 * g + 64)
                    S.op("dve", lambda e, g=g, ps_=ps_: e.tensor_tensor(out=Sst[ps_, :, :], in0=Sst[ps_, :, :], in1=etot_v[ps_, 4 * g:4 + 4 * g].unsqueeze(2).to_broadcast([64, 4, 64]), op=ALU.mult),
                         reads=[Sst_b, smb], writes=[Sst_b])
                S.op("dve", lambda e: e.tensor_tensor(out=Sst[:], in0=St[:, 0:256].rearrange("p (r n) -> p r n", r=4), in1=Sst[:], op=ALU.add), reads=[Sst_b, St_bf], writes=[Sst_b])
                S.op("act", lambda e: e.activation(out=Sbf[:], in_=Sst[:], func=AF.Copy), reads=[Sst_b], writes=[Sbf_b])
                S.op("dve", lambda e: e.tensor_tensor(out=v3(ycomb), in0=v3(Yo[:]), in1=eacs_v.unsqueeze(2).to_broadcast([128, 8, 64]), op=ALU.mult),
                     reads=[Yo_bf, smb], writes=[yc_b])
                S.op("dve", lambda e: e.tensor_tensor(out=ycomb, in0=Yd[:], in1=ycomb, op=ALU.add), reads=[Yd_bf, yc_b], writes=[yc_b])
                S.op("pool", lambda e: e.tensor_tensor(out=v3(junk), in0=v3(xtok), in1=rows[:, 16:24].unsqueeze(2).to_broadcast([128, 8, 64]), op=ALU.mult),
                     reads=[p0b, rows_b], writes=[junk_b])
                S.op("pool", lambda e: e.tensor_tensor(out=ycomb, in0=ycomb, in1=junk, op=ALU.add), reads=[yc_b, junk_b], writes=[yc_b])
                S.op("pool", lambda e: e.tensor_tensor(out=ycomb, in0=ycomb, in1=zs, op=ALU.mult), reads=[yc_b, zs_b], writes=[yc_b])
                S.op("act", lambda e: e.activation(out=junk, in_=ycomb, func=AF.Square, accum_out=sm2[:, 16:17]), reads=[yc_b], writes=[junk_b, sm2_b])
                S.op("act", lambda e: e.activation(out=sm2[:, 17:18], in_=sm2[:, 16:17], func=AF.Ln, scale=1.0 / 512, bias=4 * EPS), reads=[sm2_b], writes=[sm2_b])
                S.op("act", lambda e: e.activation(out=sm2[:, 17:18], in_=sm2[:, 17:18], func=AF.Exp, scale=-0.5), reads=[sm2_b], writes=[sm2_b])
                S.op("dve", lambda e: e.scalar_tensor_tensor(out=junk, in0=ycomb, scalar=sm2[:, 17:18], in1=rows[:, 24:536], op0=ALU.mult, op1=ALU.mult),
                     reads=[yc_b, sm2_b, rows_b], writes=[junk_b])
                bk, bkb = bank()
                S.mm([lambda e, c=c: e.transpose(out=bk[:, c * 128:(c + 1) * 128], in_=junk[:, c * 128:(c + 1) * 128], identity=identf[:]) for c in range(4)],
                     reads=[junk_b, identf_b], writes=[bkb])
                S.op("act", lambda e: e.activation(out=pb(yc_pg[0], 0)[:, ts_], in_=bk[:, 0:128], func=AF.Copy), reads=[bkb], writes=[pbuf[yc_pg[0]]])
                S.op("dve", lambda e: e.tensor_copy(out=pb(yc_pg[0], 1)[:, ts_], in_=bk[:, 128:256]), reads=[bkb], writes=[pbuf[yc_pg[0]]])
                S.op("act", lambda e: e.activation(out=pb(yc_pg[1], 0)[:, ts_], in_=bk[:, 256:384], func=AF.Copy), reads=[bkb], writes=[pbuf[yc_pg[1]]])
                S.op("dve", lambda e: e.tensor_copy(out=pb(yc_pg[1], 1)[:, ts_], in_=bk[:, 384:512]), reads=[bkb], writes=[pbuf[yc_pg[1]]])

            import os as _os
            _dbg = _os.environ.get("DBG_SSD", "")
            ssd_front(0)
            if _dbg == "front":
                break
            if _dbg == "back":
                ssd_back(0)
                break
            for c4 in range(4):
                if c4 + 1 < 4:
                    ssd_front(c4 + 1)
                ssd_back(c4)
            pfree(fs_pg[0]); pfree(fs_pg[1]); pfree(zs_pg)
            ws_done()
            pfree(w_pg); pfree(xa_pg)
            if l == 0 and j == 0:
                for c in range(4):
                    tap(f"yc{c}", pb(yc_pg[c // 2], c % 2), [pbuf[yc_pg[c // 2]]], [128, 512], BF16)

            rotn[0] = 8
            S.phase = f"L{l}T{j}_p6_merge"

            ybr = {1: yb_pg, 2: yc_pg, 3: yd_pg}
            sgp = {n: palloc(2) for n in range(4)}
            acc_pg = palloc(2); tmp_pg = palloc(2)

            def do_gate(n, dg):
                wGn, bGn = ws_get(f"G{n}_{dg}")
                for cb in range(2):
                    bk, bkb = proj_fm(wGn, bGn, 8, 256, cb * 128, 128, hrhs, hT_b)
                    S.op("act", lambda e, bk=bk, o=pf(sgp[n][cb])[:, 0:512]: e.activation(out=o, in_=bk[:], func=AF.Tanh, scale=0.5), reads=[bkb], writes=[pbuf[sgp[n][cb]]])
                ws_done()

            def do_branch(nn, dg, wB, bB):
                for cb in range(2):
                    dc = dg * 2 + cb
                    bk, bkb = bank()
                    if nn == 0:
                        w3 = wB[0:64, :].rearrange("p (h c) -> p h c", h=8)
                        S.mm([lambda e, bk=bk, h=h, cb=cb, w3=w3: e.matmul(bk[:], lhsT=w3[:, h, cb * 128:(cb + 1) * 128], rhs=pb(ya_pg[h // 2], h % 2)[0:64, :], start=(h == 0), stop=(h == 7)) for h in range(8)],
                             reads=[bB] + [pbuf[i] for i in ya_pg], writes=[bkb])
                    else:
                        if nn in (1, 2):
                            w3 = wB[:, (nn - 1) * 1024:nn * 1024].rearrange("p (k c) -> p k c", k=4)
                        else:
                            w3 = km(wB, 4, 256)
                        ypg = ybr[nn]
                        S.mm([lambda e, bk=bk, k=k, cb=cb, w3=w3, ypg=ypg: e.matmul(bk[:], lhsT=w3[:, k, cb * 128:(cb + 1) * 128], rhs=pb(ypg[k // 2], k % 2), start=(k == 0), stop=(k == 3)) for k in range(4)],
                             reads=[bB] + [pbuf[i] for i in ypg], writes=[bkb])
                    sgv = pf(sgp[nn][cb])[:, 0:512]; sgb = pbuf[sgp[nn][cb]]
                    acc = pf(acc_pg[cb])[:, 0:512]; accb = pbuf[acc_pg[cb]]
                    if nn == 0:
                        S.op("dve", lambda e, bk=bk, acc=acc, sgv=sgv: e.scalar_tensor_tensor(out=acc, in0=sgv, scalar=1.0, in1=bk[:], op0=ALU.add, op1=ALU.mult), reads=[bkb, sgb], writes=[accb])
                    else:
                        tv = pf(tmp_pg[cb])[:, 0:512]; tvb = pbuf[tmp_pg[cb]]
                        S.op("dve", lambda e, bk=bk, tv=tv, sgv=sgv: e.scalar_tensor_tensor(out=tv, in0=sgv, scalar=1.0, in1=bk[:], op0=ALU.add, op1=ALU.mult), reads=[bkb, sgb], writes=[tvb])
                        last = (nn == 3)
                        outv = mT[:, dc, :] if last else acc
                        S.op("pool", lambda e, tv=tv, acc=acc, outv=outv: e.tensor_tensor(out=outv, in0=acc, in1=tv, op=ALU.add), reads=[tvb, accb],
                             writes=[mT_b[dc]] if last else [accb])

            for dg in range(4):
                do_gate(0, dg)
                wB, bB = ws_get(f"B0_{dg}")
                do_branch(0, dg, wB, bB)
                ws_done()
                do_gate(1, dg)
                do_gate(2, dg)
                wB, bB = ws_get(f"B12_{dg}")
                do_branch(1, dg, wB, bB)
                do_branch(2, dg, wB, bB)
                ws_done()
                do_gate(3, dg)
                wB, bB = ws_get(f"B3_{dg}")
                do_branch(3, dg, wB, bB)
                ws_done()
            for n in range(4):
                pfree(sgp[n])
            pfree(acc_pg); pfree(tmp_pg)
            pfree(ya_pg); pfree(yb_pg); pfree(yc_pg); pfree(yd_pg)
            if l == 0 and j == 0:
                tap("mT0", mT[:].rearrange("p c t -> p (c t)"), mT_b, [128, 8 * T], BF16)

            S.phase = f"L{l}T{j}_p7_wout"

            for i in range(4):
                wO, bO = ws_get(f"WO{i}")
                for cb in range(2):
                    dc = i * 2 + cb
                    bk, bkb = proj_fm(wO, bO, 8, 256, cb * 128, 128, lambda k: mT[:, k, :], mT_b)
                    S.op("dve", lambda e, bk=bk, dc=dc: e.scalar_tensor_tensor(out=xT[:, dc, :], in0=bk[:], scalar=0.5, in1=xT[:, dc, :], op0=ALU.mult, op1=ALU.add), reads=[bkb, xT_b[dc]], writes=[xT_b[dc]])
                ws_done()
            if l == 0 and j == 0:
                tap("xT1", xT[:].rearrange("p c t -> p (c t)"), xT_b, [128, 8 * T])

            S.phase = f"L{l}T{j}_p8_ffn"

            norm_x_to_h(8)
            f_pg = palloc(16)
            r_pg = palloc(2)
            for i in range(16):
                wF, bF = ws_get(f"F1_{i}")
                for cb in range(2):
                    fc = i * 2 + cb
                    bk, bkb = proj_fm(wF, bF, 8, 256, cb * 128, 128, hrhs, hT_b)
                    rv_ = pf(r_pg[fc % 2])[:, 0:512]; rb_ = pbuf[r_pg[fc % 2]]
                    S.op("act", lambda e, bk=bk, rv_=rv_: e.activation(out=rv_, in_=bk[:], func=AF.Relu), reads=[bkb], writes=[rb_])
                    eng = "pool" if fc % 2 == 0 else "dve"
                    S.op(eng, lambda e, rv_=rv_, fc=fc: e.tensor_tensor(out=pb(f_pg[fc // 2], fc % 2), in0=rv_, in1=rv_, op=ALU.mult), reads=[rb_], writes=[pbuf[f_pg[fc // 2]]])
                ws_done()
            pfree(r_pg)
            if l == 0 and j == 0:
                tap("h2T", hT[:].rearrange("p c t -> p (c t)"), hT_b, [128, 8 * T], BF16)
                tap("fT0", pb(f_pg[0], 0), [pbuf[f_pg[0]]], [128, 512], BF16)
                tap("fT31", pb(f_pg[15], 1), [pbuf[f_pg[15]]], [128, 512], BF16)
            for dc in range(8):
                wa, ba = ws_get(f"F2_{dc * 2}")
                wb2, bb2 = ws_get(f"F2_{dc * 2 + 1}")
                wa3 = km(wa, 16, 128); wb3 = km(wb2, 16, 128)
                bk, bkb = bank()
                S.mm([lambda e, bk=bk, k=k: e.matmul(bk[:], lhsT=(wa3 if k < 16 else wb3)[:, k % 16, :], rhs=pb(f_pg[k // 2], k % 2), start=(k == 0), stop=(k == 31)) for k in range(32)],
                     reads=[ba, bb2] + [pbuf[i] for i in f_pg], writes=[bkb])
                S.op("dve", lambda e, bk=bk, dc=dc: e.tensor_tensor(out=xT[:, dc, :], in0=xT[:, dc, :], in1=bk[:], op=ALU.add), reads=[bkb, xT_b[dc]], writes=[xT_b[dc]])
                ws_done()
            pfree(f_pg)
            if l == 0 and j == 0:
                tap("xT2", xT[:].rearrange("p c t -> p (c t)"), xT_b, [128, 8 * T])

            S.phase = f"L{l}T{j}_p9_ple"

            norm_x_to_h(16)
            pl_pg = palloc(4)
            pT_pg = palloc(1)
            for q in range(4):
                pv = pf(pl_pg[q])[:, 0:256]
                S.dma("sp", f"pl{q}", pv, p_d[l, t0 + q * 128:t0 + (q + 1) * 128, :], writes=[pbuf[pl_pg[q]]])
            bk, bkb = bank()
            bk2, bkb2 = bank()
            for q in range(4):
                pv = pf(pl_pg[q])[:, 0:256]
                S.mm([lambda e, bk=bk, q=q, pv=pv: e.transpose(out=bk[:, q * 128:(q + 1) * 128], in_=pv[:, 0:128], identity=identf[:])], reads=[pbuf[pl_pg[q]], identf_b], writes=[bkb])
                S.mm([lambda e, bk2=bk2, q=q, pv=pv: e.transpose(out=bk2[:, q * 128:(q + 1) * 128], in_=pv[:, 128:256], identity=identf[:])], reads=[pbuf[pl_pg[q]], identf_b], writes=[bkb2])
            S.op("act", lambda e, bk=bk: e.activation(out=pb(pT_pg[0], 0), in_=bk[:], func=AF.Copy), reads=[bkb], writes=[pbuf[pT_pg[0]]])
            S.op("dve", lambda e, bk2=bk2: e.tensor_copy(out=pb(pT_pg[0], 1), in_=bk2[:]), reads=[bkb2], writes=[pbuf[pT_pg[0]]])
            pfree(pl_pg)
            sgl_pg = palloc(8)
            for i in range(4):
                wG, bG = ws_get(f"PG{i}")
                for cb in range(2):
                    dc = i * 2 + cb
                    bk, bkb = proj_fm(wG, bG, 8, 256, cb * 128, 128, hrhs, hT_b)
                    S.op("act", lambda e, bk=bk, dc=dc: e.activation(out=pf(sgl_pg[dc])[:, 0:512], in_=bk[:], func=AF.Tanh, scale=0.5), reads=[bkb], writes=[pbuf[sgl_pg[dc]]])
                ws_done()
            wPL, bPL = ws_get("PL")
            wpl3 = km(wPL, 2, 1024)
            for dc in range(8):
                bk, bkb = bank()
                S.mm([lambda e, bk=bk, k=k, dc=dc: e.matmul(bk[:], lhsT=wpl3[:, k, dc * 128:(dc + 1) * 128], rhs=pb(pT_pg[0], k), start=(k == 0), stop=(k == 1)) for k in range(2)],
                     reads=[bPL, pbuf[pT_pg[0]]], writes=[bkb])
                sv = pf(sgl_pg[dc])[:, 0:512]
                S.op("dve", lambda e, bk=bk, sv=sv: e.scalar_tensor_tensor(out=sv, in0=sv, scalar=1.0, in1=bk[:], op0=ALU.add, op1=ALU.mult), reads=[bkb, pbuf[sgl_pg[dc]]], writes=[pbuf[sgl_pg[dc]]])
                S.op("dve", lambda e, sv=sv, dc=dc: e.scalar_tensor_tensor(out=xT[:, dc, :], in0=sv, scalar=0.5, in1=xT[:, dc, :], op0=ALU.mult, op1=ALU.add), reads=[pbuf[sgl_pg[dc]], xT_b[dc]], writes=[xT_b[dc]])
            ws_done()
            pfree(sgl_pg); pfree(pT_pg)
            if l == 0 and j == 0:
                tap("xT3", xT[:].rearrange("p c t -> p (c t)"), xT_b, [128, 8 * T])

            S.phase = f"L{l}T{j}_p10_out"

            if l < DEPTH - 1:
                S.dma("sp", "st", xs_d[j], xT[:].rearrange("p c t -> p (c t)"), reads=xT_b, writes=[xs_b[j]])
            else:
                o_pg = [palloc_c(2) for _ in range(2)]
                for q in range(4):
                    pg4 = o_pg[q % 2]
                    ov = arena[:, pg4[0] * PAGE: pg4[0] * PAGE + 2048].bitcast(F32)
                    obufs = [pbuf[i] for i in pg4]
                    bks = [bank(), bank()]
                    for g2 in range(2):
                        bk, bkb = bks[g2]
                        S.mm([lambda e, bk=bk, c=c, q=q: e.transpose(out=bk[:, (c % 4) * 128:(c % 4 + 1) * 128], in_=xT[:, c, q * 128:(q + 1) * 128], identity=identf[:]) for c in range(g2 * 4, g2 * 4 + 4)],
                             reads=xT_b + [identf_b], writes=[bkb])
                    S.op("act", lambda e, ov=ov, bk=bks[0][0]: e.activation(out=ov[:, 0:512], in_=bk[:], func=AF.Square, accum_out=sm2[:, 32:33]), reads=[bks[0][1]], writes=obufs + [sm2_b])
                    S.op("act", lambda e, ov=ov, bk=bks[1][0]: e.activation(out=ov[:, 512:1024], in_=bk[:], func=AF.Square, accum_out=sm2[:, 33:34]), reads=[bks[1][1]], writes=obufs + [sm2_b])
                    S.op("dve", lambda e: e.tensor_tensor(out=sm2[:, 34:35], in0=sm2[:, 32:33], in1=sm2[:, 33:34], op=ALU.add), reads=[sm2_b], writes=[sm2_b])
                    S.op("act", lambda e: e.activation(out=sm2[:, 34:35], in_=sm2[:, 34:35], func=AF.Sqrt, scale=1.0 / D, bias=EPS), reads=[sm2_b], writes=[sm2_b])
                    S.op("dve", lambda e: e.reciprocal(out=sm2[:, 34:35], in_=sm2[:, 34:35]), reads=[sm2_b], writes=[sm2_b])
                    for g2 in range(2):
                        bk, bkb = bks[g2]
                        S.op("dve", lambda e, bk=bk, ov=ov, g2=g2: e.scalar_tensor_tensor(out=ov[:, g2 * 512:(g2 + 1) * 512], in0=bk[:], scalar=sm2[:, 34:35], in1=gfin[:, g2 * 512:(g2 + 1) * 512], op0=ALU.mult, op1=ALU.mult),
                             reads=[bkb, sm2_b, gfin_b] + obufs, writes=obufs)
                    S.dma("sp", "st", out_d[t0 + q * 128:t0 + (q + 1) * 128, :], ov, reads=obufs)
                for pg4 in o_pg:
                    pfree(pg4)
            if stop_after is not None and (l, j) == stop_after:
                break
        if stop_after is not None:
            break

    S.prog["sp"].append(lambda e: e.wait_ge(S.sems["st"], S.cnt["st"]))
    S.run()
    es.close()
    global LAST_SCHED
    LAST_SCHED = S
    S.min_free_pages = hw[0]
    return nc


def make_consts():
    inv = (np.float32(1.0) / np.power(np.float32(10000.0), np.arange(0, 32, 2, dtype=np.float32) / np.float32(32))).astype(np.float32)
    ropec = np.zeros((32, 2), np.float32)
    ropec[0:16, 0] = inv; ropec[16:32, 0] = inv
    ropec[0:16, 1] = -1.0; ropec[16:32, 1] = 1.0
    invc = np.zeros((128, 64), np.float32)
    for g, w in enumerate((2, 4, 8, 16)):
        for t in range(16):
            invc[:, g * 16 + t] = 1.0 / min(t + 1, w)
    return ropec, invc


def make_in_maps(inputs, cores):
    packs = [pack_layer(inputs, l) for l in range(DEPTH)]
    wflat = np.ascontiguousarray(np.concatenate([p_[0] for p_ in packs], 0))
    vecs = np.ascontiguousarray(np.stack([p_[1] for p_ in packs], 0))
    rows = np.ascontiguousarray(np.concatenate([p_[2] for p_ in packs], 1))
    gfin = np.asarray(inputs["g_final"], np.float32).reshape(1, D)
    ropec, invc = make_consts()
    x = np.asarray(inputs["x"], np.float32)
    p = np.asarray(inputs["p"], np.float32)
    pos = np.asarray(inputs["positions"]).astype(np.int32)
    maps = []
    for b in cores:
        maps.append({"x": np.ascontiguousarray(x[b]), "p": np.ascontiguousarray(p[:, b]), "pos": np.ascontiguousarray(pos[b:b + 1]),
                     "wflat": wflat, "vecs": vecs, "rows": rows, "gfin": gfin, "ropec": ropec, "invcnt": invc})
    return maps


def kernel(**inputs):
    nc = build_nc()
    maps = make_in_maps(inputs, list(range(8)))
    res = run_bass_kernel_spmd(nc, maps, core_ids=list(range(8)))
    out = np.stack([np.asarray(r["out"], np.float32) for r in res.results], 0)
    return out.astype(np.float32)
```

```python
import contextlib
import numpy as np
import concourse.bass as bass
import concourse.mybir as mybir
from concourse.bass_utils import run_bass_kernel_spmd

F32 = mybir.dt.float32
BF16 = mybir.dt.bfloat16
I32 = mybir.dt.int32
AF = mybir.ActivationFunctionType
ALU = mybir.AluOpType

S_LEN = 4096
D = 1024
T = 512
NT = S_LEN // T
DEPTH = 2
CH = 2048
NCH = 92
RING = 6
PAGE = 1056
NPAGE = 30
NV = 96
NR = 536
EPS = 1e-6
SCALE = 96 ** -0.5
TWO_PI = 6.283185307179586
MAGIC = 12582912.0

DEBUG_TAPS = {}
LAST_SCHED = None


class Buf:
    __slots__ = ("name", "lastw", "readers")

    def __init__(self, name=""):
        self.name = name
        self.lastw = None
        self.readers = {}


class _Rec:
    def __init__(self):
        self.call = None

    def __getattr__(self, name):
        def f(*args, **kwargs):
            self.call = (name, args, kwargs)
            return self
        return f


LAST_CALL = [None]


def _bind(fn):
    r = _Rec()
    fn(r)
    name, args, kwargs = r.call
    LAST_CALL[0] = r.call
    return lambda e: getattr(e, name)(*args, **kwargs)


class Sched:
    ENG = ("pe", "act", "dve", "pool", "sp")

    def __init__(self, nc):
        self.nc = nc
        self.prog = {e: [] for e in self.ENG}
        self.sems = {}
        self.cnt = {}
        self.known = {e: {} for e in self.ENG}
        self._sem_ctx = []
        self.dma_total = set()
        self.phase = "init"
        self.pe_phase = []
        self.pe_entry_phase = []
        self.mm_count = 0
        self.on_mm = None
        for e in ("pe", "act", "dve", "pool"):
            self.new_sem(e)

    def new_sem(self, name):
        cm = self.nc.semaphore(name)
        s = cm.__enter__()
        self._sem_ctx.append(cm)
        self.sems[name] = s
        self.cnt[name] = 0
        return name

    def _waits(self, eng, reads, writes):
        need = {}
        for b in reads:
            if b.lastw is not None:
                s, v = b.lastw
                if need.get(s, 0) < v:
                    need[s] = v
        for b in writes:
            if b.lastw is not None:
                s, v = b.lastw
                if need.get(s, 0) < v:
                    need[s] = v
            for s, v in b.readers.items():
                if need.get(s, 0) < v:
                    need[s] = v
        kn = self.known[eng]
        for s in list(need):
            if s in self.dma_total:
                need[s] = self.cnt[s]
        for s, v in need.items():
            if s == "pe" and eng == "pe":
                continue
            if kn.get(s, 0) >= v:
                continue
            kn[s] = v
            sem = self.sems[s]
            self.prog[eng].append(lambda e, sem=sem, v=v: e.wait_ge(sem, v))
            if eng == "pe":
                self.pe_entry_phase.append(self.phase)

    def _mark(self, sname, val, reads, writes):
        for b in writes:
            b.lastw = (sname, val)
            b.readers = {}
        for b in reads:
            b.readers[sname] = val

    def op(self, eng, fn, reads=(), writes=()):
        self._waits(eng, reads, writes)
        self.cnt[eng] += 1
        val = self.cnt[eng]
        sem = self.sems[eng]
        fn = _bind(fn)
        self.prog[eng].append(lambda e, fn=fn, sem=sem: fn(e).then_inc(sem, 1))
        self._mark(eng, val, reads, writes)

    def mm(self, fns, reads=(), writes=()):
        self._waits("pe", reads, writes)
        self.cnt["pe"] += 1
        val = self.cnt["pe"]
        sem = self.sems["pe"]
        fns2 = []
        for f in fns:
            fns2.append(_bind(f))
            nm, a_, kw_ = LAST_CALL[0]
            nsl = 1
            if nm == "matmul":
                lh = kw_.get("lhsT")
                if lh is not None and lh.dtype == F32:
                    nsl = 2
            self.pe_phase.extend([self.phase] * nsl)
            self.pe_entry_phase.append(self.phase)
        fns = fns2
        for f in fns[:-1]:
            self.prog["pe"].append(lambda e, f=f: f(e))
        self.prog["pe"].append(lambda e, f=fns[-1], sem=sem: f(e).then_inc(sem, 1))
        self._mark("pe", val, reads, writes)
        self.mm_count += len(fns)
        if self.on_mm is not None:
            self.on_mm()

    def dma(self, q, sname, out, in_, reads=(), writes=()):
        self._waits(q, reads, writes)
        self.cnt[sname] += 16
        val = self.cnt[sname]
        sem = self.sems[sname]
        self.prog[q].append(lambda e, out=out, in_=in_, sem=sem: e.dma_start(out=out, in_=in_).then_inc(sem, 16))
        self._mark(sname, val, reads, writes)

    def run(self):
        nc = self.nc
        with nc.Block() as block:
            @block.tensor
            def _(e):
                for f in self.prog["pe"]:
                    f(e)

            @block.scalar
            def _(e):
                for f in self.prog["act"]:
                    f(e)

            @block.vector
            def _(e):
                for f in self.prog["dve"]:
                    f(e)

            @block.gpsimd
            def _(e):
                for f in self.prog["pool"]:
                    f(e)

            @block.sync
            def _(e):
                for f in self.prog["sp"]:
                    f(e)
        for cm in reversed(self._sem_ctx):
            cm.__exit__(None, None, None)


W_CQ, W_CKV, W_KR, W_POOL, W_Z, W_XBC, W_DT, W_LG, W_LX, W_G = 0, 384, 640, 672, 1184, 1696, 2464, 2472, 2984, 3496


def _kmajor(w):
    k, n = w.shape
    kc = k // 128
    a = w.reshape(kc, 128, n).transpose(1, 0, 2).reshape(128, kc * n)
    out = np.zeros((128, CH), np.float32)
    out[:, : kc * n] = a
    return out


def chunk_names():
    names = ["A0", "A1", "A2", "UQn", "UQr", "UKV", "P0", "P1", "SMP", "LX0", "LX1", "CVL", "SML", "LG0", "LG1",
             "XB0", "XB1", "XB2", "CVS0", "CVS1", "Z0", "Z1", "DT"]
    for dg in range(4):
        names += [f"G0_{dg}", f"B0_{dg}", f"G1_{dg}", f"G2_{dg}", f"B12_{dg}", f"G3_{dg}", f"B3_{dg}"]
    names += [f"WO{i}" for i in range(4)]
    names += [f"F1_{i}" for i in range(16)]
    names += [f"F2_{i}" for i in range(16)]
    names += [f"PG{i}" for i in range(4)]
    names += ["PL"]
    assert len(names) == NCH, len(names)
    return names


def pack_layer(inp, l):
    win = np.asarray(inp["w_in"][l], np.float32)
    ch = {}
    ch["A0"] = _kmajor(win[:, 0:256])
    ch["A1"] = _kmajor(win[:, 256:512])
    a2 = np.zeros((1024, 256), np.float32)
    a2[:, 0:128] = win[:, 512:640]
    a2[:, 128:160] = win[:, 640:672]
    a2[:, 160:176] = win[:, 656:672]
    a2[:, 176:192] = win[:, 640:656]
    ch["A2"] = _kmajor(a2)
    wq = np.asarray(inp["w_uq"][l], np.float32)
    uqn = np.zeros((384, 512), np.float32)
    uqr = np.zeros((384, 512), np.float32)
    for h in range(8):
        uqn[:, h * 64:(h + 1) * 64] = wq[:, h * 96: h * 96 + 64]
        uqr[:, h * 32:(h + 1) * 32] = wq[:, h * 96 + 64: h * 96 + 96]
        uqr[:, 256 + h * 32: 256 + h * 32 + 16] = wq[:, h * 96 + 80: h * 96 + 96]
        uqr[:, 256 + h * 32 + 16: 256 + h * 32 + 32] = wq[:, h * 96 + 64: h * 96 + 80]
    ch["UQn"] = _kmajor(uqn)
    ch["UQr"] = _kmajor(uqr)
    wkv = np.asarray(inp["w_ukv"][l], np.float32)
    ukv = np.zeros((256, 1024), np.float32)
    for h in range(8):
        ukv[:, h * 64:(h + 1) * 64] = wkv[:, h * 128: h * 128 + 64]
        ukv[:, 512 + h * 64: 512 + (h + 1) * 64] = wkv[:, h * 128 + 64: h * 128 + 128]
    ch["UKV"] = _kmajor(ukv)
    ch["P0"] = _kmajor(win[:, W_POOL:W_POOL + 256])
    ch["P1"] = _kmajor(win[:, W_POOL + 256:W_POOL + 512])
    wp = np.asarray(inp["w_pool"][l], np.float32)
    smp = np.zeros((128, CH), np.float32)
    smp[:, 0:512] = wp.transpose(1, 0, 2).reshape(128, 512)
    ch["SMP"] = smp
    ch["LX0"] = _kmajor(win[:, W_LX:W_LX + 256])
    ch["LX1"] = _kmajor(win[:, W_LX + 256:W_LX + 512])
    sml = np.zeros((128, CH), np.float32)
    for wi, key in enumerate(("lru_w_a", "lru_w_i")):
        w = np.asarray(inp[key][l], np.float32)
        for c4 in range(4):
            bd = np.zeros((128, 128), np.float32)
            bd[0:64, 0:64] = w[2 * c4]
            bd[64:128, 64:128] = w[2 * c4 + 1]
            sml[:, wi * 512 + c4 * 128: wi * 512 + (c4 + 1) * 128] = bd
    ch["SML"] = sml
    lcw = np.asarray(inp["lru_conv_w"][l], np.float32)
    cvl = np.zeros((128, CH), np.float32)
    for k in range(4):
        for c in range(4):
            m = k * 4 + c
            cvl[np.arange(128), m * 128 + np.arange(128)] = lcw[k, c * 128:(c + 1) * 128]
    ch["CVL"] = cvl
    scw = np.asarray(inp["ssd_conv_w"][l], np.float32)
    cvs = [np.zeros((128, CH), np.float32), np.zeros((128, CH), np.float32)]
    for k in range(4):
        for c in range(6):
            m = k * 6 + c
            cvs[m // 16][np.arange(128), (m % 16) * 128 + np.arange(128)] = scw[k, c * 128:(c + 1) * 128]
    ch["CVS0"], ch["CVS1"] = cvs
    ch["LG0"] = _kmajor(win[:, W_LG:W_LG + 256])
    ch["LG1"] = _kmajor(win[:, W_LG + 256:W_LG + 512])
    for i in range(3):
        ch[f"XB{i}"] = _kmajor(win[:, W_XBC + i * 256: W_XBC + (i + 1) * 256])
    ch["Z0"] = _kmajor(win[:, W_Z:W_Z + 256])
    ch["Z1"] = _kmajor(win[:, W_Z + 256:W_Z + 512])
    ch["DT"] = _kmajor(win[:, W_DT:W_DT + 8])
    wb = np.asarray(inp["w_branch"][l], np.float32)
    for dg in range(4):
        cs = slice(dg * 256, (dg + 1) * 256)
        for n in range(4):
            ch[f"G{n}_{dg}"] = _kmajor(win[:, W_G + n * 1024 + dg * 256: W_G + n * 1024 + (dg + 1) * 256])
        b0 = np.zeros((128, CH), np.float32)
        b0[0:64, :] = wb[0][:, cs].reshape(8, 64, 256).transpose(1, 0, 2).reshape(64, 2048)
        ch[f"B0_{dg}"] = b0
        b12 = np.zeros((128, CH), np.float32)
        b12[:, 0:1024] = _kmajor(wb[1][:, cs])[:, 0:1024]
        b12[:, 1024:2048] = _kmajor(wb[2][:, cs])[:, 0:1024]
        ch[f"B12_{dg}"] = b12
        ch[f"B3_{dg}"] = _kmajor(wb[3][:, cs])
    wo = np.asarray(inp["w_out"][l], np.float32)
    for i in range(4):
        ch[f"WO{i}"] = _kmajor(wo[:, i * 256:(i + 1) * 256])
    w1 = np.asarray(inp["w_ff1"][l], np.float32)
    for i in range(16):
        ch[f"F1_{i}"] = _kmajor(w1[:, i * 256:(i + 1) * 256])
    w2 = np.asarray(inp["w_ff2"][l], np.float32)
    for dc in range(8):
        for half in range(2):
            ch[f"F2_{dc * 2 + half}"] = _kmajor(w2[half * 2048:(half + 1) * 2048, dc * 128:(dc + 1) * 128])
    wg = np.asarray(inp["w_ple_gate"][l], np.float32)
    for i in range(4):
        ch[f"PG{i}"] = _kmajor(wg[:, i * 256:(i + 1) * 256])
    ch["PL"] = _kmajor(np.asarray(inp["w_ple"][l], np.float32))
    wflat = np.stack([ch[n] for n in chunk_names()], 0)

    def col(v):
        v = np.asarray(v, np.float32)
        return v.reshape(-1, 128).T

    vecs = np.zeros((128, NV), np.float32)
    vecs[:, 0:8] = col(inp["g_mix"][l]); vecs[:, 8:16] = col(inp["g_mlp"][l]); vecs[:, 16:24] = col(inp["g_ple"][l])
    vecs[:, 24:27] = col(inp["q_norm"][l]); vecs[:, 27:29] = col(inp["kv_norm"][l])
    vecs[:, 29:33] = col(inp["pool_scale"][l]); vecs[:, 33:37] = col(inp["lru_conv_b"][l])
    vecs[:, 37:41] = col(inp["lru_b_a"][l]); vecs[:, 41:45] = col(inp["lru_b_i"][l]); vecs[:, 45:49] = col(inp["lru_lambda"][l])
    vecs[:, 49:55] = col(inp["ssd_conv_b"][l])
    for k in range(4):
        vecs[:, 55 + k * 4: 55 + k * 4 + 4] = col(inp["lru_conv_w"][l][k])
        vecs[:, 71 + k * 6: 71 + k * 6 + 6] = col(inp["ssd_conv_w"][l][k])
    rows = np.zeros((1, NR), np.float32)
    rows[0, 0:8] = inp["ssd_dt_bias"][l]; rows[0, 8:16] = inp["ssd_a_log"][l]; rows[0, 16:24] = inp["ssd_d"][l]
    rows[0, 24:536] = inp["ssd_norm"][l]
    return wflat, vecs, rows


def build_nc(taps=None, nlayers=DEPTH, ntiles=NT, stop_after=None):
    taps = taps or []
    nc = bass.Bass("TRN2", target_bir_lowering=False)
    x_d = nc.dram_tensor("x", [S_LEN, D], F32, kind="ExternalInput").ap()
    p_d = nc.dram_tensor("p", [DEPTH, S_LEN, 256], F32, kind="ExternalInput").ap()
    pos_d = nc.dram_tensor("pos", [1, S_LEN], I32, kind="ExternalInput").ap()
    wf_d = nc.dram_tensor("wflat", [DEPTH * NCH, 128, CH], F32, kind="ExternalInput").ap()
    vecs_d = nc.dram_tensor("vecs", [DEPTH, 128, NV], F32, kind="ExternalInput").ap()
    rows_d = nc.dram_tensor("rows", [1, DEPTH * NR], F32, kind="ExternalInput").ap()
    gfin_d = nc.dram_tensor("gfin", [1, D], F32, kind="ExternalInput").ap()
    ropec_d = nc.dram_tensor("ropec", [32, 2], F32, kind="ExternalInput").ap()
    invc_d = nc.dram_tensor("invcnt", [128, 64], F32, kind="ExternalInput").ap()
    out_d = nc.dram_tensor("out", [S_LEN, D], F32, kind="ExternalOutput").ap()
    wbf_d = nc.dram_tensor("wbf", [DEPTH * NCH, 128, CH], BF16, kind="Internal").ap()
    xs_d = nc.dram_tensor("xs", [NT, 128, 8 * T], F32, kind="Internal").ap()
    tap_d = {}

    S = Sched(nc)
    for s in ("st", "pst"):
        S.new_sem(s)
        S.dma_total.add(s)
    for s in ("ld0", "ld1", "ld2", "ld3", "xl0", "xl1", "pl0", "pl1", "pl2", "pl3", "pst0", "pst1", "c_ropec", "c_invc", "c_gfin", "c_vecs", "c_rows", "c_pos"):
        S.new_sem(s)
    xs_b = [Buf(f"xs{i}") for i in range(NT)]
    ring_sems = [S.new_sem(f"r{i}") for i in range(RING)]
    es = contextlib.ExitStack()
    _n = [0]

    def sb(shape, dt, name=None):
        _n[0] += 1
        return es.enter_context(nc.sbuf_tensor("s_" + (name or f"sb{_n[0]}"), shape, dt))

    def psb(shape, dt, name=None):
        _n[0] += 1
        return es.enter_context(nc.psum_tensor(name or f"ps{_n[0]}", shape, dt))

    arena = sb([128, NPAGE * PAGE], BF16, "arena")
    pbuf = [Buf(f"pg{i}") for i in range(NPAGE)]
    free_pages = list(range(NPAGE))

    hw = [NPAGE]

    def palloc(n=1):
        assert len(free_pages) >= n, "out of pages"
        r = [free_pages.pop(0) for _ in range(n)]
        hw[0] = min(hw[0], len(free_pages))
        return r

    def pfree(pgs):
        free_pages.extend(pgs)

    def palloc_c(n):
        fs = sorted(free_pages)
        for i in range(len(fs) - n + 1):
            if fs[i + n - 1] - fs[i] == n - 1:
                r = fs[i:i + n]
                for x_ in r:
                    free_pages.remove(x_)
                return r
        raise AssertionError("no contiguous pages")

    def pb(pg, half):
        return arena[:, pg * PAGE + half * 512: pg * PAGE + (half + 1) * 512]

    def pf(pg):
        return arena[:, pg * PAGE: (pg + 1) * PAGE].bitcast(F32)

    kn_c = sb([128, 4, S_LEN], BF16, "kn_c"); kn_b = [Buf() for _ in range(NT)]
    kr_c = sb([128, S_LEN], BF16, "kr_c"); kr_b = [Buf() for _ in range(NT)]
    v_cf = sb([128, (S_LEN // 128) * 8 * 65 + 64], BF16, "v_c")
    v_c = v_cf[:, 0:(S_LEN // 128) * 8 * 65].rearrange("p (b h d) -> p b h d", b=S_LEN // 128, h=8); v_b = [Buf() for _ in range(NT)]
    xT = sb([128, 8, T], F32, "xT"); xT_b = [Buf() for _ in range(8)]
    hT = sb([128, 8, T], BF16, "hT"); hT_b = [Buf() for _ in range(8)]
    mT = sb([128, 8, T], BF16, "mT"); mT_b = [Buf() for _ in range(8)]
    ring = sb([128, RING, CH], BF16, "ring"); ring_b = [Buf() for _ in range(RING)]
    ps_t = [psb([128, 512], F32, f"bank{i}") for i in range(8)]
    ps_b = [Buf(f"bank{i}") for i in range(8)]
    rot = [0]

    rotn = [5]

    def bank():
        i = rot[0] % rotn[0]
        rot[0] += 1
        return ps_t[i], ps_b[i]

    identf = sb([128, 128], F32, "identf"); identf_b = Buf()
    identb = sb([128, 128], BF16, "identb"); identb_b = Buf()
    onesb = sb([128, 128], BF16, "onesb"); onesb_b = Buf()
    onesf = sb([128, 128], F32, "onesf"); onesf_b = Buf()
    U_le = sb([128, 128], F32, "U_le"); U_b = Buf()
    SU_gt = sb([128, 128], F32, "SU_gt"); SU_b = Buf()
    cmask = sb([128, 128], BF16, "cmask"); cmask_b = Buf()
    neghalf = sb([128, 1], F32, "neghalf"); neghalf_b = Buf()
    vecs = sb([128, NV], F32, "vecs"); vecs_b = Buf()
    rows = sb([128, NR], F32, "rows"); rows_b = Buf()
    gfin = sb([128, D], F32, "gfin"); gfin_b = Buf()
    ropec = sb([32, 2], F32, "ropec"); ropec_b = Buf()
    invc = sb([128, 64], F32, "invc"); invc_b = Buf()
    lcv = sb([128, 4], F32, "lcv"); lcv_b = Buf()
    vhalf = sb([128, 48], F32, "vhalf"); vhalf_b = Buf()
    Abc = sb([128, 8], F32, "Abc"); Abc_b = Buf()
    Sst = sb([128, 4, 64], F32, "Sst"); Sst_b = Buf()
    Sbf = sb([128, 4, 64], BF16, "Sbf"); Sbf_b = Buf()
    hcar = sb([128, 4], F32, "hcar"); hcar_b = Buf()
    hal_pool = sb([128, 4, 16], F32, "hal_pool"); hal_pool_b = Buf()
    hal_lx = sb([128, 4, 3], F32, "hal_lx"); hal_lx_b = Buf()
    hal_xb = sb([128, 6, 3], F32, "hal_xb"); hal_xb_b = Buf()
    sm = sb([128, 256], F32, "sm"); sm_b = Buf(); sm_bs = [Buf(), Buf()]
    sm2 = sb([128, 64], F32, "sm2"); sm2_b = Buf()

    def tap(name, ap, bufs, shape, dt=F32):
        if name not in taps:
            return
        if name not in tap_d:
            tap_d[name] = nc.dram_tensor("tap_" + name, list(shape), dt, kind="ExternalOutput").ap()
        S.dma("sp", "st", tap_d[name], ap, reads=bufs)

    S.dma("sp", "c_ropec", ropec[:], ropec_d, writes=[ropec_b])
    S.dma("sp", "c_invc", invc[:], invc_d, writes=[invc_b])
    S.dma("sp", "c_gfin", gfin[:], gfin_d.partition_broadcast(128), writes=[gfin_b])
    S.op("pool", lambda e: e.memset(identf[:], 0.0), writes=[identf_b])
    S.op("pool", lambda e: e.affine_select(out=identf[:], in_=identf[:], compare_op=ALU.not_equal, fill=1.0, base=0,
                                           pattern=[[-1, 128]], channel_multiplier=1), reads=[identf_b], writes=[identf_b])
    S.op("pool", lambda e: e.tensor_copy(out=identb[:], in_=identf[:]), reads=[identf_b], writes=[identb_b])
    S.op("pool", lambda e: e.memset(onesb[:], 1.0), writes=[onesb_b])
    S.op("pool", lambda e: e.memset(onesf[:], 1.0), writes=[onesf_b])
    S.op("pool", lambda e: e.memset(neghalf[:], -0.5), writes=[neghalf_b])
    S.op("pool", lambda e: e.memset(U_le[:], 1.0), writes=[U_b])
    S.op("pool", lambda e: e.affine_select(out=U_le[:], in_=U_le[:], compare_op=ALU.is_ge, fill=0.0, base=0,
                                           pattern=[[1, 128]], channel_multiplier=-1), reads=[U_b], writes=[U_b])
    S.op("pool", lambda e: e.memset(SU_gt[:], 1.0), writes=[SU_b])
    S.op("pool", lambda e: e.affine_select(out=SU_gt[:], in_=SU_gt[:], compare_op=ALU.is_gt, fill=0.0, base=0,
                                           pattern=[[-1, 128]], channel_multiplier=1), reads=[SU_b], writes=[SU_b])
    S.op("pool", lambda e: e.tensor_copy(out=cmask[:], in_=U_le[:]), reads=[U_b], writes=[cmask_b])
    S.op("pool", lambda e: e.memset(v_cf[:], 0.0), writes=v_b)
    S.op("pool", lambda e: e.memset(v_c[:, :, :, 64:65], 1.0), writes=v_b)
    S.op("pool", lambda e: e.memset(kr_c[:], 0.0), writes=kr_b)

    nchunks_total = nlayers * NCH
    wbf_b = [Buf(f"wbf{i}") for i in range(nchunks_total)]
    CS = 4
    HEAD_CAST = 8
    cast_sems = [S.new_sem(f"cs{i}") for i in range(CS)]
    cst = {"next": 0}

    def cast_emit(n=1):
        for _ in range(n):
            i = cst["next"]
            if i >= nchunks_total:
                return
            S.dma("pool", cast_sems[i % CS], wbf_d[i], wf_d[i], reads=[wbf_b[i - CS]] if i >= CS else [], writes=[wbf_b[i]])
            cst["next"] += 1

    cast_emit(HEAD_CAST)
    pace = {"last": 0}

    def _pace():
        if cst["next"] < nchunks_total and S.mm_count - pace["last"] >= 8:
            pace["last"] = S.mm_count
            cast_emit(1)
    S.on_mm = _pace

    names = chunk_names()
    seq = [(l, j, c) for l in range(nlayers) for j in range(ntiles) for c in range(NCH)]
    ws = {"next_dma": 0, "next_get": 0, "consumed": 0}

    def ws_pump():
        while ws["next_dma"] < len(seq) and ws["next_dma"] < ws["consumed"] + RING:
            i = ws["next_dma"]
            l, j, c = seq[i]
            slot = i % RING
            gi = l * NCH + c
            while cst["next"] <= gi:
                cast_emit()
            S.dma("sp", ring_sems[slot], ring[:, slot, :], wbf_d[gi], reads=[wbf_b[gi]], writes=[ring_b[slot]])
            ws["next_dma"] += 1

    def ws_get(name):
        i = ws["next_get"]
        l, j, c = seq[i]
        assert names[c] == name, (names[c], name)
        assert i < ws["consumed"] + RING
        ws_pump()
        ws["next_get"] += 1
        slot = i % RING
        return ring[:, slot, :], ring_b[slot]

    def ws_done():
        ws["consumed"] = ws["next_get"]
        ws_pump()

    def km(view, kc, ncol):
        return view[:, 0:kc * ncol].rearrange("p (k c) -> p k c", k=kc)

    def rms_bc(srcs, src_bufs, nfeat):
        bk, bkb = bank()
        sq_pg = palloc(1)
        n = len(srcs)
        for c, (s_, b_) in enumerate(zip(srcs, src_bufs)):
            sq = pb(sq_pg[0], c % 2)
            S.op("act", lambda e, sq=sq, s_=s_: e.activation(out=sq, in_=s_, func=AF.Square), reads=[b_], writes=[pbuf[sq_pg[0]]])
            S.mm([lambda e, sq=sq, c=c: e.matmul(bk[:], lhsT=onesb[:], rhs=sq, start=(c == 0), stop=(c == n - 1))],
                 reads=[pbuf[sq_pg[0]], onesb_b], writes=[bkb])
        pfree(sq_pg)
        rp = palloc(1)
        rv = pf(rp[0])[:, 0:512]
        S.op("act", lambda e: e.activation(out=rv, in_=bk[:], func=AF.Sqrt, scale=1.0 / nfeat, bias=EPS), reads=[bkb], writes=[pbuf[rp[0]]])
        S.op("dve", lambda e: e.reciprocal(out=rv, in_=rv), reads=[pbuf[rp[0]]], writes=[pbuf[rp[0]]])
        pfree(rp)
        return rv, pbuf[rp[0]]

    def norm_x_to_h(gcol0):
        rv, rvb = rms_bc([xT[:, c, :] for c in range(8)], xT_b, D)
        for c in range(8):
            S.op("dve", lambda e, c=c: e.scalar_tensor_tensor(out=hT[:, c, :], in0=xT[:, c, :], scalar=vecs[:, gcol0 + c: gcol0 + c + 1],
                                                              in1=rv, op0=ALU.mult, op1=ALU.mult),
                 reads=[xT_b[c], rvb, vecs_b], writes=[hT_b[c]])

    def proj_fm(wview, wbuf, kc, ncol, col0, m, rhs_fn, rhs_bufs):
        bk, bkb = bank()
        w3 = km(wview, kc, ncol)
        S.mm([lambda e, k=k: e.matmul(bk[0:m, :], lhsT=w3[:, k, col0:col0 + m], rhs=rhs_fn(k), start=(k == 0), stop=(k == kc - 1))
              for k in range(kc)], reads=[wbuf] + list(rhs_bufs), writes=[bkb])
        return bk, bkb

    hrhs = lambda k: hT[:, k, :]

    import os as _os2
    _dbgstop = _os2.environ.get('DBG_STOP', '')
    for l in range(nlayers):
        S.dma("sp", "c_vecs", vecs[:], vecs_d[l], writes=[vecs_b])
        S.dma("sp", "c_rows", rows[:], rows_d[:, l * NR:(l + 1) * NR].partition_broadcast(128), writes=[rows_b])
        S.op("act", lambda e: e.activation(out=lcv[:], in_=vecs[:, 45:49], func=AF.Exp, scale=-1.0), reads=[vecs_b], writes=[lcv_b])
        S.op("act", lambda e: e.activation(out=lcv[:], in_=lcv[:], func=AF.Ln, bias=1.0), reads=[lcv_b], writes=[lcv_b])
        S.op("dve", lambda e: e.tensor_scalar(out=lcv[:], in0=lcv[:], scalar1=-8.0, scalar2=None, op0=ALU.mult), reads=[lcv_b], writes=[lcv_b])
        S.op("dve", lambda e: e.tensor_scalar(out=vhalf[:, 0:8], in0=vecs[:, 37:45], scalar1=0.5, scalar2=None, op0=ALU.mult), reads=[vecs_b], writes=[vhalf_b])
        S.op("dve", lambda e: e.tensor_scalar(out=vhalf[:, 8:12], in0=lcv[:], scalar1=0.5, scalar2=None, op0=ALU.mult), reads=[lcv_b, vhalf_b], writes=[vhalf_b])
        S.op("dve", lambda e: e.tensor_scalar(out=vhalf[:, 12:18], in0=vecs[:, 49:55], scalar1=0.5, scalar2=None, op0=ALU.mult), reads=[vecs_b, vhalf_b], writes=[vhalf_b])
        S.op("dve", lambda e: e.tensor_scalar(out=vhalf[:, 18:42], in0=vecs[:, 71:95], scalar1=0.5, scalar2=None, op0=ALU.mult), reads=[vecs_b, vhalf_b], writes=[vhalf_b])
        S.op("act", lambda e: e.activation(out=Abc[:], in_=rows[:, 8:16], func=AF.Exp), reads=[rows_b], writes=[Abc_b])
        S.op("dve", lambda e: e.tensor_scalar(out=Abc[:], in0=Abc[:], scalar1=-1.0, scalar2=None, op0=ALU.mult), reads=[Abc_b], writes=[Abc_b])
        S.op("pool", lambda e: e.memset(Sst[:], 0.0), writes=[Sst_b])
        S.op("pool", lambda e: e.memset(Sbf[:], 0.0), writes=[Sbf_b])
        S.op("pool", lambda e: e.memset(hcar[:], 0.0), writes=[hcar_b])
        S.op("pool", lambda e: e.memset(hal_pool[:], 0.0), writes=[hal_pool_b])
        S.op("pool", lambda e: e.memset(hal_lx[:], 0.0), writes=[hal_lx_b])
        S.op("pool", lambda e: e.memset(hal_xb[:], 0.0), writes=[hal_xb_b])

        for j in range(ntiles):
            t0 = j * T
            rotn[0] = 5
            S.phase = f"L{l}T{j}_p0_x"

            if l == 0:
                xpg = [palloc_c(2) for _ in range(2)]
                for q in range(4):
                    pg4 = xpg[q % 2]
                    xv = arena[:, pg4[0] * PAGE: pg4[0] * PAGE + 2048].bitcast(F32)
                    bufs4 = [pbuf[i] for i in pg4]
                    S.dma("sp", f"xl{q % 2}", xv, x_d[t0 + q * 128: t0 + (q + 1) * 128, :], writes=bufs4)
                    for c in range(8):
                        pass
                    for g2 in range(2):
                        bk, bkb = bank()
                        S.mm([lambda e, c=c, bk=bk, xv=xv: e.transpose(out=bk[:, (c % 4) * 128:(c % 4 + 1) * 128], in_=xv[:, c * 128:(c + 1) * 128], identity=identf[:])
                              for c in range(g2 * 4, g2 * 4 + 4)], reads=bufs4 + [identf_b], writes=[bkb])
                        eng = "act" if g2 == 0 else "dve"
                        outv = xT[:, g2 * 4:(g2 + 1) * 4, q * 128:(q + 1) * 128]
                        inv_ = bk[:].rearrange("p (c t) -> p c t", c=4)
                        if eng == "act":
                            S.op("act", lambda e, o=outv, i_=inv_: e.activation(out=o, in_=i_, func=AF.Copy), reads=[bkb], writes=xT_b[g2 * 4:(g2 + 1) * 4])
                        else:
                            S.op("dve", lambda e, o=outv, i_=inv_: e.tensor_copy(out=o, in_=i_), reads=[bkb], writes=xT_b[g2 * 4:(g2 + 1) * 4])
                for pg4 in xpg:
                    pfree(pg4)
            else:
                S.dma("sp", "xl0", xT[:].rearrange("p c t -> p (c t)"), xs_d[j], reads=[xs_b[j]], writes=xT_b)
            if l == 0 and j == 0:
                tap("xT0", xT[:].rearrange("p c t -> p (c t)"), xT_b, [128, 8 * T])

            S.phase = f"L{l}T{j}_p1_norm"

            norm_x_to_h(0)
            if l == 0 and j == 0:
                tap("hT0", hT[:].rearrange("p c t -> p (c t)"), hT_b, [128, 8 * T], BF16)

            S.phase = f"L{l}T{j}_p2_mla_proj"

            cq_pg = palloc(3); ckv_pg = palloc(2); kr_pg = palloc(2)
            wA0, bA0 = ws_get("A0")
            for cb in range(2):
                bk, bkb = proj_fm(wA0, bA0, 8, 256, cb * 128, 128, hrhs, hT_b)
                S.op("act", lambda e, bk=bk, cb=cb: e.activation(out=pf(cq_pg[cb])[:, 0:512], in_=bk[:], func=AF.Copy), reads=[bkb], writes=[pbuf[cq_pg[cb]]])
            ws_done()
            wA1, bA1 = ws_get("A1")
            bk, bkb = proj_fm(wA1, bA1, 8, 256, 0, 128, hrhs, hT_b)
            S.op("act", lambda e, bk=bk: e.activation(out=pf(cq_pg[2])[:, 0:512], in_=bk[:], func=AF.Copy), reads=[bkb], writes=[pbuf[cq_pg[2]]])
            bk, bkb = proj_fm(wA1, bA1, 8, 256, 128, 128, hrhs, hT_b)
            S.op("dve", lambda e, bk=bk: e.tensor_copy(out=pf(ckv_pg[0])[:, 0:512], in_=bk[:]), reads=[bkb], writes=[pbuf[ckv_pg[0]]])
            ws_done()
            wA2, bA2 = ws_get("A2")
            bk, bkb = proj_fm(wA2, bA2, 8, 256, 0, 128, hrhs, hT_b)
            S.op("dve", lambda e, bk=bk: e.tensor_copy(out=pf(ckv_pg[1])[:, 0:512], in_=bk[:]), reads=[bkb], writes=[pbuf[ckv_pg[1]]])
            for i2 in range(2):
                bk, bkb = proj_fm(wA2, bA2, 8, 256, 128 + 32 * i2, 32, hrhs, hT_b)
                S.op("act", lambda e, bk=bk, i2=i2: e.activation(out=pf(kr_pg[i2])[0:32, 0:512], in_=bk[0:32, :], func=AF.Copy), reads=[bkb], writes=[pbuf[kr_pg[i2]]])
            ws_done()
            cqn_pg = palloc(2); ckvn_pg = palloc(1)
            cqn = [pb(cqn_pg[c // 2], c % 2) for c in range(3)]
            cqn_bufs = [pbuf[cqn_pg[c // 2]] for c in range(3)]
            ckvn = [pb(ckvn_pg[0], c) for c in range(2)]
            rv, rvb = rms_bc([pf(cq_pg[c])[:, 0:512] for c in range(3)], [pbuf[i] for i in cq_pg], 384)
            for c in range(3):
                S.op("dve", lambda e, c=c: e.scalar_tensor_tensor(out=cqn[c], in0=pf(cq_pg[c])[:, 0:512], scalar=vecs[:, 24 + c:25 + c], in1=rv,
                                                                  op0=ALU.mult, op1=ALU.mult), reads=[pbuf[cq_pg[c]], rvb, vecs_b], writes=[cqn_bufs[c]])
            rv, rvb = rms_bc([pf(ckv_pg[c])[:, 0:512] for c in range(2)], [pbuf[i] for i in ckv_pg], 256)
            for c in range(2):
                S.op("dve", lambda e, c=c: e.scalar_tensor_tensor(out=ckvn[c], in0=pf(ckv_pg[c])[:, 0:512], scalar=vecs[:, 27 + c:28 + c], in1=rv,
                                                                  op0=ALU.mult, op1=ALU.mult), reads=[pbuf[ckv_pg[c]], rvb, vecs_b], writes=[pbuf[ckvn_pg[0]]])
            pfree(cq_pg); pfree(ckv_pg)
            tb_pg = palloc(4)
            cosv = pf(tb_pg[0])[0:32, 0:512]; sinv = pf(tb_pg[1])[0:32, 0:512]
            t1 = pf(tb_pg[2])[0:32, 0:512]; t2 = pf(tb_pg[3])[0:32, 0:512]
            tb = [pbuf[i] for i in tb_pg]
            posi = arena[0:32, tb_pg[3] * PAGE: tb_pg[3] * PAGE + 1024].bitcast(I32)
            S.dma("sp", "c_pos", posi, pos_d[:, t0:t0 + T].partition_broadcast(32), writes=[tb[3]])
            S.op("dve", lambda e: e.tensor_scalar(out=t1, in0=posi, scalar1=ropec[:, 0:1], scalar2=None, op0=ALU.mult),
                 reads=[tb[3], ropec_b], writes=[tb[2]])
            for which, dst, shift in ((0, sinv, 0.0), (1, cosv, 1.5707963267948966)):
                db = tb[1] if which == 0 else tb[0]
                S.op("dve", lambda e, shift=shift: e.tensor_scalar(out=t2, in0=t1, scalar1=shift, scalar2=1.0 / TWO_PI, op0=ALU.add, op1=ALU.mult),
                     reads=[tb[2]], writes=[tb[3]])
                S.op("dve", lambda e: e.tensor_scalar(out=t2, in0=t2, scalar1=MAGIC, scalar2=None, op0=ALU.add), reads=[tb[3]], writes=[tb[3]])
                S.op("dve", lambda e: e.tensor_scalar(out=t2, in0=t2, scalar1=-MAGIC, scalar2=None, op0=ALU.add), reads=[tb[3]], writes=[tb[3]])
                S.op("dve", lambda e, dst=dst: e.scalar_tensor_tensor(out=dst, in0=t2, scalar=-6.28125, in1=t1, op0=ALU.mult, op1=ALU.add),
                     reads=[tb[3], tb[2]], writes=[db])
                S.op("dve", lambda e, dst=dst: e.scalar_tensor_tensor(out=dst, in0=t2, scalar=-(TWO_PI - 6.28125), in1=dst, op0=ALU.mult, op1=ALU.add),
                     reads=[tb[3], db], writes=[db])
                S.op("dve", lambda e, dst=dst, shift=shift: e.tensor_scalar(out=dst, in0=dst, scalar1=shift, scalar2=3.14159, op0=ALU.add, op1=ALU.min),
                     reads=[db], writes=[db])
                S.op("dve", lambda e, dst=dst: e.tensor_scalar(out=dst, in0=dst, scalar1=-3.14159, scalar2=None, op0=ALU.max), reads=[db], writes=[db])
                S.op("act", lambda e, dst=dst: e.activation(out=dst, in_=dst, func=AF.Sin), reads=[db], writes=[db])
            S.op("dve", lambda e: e.tensor_scalar(out=sinv, in0=sinv, scalar1=ropec[:, 1:2], scalar2=None, op0=ALU.mult), reads=[tb[1], ropec_b], writes=[tb[1]])
            if l == 0 and j == 0:
                tap("cos0", cosv, [tb[0]], [32, 512]); tap("sin0", sinv, [tb[1]], [32, 512])
            S.op("dve", lambda e: e.tensor_tensor(out=t1, in0=pf(kr_pg[0])[0:32, 0:512], in1=cosv, op=ALU.mult), reads=[pbuf[kr_pg[0]], tb[0]], writes=[tb[2]])
            S.op("dve", lambda e: e.tensor_tensor(out=t2, in0=pf(kr_pg[1])[0:32, 0:512], in1=sinv, op=ALU.mult), reads=[pbuf[kr_pg[1]], tb[1]], writes=[tb[3]])
            S.op("dve", lambda e: e.tensor_tensor(out=kr_c[0:32, t0:t0 + T], in0=t1, in1=t2, op=ALU.add), reads=[tb[2], tb[3]], writes=[kr_b[j]])
            pfree(kr_pg)
            qz_pg = palloc(4); qr_pg = palloc(4)
            for pg_ in qz_pg + qr_pg:
                S.op("pool", lambda e, pg_=pg_: e.memset(arena[:, pg_ * PAGE: pg_ * PAGE + 1024], 0.0), writes=[pbuf[pg_]])
            wq, bq = ws_get("UQn")
            for pr in range(4):
                bk, bkb = proj_fm(wq, bq, 3, 512, pr * 128, 128, lambda k: cqn[k], cqn_bufs)
                hA, hB = 2 * pr, 2 * pr + 1
                S.op("act", lambda e, bk=bk, hA=hA: e.activation(out=pb(qz_pg[hA // 2], hA % 2)[0:64, :], in_=bk[0:64, :], func=AF.Copy), reads=[bkb], writes=[pbuf[qz_pg[hA // 2]]])
                S.op("dve", lambda e, bk=bk, hB=hB: e.tensor_copy(out=pb(qz_pg[hB // 2], hB % 2)[64:128, :], in_=bk[64:128, :]), reads=[bkb], writes=[pbuf[qz_pg[hB // 2]]])
            ws_done()
            wq, bq = ws_get("UQr")
            for h in range(8):
                bk1, bkb1 = proj_fm(wq, bq, 3, 512, h * 32, 32, lambda k: cqn[k], cqn_bufs)
                bk2, bkb2 = proj_fm(wq, bq, 3, 512, 256 + h * 32, 32, lambda k: cqn[k], cqn_bufs)
                S.op("dve", lambda e, bk1=bk1: e.tensor_tensor(out=t1, in0=bk1[0:32, :], in1=cosv, op=ALU.mult), reads=[bkb1, tb[0]], writes=[tb[2]])
                S.op("dve", lambda e, bk2=bk2: e.tensor_tensor(out=t2, in0=bk2[0:32, :], in1=sinv, op=ALU.mult), reads=[bkb2, tb[1]], writes=[tb[3]])
                dst = pb(qr_pg[h // 2], h % 2)[0:32, :]
                S.op("pool", lambda e, dst=dst: e.tensor_tensor(out=dst, in0=t1, in1=t2, op=ALU.add), reads=[tb[2], tb[3]], writes=[pbuf[qr_pg[h // 2]]])
            ws_done()
            pfree(tb_pg)
            wkv, bkv = ws_get("UKV")
            for pr in range(4):
                bk, bkb = proj_fm(wkv, bkv, 2, 1024, pr * 128, 128, lambda k: ckvn[k], [pbuf[ckvn_pg[0]]])
                if pr % 2 == 0:
                    S.op("act", lambda e, bk=bk, pr=pr: e.activation(out=kn_c[:, pr, t0:t0 + T], in_=bk[:], func=AF.Copy), reads=[bkb], writes=[kn_b[j]])
                else:
                    S.op("dve", lambda e, bk=bk, pr=pr: e.tensor_copy(out=kn_c[:, pr, t0:t0 + T], in_=bk[:]), reads=[bkb], writes=[kn_b[j]])
            w3 = km(wkv, 2, 1024)
            for q in range(4):
                bk, bkb = bank()
                S.mm([lambda e, k=k, q=q, bk=bk: e.matmul(bk[:], lhsT=ckvn[k][:, q * 128:(q + 1) * 128], rhs=w3[:, k, 512:1024], start=(k == 0), stop=(k == 1))
                      for k in range(2)], reads=[bkv, pbuf[ckvn_pg[0]]], writes=[bkb])
                dst = v_c[:, j * 4 + q, :, 0:64]
                src = bk[:].rearrange("p (h d) -> p h d", h=8)
                if q % 2 == 0:
                    S.op("act", lambda e, dst=dst, src=src: e.activation(out=dst, in_=src, func=AF.Copy), reads=[bkb], writes=[v_b[j]])
                else:
                    S.op("dve", lambda e, dst=dst, src=src: e.tensor_copy(out=dst, in_=src), reads=[bkb], writes=[v_b[j]])
            ws_done()
            pfree(cqn_pg); pfree(ckvn_pg)
            if l == 0 and j == 0:
                tap("qn0", pb(qz_pg[0], 0), [pbuf[qz_pg[0]]], [128, 512], BF16)
                tap("qr0", pb(qr_pg[0], 0)[0:32, :], [pbuf[qr_pg[0]]], [32, 512], BF16)
                tap("kn0", kn_c[:, 0, 0:512], [kn_b[0]], [128, 512], BF16)
                tap("krc", kr_c[0:32, 0:512], [kr_b[0]], [32, 512], BF16)
                tap("vc0", v_c[:, 0, 0, :], [v_b[0]], [128, 65], BF16)
            S.phase = f"L{l}T{j}_p2_attn"
            ya_pg = palloc(4)
            pt_pg = palloc(2)
            nrm_pg = palloc(3)
            nkb = 4 * (j + 1)
            units = [(h, kb) for h in range(8) for kb in range(nkb)]
            st_bank = {}
            st_pt = {}

            def stage_qk(u):
                h, kb = units[u]
                q0 = 0 if kb < 4 * j else 128 * (kb - 4 * j)
                n = 512 - q0
                jt = kb // 4
                bk, bkb = bank()
                st_bank[u] = (bk, bkb)
                qz_v = pb(qz_pg[h // 2], h % 2)
                qr_v = pb(qr_pg[h // 2], h % 2)
                S.mm([lambda e: e.matmul(bk[:, 0:n], lhsT=kn_c[:, h // 2, kb * 128:(kb + 1) * 128], rhs=qz_v[:, q0:512], start=True, stop=False),
                      lambda e: e.matmul(bk[:, 0:n], lhsT=kr_c[:, kb * 128:(kb + 1) * 128], rhs=qr_v[:, q0:512], start=False, stop=True)],
                     reads=[kn_b[jt], kr_b[jt], pbuf[qz_pg[h // 2]], pbuf[qr_pg[h // 2]]], writes=[bkb])

            def stage_exp(u):
                h, kb = units[u]
                q0 = 0 if kb < 4 * j else 128 * (kb - 4 * j)
                n = 512 - q0
                bk, bkb = st_bank.pop(u)
                ptv = pb(pt_pg[(u % 4) // 2], u % 2)
                ptb = pbuf[pt_pg[(u % 4) // 2]]
                st_pt[u] = (ptv, ptb)
                S.op("act", lambda e: e.activation(out=ptv[:, 0:n], in_=bk[:, 0:n], func=AF.Exp, scale=SCALE), reads=[bkb], writes=[ptb])
                if l == 0 and j == 0 and h < 2 and kb == 0:
                    tap(f"pt{h}", ptv, [ptb], [128, 512], BF16)
                if kb >= 4 * j:
                    S.op("pool", lambda e: e.tensor_tensor(out=ptv[:, 0:128], in0=ptv[:, 0:128], in1=cmask[:], op=ALU.mult), reads=[ptb, cmask_b], writes=[ptb])

            def stage_pv(u):
                h, kb = units[u]
                q0 = 0 if kb < 4 * j else 128 * (kb - 4 * j)
                n = 512 - q0
                jt = kb // 4
                ob = 6 + (h % 2)
                O, Ob = ps_t[ob], ps_b[ob]
                ptv, ptb = st_pt.pop(u)
                off = (kb * 8 + h) * 65
                S.mm([lambda e: e.matmul(O[:, q0:512], lhsT=v_cf[:, off:off + 128], rhs=ptv[:, 0:n], start=(kb == 0), stop=(kb == nkb - 1))],
                     reads=[v_b[jt], ptb], writes=[Ob])
                if kb == nkb - 1:
                    if l == 0 and j == 0 and h < 2:
                        dbp = palloc(1)
                        S.op("dve", lambda e: e.tensor_copy(out=pf(dbp[0])[:, 0:512], in_=O[:]), reads=[Ob], writes=[pbuf[dbp[0]]])
                        tap(f"O{h}", pf(dbp[0])[:, 0:512], [pbuf[dbp[0]]], [128, 512])
                        pfree(dbp)
                    rs = pf(nrm_pg[0])[64:65, (h % 2) * 512 // 2 * 0:512]
                    rs = pf(nrm_pg[h % 2])[64:65, 0:512]
                    S.op("dve", lambda e: e.reciprocal(out=rs, in_=O[64:65, :]), reads=[Ob], writes=[pbuf[nrm_pg[h % 2]]])
                    pend_norm.append((t_now[0] + 2, h, O, Ob, rs))

            def norm_b(h, O, Ob, rs):
                bk, bkb = bank()
                S.mm([lambda e: e.matmul(bk[0:64, :], lhsT=onesf[64:65, 0:64], rhs=rs, start=True, stop=True)], reads=[onesf_b, pbuf[nrm_pg[h % 2]]], writes=[bkb])
                bcs = pf(nrm_pg[2])[0:64, 0:512]
                S.op("act", lambda e: e.activation(out=bcs, in_=bk[0:64, :], func=AF.Copy), reads=[bkb], writes=[pbuf[nrm_pg[2]]])
                dst = pb(ya_pg[h // 2], h % 2)[0:64, :]
                S.op("dve", lambda e: e.tensor_tensor(out=dst, in0=O[0:64, :], in1=bcs, op=ALU.mult), reads=[Ob, pbuf[nrm_pg[2]]], writes=[pbuf[ya_pg[h // 2]]])

            pend_norm = []
            t_now = [0]
            NU = len(units)
            for t in range(NU + 3):
                t_now[0] = t
                if t < NU:
                    stage_qk(t)
                if 0 <= t - 1 < NU:
                    stage_exp(t - 1)
                while pend_norm and pend_norm[0][0] <= t:
                    _, h_, O_, Ob_, rs_ = pend_norm.pop(0)
                    norm_b(h_, O_, Ob_, rs_)
                if 0 <= t - 3 < NU:
                    stage_pv(t - 3)
            while pend_norm:
                _, h_, O_, Ob_, rs_ = pend_norm.pop(0)
                norm_b(h_, O_, Ob_, rs_)
            pfree(pt_pg); pfree(nrm_pg); pfree(qz_pg); pfree(qr_pg)
            if l == 0 and j == 0:
                for h in range(8):
                    tap(f"ya{h}", pb(ya_pg[h // 2], h % 2)[0:64, :], [pbuf[ya_pg[h // 2]]], [64, 512], BF16)

            S.phase = f"L{l}T{j}_p3_pool"

            yb_pg = palloc(2)
            up_pg = palloc(4)
            for ci in range(2):
                wP, bP = ws_get(f"P{ci}")
                for cb in range(2):
                    g = ci * 2 + cb
                    bk, bkb = proj_fm(wP, bP, 8, 256, cb * 128, 128, hrhs, hT_b)
                    uv = pf(up_pg[g])
                    S.op("act", lambda e, bk=bk, uv=uv: e.activation(out=uv[:, 16:528], in_=bk[:], func=AF.Copy), reads=[bkb], writes=[pbuf[up_pg[g]]])
                    S.op("pool", lambda e, uv=uv, g=g: e.tensor_copy(out=uv[:, 0:16], in_=hal_pool[:, g, :]), reads=[hal_pool_b], writes=[pbuf[up_pg[g]]])
                ws_done()
            wS, bS = ws_get("SMP")
            wp3 = km(wS, 4, 128)
            tmp_pg = palloc(2)
            d_pg = palloc(2)
            for g in range(4):
                uv = pf(up_pg[g]); ub = pbuf[up_pg[g]]
                a = pf(tmp_pg[0]); b_ = pf(tmp_pg[1]); ab = pbuf[tmp_pg[0]]; bb = pbuf[tmp_pg[1]]
                cur, curb = uv, ub
                sh = 1
                lo = 0
                for step in range(g + 1):
                    nxt, nxtb = (a, ab) if step % 2 == 0 else (b_, bb)
                    lo2 = lo + sh
                    S.op("pool", lambda e, cur=cur, nxt=nxt, lo2=lo2, sh=sh: e.tensor_tensor(out=nxt[:, lo2:528], in0=cur[:, lo2:528], in1=cur[:, lo2 - sh:528 - sh], op=ALU.add),
                         reads=[curb], writes=[nxtb])
                    cur, curb = nxt, nxtb
                    lo = lo2
                    sh *= 2
                w_ = float(2 ** (g + 1))
                dv = pb(d_pg[g // 2], g % 2); db_ = pbuf[d_pg[g // 2]]
                S.op("dve", lambda e, cur=cur, uv=uv, dv=dv, w_=w_: e.scalar_tensor_tensor(out=dv, in0=cur[:, 16:528], scalar=1.0 / w_, in1=uv[:, 16:528], op0=ALU.mult, op1=ALU.subtract),
                     reads=[curb, ub], writes=[db_])
                if j == 0:
                    S.op("dve", lambda e, cur=cur, g=g: e.tensor_tensor(out=sm2[:, 0:16], in0=cur[:, 16:32], in1=invc[:, g * 16:(g + 1) * 16], op=ALU.mult), reads=[curb, invc_b], writes=[sm2_b])
                    S.op("dve", lambda e, uv=uv, dv=dv: e.tensor_tensor(out=dv[:, 0:16], in0=sm2[:, 0:16], in1=uv[:, 16:32], op=ALU.subtract), reads=[sm2_b, ub], writes=[db_])
                S.op("pool", lambda e, uv=uv, g=g: e.tensor_copy(out=hal_pool[:, g, :], in_=uv[:, 512:528]), reads=[ub], writes=[hal_pool_b])
                bk, bkb = bank()
                S.mm([lambda e, bk=bk, g=g, dv=dv: e.matmul(bk[:], lhsT=wp3[:, g, :], rhs=dv, start=True, stop=True)], reads=[bS, db_], writes=[bkb])
                S.op("act", lambda e, bk=bk, g=g: e.activation(out=pb(yb_pg[g // 2], g % 2), in_=bk[:], func=AF.Copy, scale=vecs[:, 29 + g:30 + g]), reads=[bkb, vecs_b], writes=[pbuf[yb_pg[g // 2]]])
            ws_done()
            pfree(tmp_pg); pfree(d_pg); pfree(up_pg)
            if l == 0 and j == 0:
                for g in range(4):
                    tap(f"yb{g}", pb(yb_pg[g // 2], g % 2), [pbuf[yb_pg[g // 2]]], [128, 512], BF16)

            if _dbgstop == 'pool':
                break
            S.phase = f"L{l}T{j}_p4_lru"

            yd_pg = palloc(2)
            lx_pg = palloc(4)
            xpb = lambda pg: arena[:, pg * PAGE: pg * PAGE + 515]
            for ci in range(2):
                wX, bX = ws_get(f"LX{ci}")
                for cb in range(2):
                    c = ci * 2 + cb
                    bk, bkb = proj_fm(wX, bX, 8, 256, cb * 128, 128, hrhs, hT_b)
                    xv = xpb(lx_pg[c])
                    S.op("act", lambda e, bk=bk, xv=xv: e.activation(out=xv[:, 3:515], in_=bk[:], func=AF.Copy), reads=[bkb], writes=[pbuf[lx_pg[c]]])
                    S.op("pool", lambda e, xv=xv, c=c: e.tensor_copy(out=xv[:, 0:3], in_=hal_lx[:, c, :]), reads=[hal_lx_b], writes=[pbuf[lx_pg[c]]])
                ws_done()
            if _dbgstop == 'lx':
                break
            xc_pg = palloc(4); xcb_pg = palloc(2)
            wCV, bCV = ws_get("CVL")
            for c in range(4):
                xv = xpb(lx_pg[c]); xb_ = pbuf[lx_pg[c]]
                xc = pf(xc_pg[c])[:, 0:512]; xcb = pbuf[xc_pg[c]]
                bk, bkb = bank()
                S.mm([lambda e, k=k, c=c, xv=xv, bk=bk: e.matmul(bk[:], lhsT=wCV[:, (k * 4 + c) * 128:(k * 4 + c + 1) * 128], rhs=xv[:, k:k + 512], start=(k == 0), stop=(k == 3)) for k in range(4)],
                     reads=[bCV, xb_], writes=[bkb])
                S.op("act", lambda e, bk=bk, xc=xc, c=c: e.activation(out=xc, in_=bk[:], func=AF.Identity, bias=vecs[:, 33 + c:34 + c]), reads=[bkb, vecs_b], writes=[xcb])
                S.op("dve", lambda e, xc=xc, c=c: e.tensor_copy(out=pb(xcb_pg[c // 2], c % 2), in_=xc), reads=[xcb], writes=[pbuf[xcb_pg[c // 2]]])
                S.op("pool", lambda e, xv=xv, c=c: e.tensor_copy(out=hal_lx[:, c, :], in_=xv[:, 512:515]), reads=[xb_], writes=[hal_lx_b])
            ws_done()
            if _dbgstop == 'lruconv':
                break
            pfree(lx_pg)
            wL, bL = ws_get("SML")
            wl3 = km(wL, 8, 128)
            a_pg = palloc(4); u_pg = palloc(4); t_pg = palloc(4)
            AV = [pf(a_pg[c])[:, 0:512] for c in range(4)]; AB = [pbuf[a_pg[c]] for c in range(4)]
            UV = [pf(u_pg[c])[:, 0:512] for c in range(4)]; UB = [pbuf[u_pg[c]] for c in range(4)]
            TV = [pf(t_pg[c])[:, 0:512] for c in range(4)]; TB = [pbuf[t_pg[c]] for c in range(4)]
            XC = [pf(xc_pg[c])[:, 0:512] for c in range(4)]; XCB = [pbuf[xc_pg[c]] for c in range(4)]
            for c in range(4):
                xcbv = pb(xcb_pg[c // 2], c % 2)
                bk, bkb = bank()
                S.mm([lambda e, bk=bk, c=c, xcbv=xcbv: e.matmul(bk[:], lhsT=wl3[:, c, :], rhs=xcbv, start=True, stop=True)], reads=[bL, pbuf[xcb_pg[c // 2]]], writes=[bkb])
                S.op("act", lambda e, bk=bk, c=c: e.activation(out=AV[c], in_=bk[:], func=AF.Tanh, scale=0.5, bias=vhalf[:, c:c + 1]), reads=[bkb, vhalf_b], writes=[AB[c]])
                bk2, bkb2 = bank()
                S.mm([lambda e, bk2=bk2, c=c, xcbv=xcbv: e.matmul(bk2[:], lhsT=wl3[:, 4 + c, :], rhs=xcbv, start=True, stop=True)], reads=[bL, pbuf[xcb_pg[c // 2]]], writes=[bkb2])
                S.op("act", lambda e, bk2=bk2, c=c: e.activation(out=UV[c], in_=bk2[:], func=AF.Tanh, scale=0.5, bias=vhalf[:, 4 + c:5 + c]), reads=[bkb2, vhalf_b], writes=[UB[c]])
            ws_done()
            for c in range(4):
                S.op("act", lambda e, c=c: e.activation(out=AV[c], in_=AV[c], func=AF.Exp, scale=vhalf[:, 8 + c:9 + c], bias=vhalf[:, 8 + c:9 + c]), reads=[AB[c], vhalf_b], writes=[AB[c]])
                S.op("pool", lambda e, c=c: e.tensor_tensor(out=TV[c], in0=AV[c], in1=AV[c], op=ALU.mult), reads=[AB[c]], writes=[TB[c]])
                S.op("dve", lambda e, c=c: e.scalar_tensor_tensor(out=UV[c], in0=UV[c], scalar=1.0, in1=XC[c], op0=ALU.add, op1=ALU.mult), reads=[UB[c], XCB[c]], writes=[UB[c]])
            for c in range(4):
                S.op("act", lambda e, c=c: e.activation(out=TV[c], in_=TV[c], func=AF.Sqrt, scale=-0.25, bias=0.25), reads=[TB[c]], writes=[TB[c]])
            for c in range(4):
                S.op("pool", lambda e, c=c: e.tensor_tensor(out=UV[c], in0=UV[c], in1=TV[c], op=ALU.mult), reads=[UB[c], TB[c]], writes=[UB[c]])
                S.op("dve", lambda e, c=c: e.tensor_tensor_scan(out=XC[c], data0=AV[c], data1=UV[c], initial=hcar[:, c:c + 1], op0=ALU.mult, op1=ALU.add),
                     reads=[AB[c], UB[c], hcar_b, XCB[c]], writes=[XCB[c]])
                S.op("pool", lambda e, c=c: e.tensor_copy(out=hcar[:, c:c + 1], in_=XC[c][:, 511:512]), reads=[XCB[c]], writes=[hcar_b])
            pfree(a_pg); pfree(u_pg); pfree(t_pg); pfree(xcb_pg)
            g_pg = palloc(4)
            for ci in range(2):
                wG, bG = ws_get(f"LG{ci}")
                for cb in range(2):
                    c = ci * 2 + cb
                    bk, bkb = proj_fm(wG, bG, 8, 256, cb * 128, 128, hrhs, hT_b)
                    gv = pf(g_pg[(c % 2) * 2])[:, 0:512]; gb = pbuf[g_pg[(c % 2) * 2]]
                    tv = pf(g_pg[(c % 2) * 2 + 1])[:, 0:512]; tb_ = pbuf[g_pg[(c % 2) * 2 + 1]]
                    xc = XC[c]; xcb = XCB[c]
                    S.op("act", lambda e, bk=bk, gv=gv: e.activation(out=gv, in_=bk[:], func=AF.Copy), reads=[bkb], writes=[gb])
                    S.op("pool", lambda e, gv=gv, tv=tv: e.tensor_tensor(out=tv, in0=gv, in1=gv, op=ALU.mult), reads=[gb], writes=[tb_])
                    S.op("dve", lambda e, tv=tv: e.tensor_scalar(out=tv, in0=tv, scalar1=0.044715, scalar2=1.0, op0=ALU.mult, op1=ALU.add), reads=[tb_], writes=[tb_])
                    S.op("pool", lambda e, gv=gv, tv=tv: e.tensor_tensor(out=tv, in0=tv, in1=gv, op=ALU.mult), reads=[gb, tb_], writes=[tb_])
                    S.op("act", lambda e, tv=tv: e.activation(out=tv, in_=tv, func=AF.Tanh, scale=0.7978845608028654), reads=[tb_], writes=[tb_])
                    S.op("dve", lambda e, gv=gv, tv=tv: e.scalar_tensor_tensor(out=tv, in0=tv, scalar=1.0, in1=gv, op0=ALU.add, op1=ALU.mult), reads=[gb, tb_], writes=[tb_])
                    S.op("dve", lambda e, tv=tv, xc=xc, c=c: e.scalar_tensor_tensor(out=pb(yd_pg[c // 2], c % 2), in0=tv, scalar=0.5, in1=xc, op0=ALU.mult, op1=ALU.mult), reads=[tb_, xcb], writes=[pbuf[yd_pg[c // 2]]])
                ws_done()
            pfree(g_pg); pfree(xc_pg)
            if l == 0 and j == 0:
                for c in range(4):
                    tap(f"yd{c}", pb(yd_pg[c // 2], c % 2), [pbuf[yd_pg[c // 2]]], [128, 512], BF16)

            if _dbgstop == 'lru':
                break
            S.phase = f"L{l}T{j}_p5_ssd"

            yc_pg = palloc(2)
            xb_pg = palloc(6)
            for ci in range(3):
                wX, bX = ws_get(f"XB{ci}")
                for cb in range(2):
                    c = ci * 2 + cb
                    bk, bkb = proj_fm(wX, bX, 8, 256, cb * 128, 128, hrhs, hT_b)
                    xv = xpb(xb_pg[c])
                    S.op("act", lambda e, bk=bk, xv=xv: e.activation(out=xv[:, 3:515], in_=bk[:], func=AF.Copy), reads=[bkb], writes=[pbuf[xb_pg[c]]])
                    S.op("pool", lambda e, xv=xv, c=c: e.tensor_copy(out=xv[:, 0:3], in_=hal_xb[:, c, :]), reads=[hal_xb_b], writes=[pbuf[xb_pg[c]]])
                ws_done()
            xa_pg = palloc(3)
            cv_pg = palloc(4)
            wC0, bC0 = ws_get("CVS0"); wC1, bC1 = ws_get("CVS1")
            for c in range(6):
                xv = xpb(xb_pg[c]); xb_ = pbuf[xb_pg[c]]
                vb = pf(cv_pg[c % 2])[:, 0:512]; vbb = pbuf[cv_pg[c % 2]]
                tt = pf(cv_pg[2 + c % 2])[:, 0:512]; ttb = pbuf[cv_pg[2 + c % 2]]
                bk, bkb = bank()

                def tapw(k, c=c):
                    m = k * 6 + c
                    w_ = wC0 if m < 16 else wC1
                    return w_[:, (m % 16) * 128:(m % 16 + 1) * 128]
                S.mm([lambda e, k=k, xv=xv, bk=bk: e.matmul(bk[:], lhsT=tapw(k), rhs=xv[:, k:k + 512], start=(k == 0), stop=(k == 3)) for k in range(4)],
                     reads=[bC0, bC1, xb_], writes=[bkb])
                S.op("pool", lambda e, xv=xv, c=c: e.tensor_copy(out=hal_xb[:, c, :], in_=xv[:, 512:515]), reads=[xb_], writes=[hal_xb_b])
                S.op("act", lambda e, bk=bk, tt=tt, c=c: e.activation(out=tt, in_=bk[:], func=AF.Tanh, scale=0.5, bias=vhalf[:, 12 + c:13 + c]), reads=[bkb, vhalf_b], writes=[ttb])
                S.op("act", lambda e, bk=bk, vb=vb, c=c: e.activation(out=vb, in_=bk[:], func=AF.Identity, scale=0.5, bias=vhalf[:, 12 + c:13 + c]), reads=[bkb, vhalf_b], writes=[vbb])
                S.op("dve", lambda e, vb=vb, tt=tt, c=c: e.scalar_tensor_tensor(out=pb(xa_pg[c // 2], c % 2), in0=tt, scalar=1.0, in1=vb, op0=ALU.add, op1=ALU.mult),
                     reads=[ttb, vbb], writes=[pbuf[xa_pg[c // 2]]])
            ws_done()
            if _dbgstop == 'ssdconv':
                break
            pfree(xb_pg); pfree(cv_pg)
            if l == 0 and j == 0:
                for c in range(6):
                    tap(f"xa{c}", pb(xa_pg[c // 2], c % 2), [pbuf[xa_pg[c // 2]]], [128, 512], BF16)
            wZ0, bZ0 = ws_get("Z0"); wZ1, bZ1 = ws_get("Z1"); wDT, bDT = ws_get("DT")
            z3 = [km(wZ0, 8, 256), km(wZ1, 8, 256)]
            dt3 = km(wDT, 8, 8)
            xa = [pb(xa_pg[c // 2], c % 2) for c in range(6)]
            xab = [pbuf[xa_pg[c // 2]] for c in range(6)]
            fs_pg = [palloc(4) for _ in range(2)]
            zs_pg = palloc(4)
            for c4 in range(4):
                ts_ = slice(c4 * 128, (c4 + 1) * 128)
                bk3, bkb3 = bank()
                for hf in range(2):
                    S.mm([lambda e, k=k, hf=hf: e.matmul(bk3[:, hf * 256:(hf + 1) * 256], lhsT=hT[:, k, ts_], rhs=z3[hf][:, k, :], start=(k == 0), stop=(k == 7)) for k in range(8)],
                         reads=hT_b + [bZ0, bZ1], writes=[bkb3])
                zs_ = pf(zs_pg[c4])[:, 0:512]
                S.op("act", lambda e: e.activation(out=zs_, in_=bk3[:], func=AF.Tanh, scale=0.5), reads=[bkb3], writes=[pbuf[zs_pg[c4]]])
                S.op("dve", lambda e: e.scalar_tensor_tensor(out=zs_, in0=zs_, scalar=1.0, in1=bk3[:], op0=ALU.add, op1=ALU.mult), reads=[bkb3, pbuf[zs_pg[c4]]], writes=[pbuf[zs_pg[c4]]])
            bk4, bkb4 = bank()
            for c4 in range(4):
                ts_ = slice(c4 * 128, (c4 + 1) * 128)
                S.mm([lambda e, k=k: e.matmul(bk4[:, c4 * 8:(c4 + 1) * 8], lhsT=hT[:, k, ts_], rhs=dt3[:, k, :], start=(k == 0), stop=(k == 7)) for k in range(8)],
                     reads=hT_b + [bDT], writes=[bkb4])
            v48 = lambda ap: ap.rearrange("p (c h) -> p c h", c=4)
            S.op("dve", lambda e: e.tensor_tensor(out=v48(sm[:, 64:96]), in0=v48(bk4[:, 0:32]), in1=rows[:, 0:8].unsqueeze(1).to_broadcast([128, 4, 8]), op=ALU.add), reads=[bkb4, rows_b], writes=[sm_b])
            S.op("act", lambda e: e.activation(out=sm[:, 64:96], in_=sm[:, 64:96], func=AF.Exp), reads=[sm_b], writes=[sm_b])
            S.op("act", lambda e: e.activation(out=sm[:, 0:32], in_=sm[:, 64:96], func=AF.Ln, bias=1.0), reads=[sm_b], writes=[sm_b])
            S.op("dve", lambda e: e.tensor_tensor(out=v48(sm[:, 32:64]), in0=v48(sm[:, 0:32]), in1=Abc[:].unsqueeze(1).to_broadcast([128, 4, 8]), op=ALU.mult), reads=[sm_b, Abc_b], writes=[sm_b])
            bk5, bkb5 = bank()
            mmsmall = []
            for c4 in range(4):
                a_ = sm[:, 32 + c4 * 8:40 + c4 * 8]
                mmsmall += [lambda e, c4=c4, a_=a_: e.matmul(bk5[:, c4 * 24:c4 * 24 + 8], lhsT=U_le[:], rhs=a_, start=True, stop=True),
                            lambda e, c4=c4, a_=a_: e.matmul(bk5[:, c4 * 24 + 8:c4 * 24 + 16], lhsT=SU_gt[:], rhs=a_, start=True, stop=True),
                            lambda e, c4=c4, a_=a_: e.matmul(bk5[:, c4 * 24 + 16:c4 * 24 + 24], lhsT=onesf[:], rhs=a_, start=True, stop=True)]
            S.mm(mmsmall, reads=[U_b, SU_b, onesf_b, sm_b], writes=[bkb5])
            S.op("act", lambda e: e.activation(out=sm[:, 96:192], in_=bk5[:, 0:96], func=AF.Exp), reads=[bkb5], writes=[sm_b])
            w_pg = palloc(4)
            Mdt_pg, xb_pg2, ycomb_pg, junk_pg = w_pg
            Yd, Yd_bf = ps_t[5], ps_b[5]
            Yo, Yo_bf = ps_t[6], ps_b[6]
            St, St_bf = ps_t[7], ps_b[7]
            fstate = {}

            import os as _os
            _flim = int(_os.environ.get('DBG_FLIM', '0'))

            def ssd_front(c4):
                st = c4 % 2
                fp = fs_pg[st]
                ts_ = slice(c4 * 128, (c4 + 1) * 128)
                smb = sm_b
                a_v = sm[:, 32 + c4 * 8:40 + c4 * 8]
                xtok = pb(fp[0], 0); btok = pb(fp[0], 1); p0b = pbuf[fp[0]]
                zs = pf(zs_pg[c4])[:, 0:512]; zs_b = pbuf[zs_pg[c4]]
                cbm = pf(fp[1])[:, 0:256]; cbm_b = pbuf[fp[1]]
                E = [pf(fp[2])[:, 0:512], pf(fp[3])[:, 0:512]]; E_b = [pbuf[fp[2]], pbuf[fp[3]]]
                bk, bkb = bank()
                S.mm([lambda e, c=c: e.matmul(bk[:, c * 128:(c + 1) * 128], lhsT=xa[c][:, ts_], rhs=identb[:], start=True, stop=True) for c in range(4)],
                     reads=xab[0:4] + [identb_b], writes=[bkb])
                S.op("act", lambda e: e.activation(out=xtok, in_=bk[:], func=AF.Copy), reads=[bkb], writes=[p0b])
                bk2, bkb2 = bank()
                S.mm([lambda e: e.matmul(bk2[:, 0:128], lhsT=xa[4][:, ts_], rhs=identb[:], start=True, stop=True)], reads=[xab[4], identb_b], writes=[bkb2])
                S.op("dve", lambda e: e.tensor_copy(out=btok[:, 0:128], in_=bk2[:, 0:128]), reads=[bkb2], writes=[p0b])
                if _flim == 1:
                    return
                if _flim == 2:
                    return
                for g in range(2):
                    bk6, bkb6 = bank()
                    S.mm([lambda e, g=g, bk6=bk6: e.matmul(bk6[:, 0:128], lhsT=xa[4][64 * g:64 * g + 64, ts_], rhs=xa[5][64 * g:64 * g + 64, ts_], start=True, stop=True)],
                         reads=[xab[4], xab[5]], writes=[bkb6])
                    S.op("dve", lambda e, g=g, bk6=bk6: e.tensor_tensor(out=cbm[:, g * 128:(g + 1) * 128], in0=bk6[:, 0:128], in1=U_le[:], op=ALU.mult), reads=[bkb6, U_b], writes=[cbm_b])
                if _flim == 3:
                    return
                for half in range(2):
                    lh = E[half].rearrange("p (h n) -> p h n", h=4)
                    S.op("pool", lambda e, lh=lh, half=half: e.tensor_tensor(out=lh, in0=SU_gt[:].unsqueeze(1).to_broadcast([128, 4, 128]),
                                                                            in1=a_v[:, 4 * half:4 + 4 * half].unsqueeze(2).to_broadcast([128, 4, 128]), op=ALU.mult),
                         reads=[SU_b, smb], writes=[E_b[half]])
                if _flim == 4:
                    return
                for half in range(2):
                    bk7, bkb7 = bank()
                    S.mm([lambda e, r=r, half=half, bk7=bk7: e.matmul(bk7[:, r * 128:(r + 1) * 128], lhsT=E[half][:, r * 128:(r + 1) * 128], rhs=U_le[:], start=True, stop=True) for r in range(4)],
                         reads=[E_b[half], U_b], writes=[bkb7])
                    S.op("act", lambda e, half=half, bk7=bk7: e.activation(out=E[half], in_=bk7[:], func=AF.Exp), reads=[bkb7], writes=[E_b[half]])
                fstate[c4] = (xtok, btok, p0b, zs, zs_b, cbm, cbm_b, E, E_b, None, smb)

            def ssd_back(c4):
                xtok, btok, p0b, zs, zs_b, cbm, cbm_b, E, E_b, smc, smb = fstate.pop(c4)
                dt_v = sm[:, c4 * 8:(c4 + 1) * 8]
                eacs_v = sm[:, 96 + c4 * 24:104 + c4 * 24]
                edec_v = sm[:, 104 + c4 * 24:112 + c4 * 24]
                etot_v = sm[:, 112 + c4 * 24:120 + c4 * 24]
                ts_ = slice(c4 * 128, (c4 + 1) * 128)
                Mdt = [pb(Mdt_pg, 0), pb(Mdt_pg, 1)]; Mdt_b = pbuf[Mdt_pg]
                xdt = pb(xb_pg2, 0); Bw = pb(xb_pg2, 1); xb_b = pbuf[xb_pg2]
                ycomb = pf(ycomb_pg)[:, 0:512]; yc_b = pbuf[ycomb_pg]
                junk = pf(junk_pg)[:, 0:512]; junk_b = pbuf[junk_pg]
                v3 = lambda ap: ap.rearrange("p (h n) -> p h n", h=8)
                S.op("dve", lambda e: e.tensor_tensor(out=v3(xdt), in0=v3(xtok), in1=dt_v.unsqueeze(2).to_broadcast([128, 8, 64]), op=ALU.mult),
                     reads=[p0b, smb], writes=[xb_b])
                for g in range(2):
                    S.op("pool", lambda e, g=g: e.tensor_tensor(out=Bw[:, g * 256:(g + 1) * 256].rearrange("p (r n) -> p r n", r=4),
                                                                in0=btok[:, g * 64:(g + 1) * 64].unsqueeze(1).to_broadcast([128, 4, 64]),
                                                                in1=edec_v[:, 4 * g:4 + 4 * g].unsqueeze(2).to_broadcast([128, 4, 64]), op=ALU.mult),
                         reads=[p0b, smb], writes=[xb_b])
                for g in range(2):
                    S.op("dve", lambda e, g=g: e.tensor_tensor(out=Mdt[g].rearrange("p (r n) -> p r n", r=4), in0=E[g].rearrange("p (r n) -> p r n", r=4),
                                                               in1=cbm[:, g * 128:(g + 1) * 128].unsqueeze(1).to_broadcast([128, 4, 128]), op=ALU.mult),
                         reads=[E_b[g], cbm_b], writes=[Mdt_b])
                def yo_mm(hs_):
                    S.mm([lambda e, h=h: e.matmul(Yo[:, h * 64:(h + 1) * 64], lhsT=xa[5][64 * (h // 4):64 * (h // 4) + 64, ts_], rhs=Sbf[64 * (h // 4):64 * (h // 4) + 64, h % 4, :], start=True, stop=True) for h in hs_],
                         reads=[xab[5], Sbf_b], writes=[Yo_bf])
                yo_mm(range(0, 4))
                S.mm([lambda e, h=h: e.matmul(Yd[:, h * 64:(h + 1) * 64], lhsT=Mdt[h // 4][:, (h % 4) * 128:(h % 4 + 1) * 128], rhs=xdt[:, h * 64:(h + 1) * 64], start=True, stop=True) for h in range(8)],
                     reads=[Mdt_b, xb_b], writes=[Yd_bf])
                yo_mm(range(4, 8))
                S.mm([lambda e, h=h: e.matmul(St[64 * (h // 4):64 * (h // 4) + 64, (h % 4) * 64:(h % 4 + 1) * 64], lhsT=Bw[:, h * 64:(h + 1) * 64], rhs=xdt[:, h * 64:(h + 1) * 64], start=True, stop=True) for h in range(8)],
                     reads=[xb_b], writes=[St_bf])
                for g in range(2):
                    ps_ = slice(64 * g, 64 * g + 64)
                    S.op("dve", lambda e, g=g, ps_=ps_: e.tensor_tensor(out=Sst[ps_, :, :], in0=Sst[ps_, :, :], in1=etot_v[ps_, 4 * g:4 + 4 * g].unsqueeze(2).to_broadcast([64, 4, 64]), op=ALU.mult),
                         reads=[Sst_b, smb], writes=[Sst_b])
                S.op("dve", lambda e: e.tensor_tensor(out=Sst[:], in0=St[:, 0:256].rearrange("p (r n) -> p r n", r=4), in1=Sst[:], op=ALU.add), reads=[Sst_b, St_bf], writes=[Sst_b])
                S.op("act", lambda e: e.activation(out=Sbf[:], in_=Sst[:], func=AF.Copy), reads=[Sst_b], writes=[Sbf_b])
                S.op("dve", lambda e: e.tensor_tensor(out=v3(ycomb), in0=v3(Yo[:]), in1=eacs_v.unsqueeze(2).to_broadcast([128, 8, 64]), op=ALU.mult),
                     reads=[Yo_bf, smb], writes=[yc_b])
                S.op("dve", lambda e: e.tensor_tensor(out=ycomb, in0=Yd[:], in1=ycomb, op=ALU.add), reads=[Yd_bf, yc_b], writes=[yc_b])
                S.op("pool", lambda e: e.tensor_tensor(out=v3(junk), in0=v3(xtok), in1=rows[:, 16:24].unsqueeze(2).to_broadcast([128, 8, 64]), op=ALU.mult),
                     reads=[p0b, rows_b], writes=[junk_b])
                S.op("pool", lambda e: e.tensor_tensor(out=ycomb, in0=ycomb, in1=junk, op=ALU.add), reads=[yc_b, junk_b], writes=[yc_b])
                S.op("pool", lambda e: e.tensor_tensor(out=ycomb, in0=ycomb, in1=zs, op=ALU.mult), reads=[yc_b, zs_b], writes=[yc_b])
                S.op("act", lambda e: e.activation(out=junk, in_=ycomb, func=AF.Square, accum_out=sm2[:, 16:17]), reads=[yc_b], writes=[junk_b, sm2_b])
                S.op("act", lambda e: e.activation(out=sm2[:, 17:18], in_=sm2[:, 16:17], func=AF.Ln, scale=1.0 / 512, bias=4 * EPS), reads=[sm2_b], writes=[sm2_b])
                S.op("act", lambda e: e.activation(out=sm2[:, 17:18], in_=sm2[:, 17:18], func=AF.Exp, scale=-0.5), reads=[sm2_b], writes=[sm2_b])
                S.op("dve", lambda e: e.scalar_tensor_tensor(out=junk, in0=ycomb, scalar=sm2[:, 17:18], in1=rows[:, 24:536], op0=ALU.mult, op1=ALU.mult),
                     reads=[yc_b, sm2_b, rows_b], writes=[junk_b])
                bk, bkb = bank()
                S.mm([lambda e, c=c: e.transpose(out=bk[:, c * 128:(c + 1) * 128], in_=junk[:, c * 128:(c + 1) * 128], identity=identf[:]) for c in range(4)],
                     reads=[junk_b, identf_b], writes=[bkb])
                S.op("act", lambda e: e.activation(out=pb(yc_pg[0], 0)[:, ts_], in_=bk[:, 0:128], func=AF.Copy), reads=[bkb], writes=[pbuf[yc_pg[0]]])
                S.op("dve", lambda e: e.tensor_copy(out=pb(yc_pg[0], 1)[:, ts_], in_=bk[:, 128:256]), reads=[bkb], writes=[pbuf[yc_pg[0]]])
                S.op("act", lambda e: e.activation(out=pb(yc_pg[1], 0)[:, ts_], in_=bk[:, 256:384], func=AF.Copy), reads=[bkb], writes=[pbuf[yc_pg[1]]])
                S.op("dve", lambda e: e.tensor_copy(out=pb(yc_pg[1], 1)[:, ts_], in_=bk[:, 384:512]), reads=[bkb], writes=[pbuf[yc_pg[1]]])

            import os as _os
            _dbg = _os.environ.get("DBG_SSD", "")
            ssd_front(0)
            if _dbg == "front":
                break
            if _dbg == "back":
                ssd_back(0)
                break
            for c4 in range(4):
                if c4 + 1 < 4:
                    ssd_front(c4 + 1)
                ssd_back(c4)
            pfree(fs_pg[0]); pfree(fs_pg[1]); pfree(zs_pg)
            ws_done()
            pfree(w_pg); pfree(xa_pg)
            if l == 0 and j == 0:
                for c in range(4):
                    tap(f"yc{c}", pb(yc_pg[c // 2], c % 2), [pbuf[yc_pg[c // 2]]], [128, 512], BF16)

            rotn[0] = 8
            S.phase = f"L{l}T{j}_p6_merge"

            ybr = {1: yb_pg, 2: yc_pg, 3: yd_pg}
            sgp = {n: palloc(2) for n in range(4)}
            acc_pg = palloc(2); tmp_pg = palloc(2)

            def do_gate(n, dg):
                wGn, bGn = ws_get(f"G{n}_{dg}")
                for cb in range(2):
                    bk, bkb = proj_fm(wGn, bGn, 8, 256, cb * 128, 128, hrhs, hT_b)
                    S.op("act", lambda e, bk=bk, o=pf(sgp[n][cb])[:, 0:512]: e.activation(out=o, in_=bk[:], func=AF.Tanh, scale=0.5), reads=[bkb], writes=[pbuf[sgp[n][cb]]])
                ws_done()

            def do_branch(nn, dg, wB, bB):
                for cb in range(2):
                    dc = dg * 2 + cb
                    bk, bkb = bank()
                    if nn == 0:
                        w3 = wB[0:64, :].rearrange("p (h c) -> p h c", h=8)
                        S.mm([lambda e, bk=bk, h=h, cb=cb, w3=w3: e.matmul(bk[:], lhsT=w3[:, h, cb * 128:(cb + 1) * 128], rhs=pb(ya_pg[h // 2], h % 2)[0:64, :], start=(h == 0), stop=(h == 7)) for h in range(8)],
                             reads=[bB] + [pbuf[i] for i in ya_pg], writes=[bkb])
                    else:
                        if nn in (1, 2):
                            w3 = wB[:, (nn - 1) * 1024:nn * 1024].rearrange("p (k c) -> p k c", k=4)
                        else:
                            w3 = km(wB, 4, 256)
                        ypg = ybr[nn]
                        S.mm([lambda e, bk=bk, k=k, cb=cb, w3=w3, ypg=ypg: e.matmul(bk[:], lhsT=w3[:, k, cb * 128:(cb + 1) * 128], rhs=pb(ypg[k // 2], k % 2), start=(k == 0), stop=(k == 3)) for k in range(4)],
                             reads=[bB] + [pbuf[i] for i in ypg], writes=[bkb])
                    sgv = pf(sgp[nn][cb])[:, 0:512]; sgb = pbuf[sgp[nn][cb]]
                    acc = pf(acc_pg[cb])[:, 0:512]; accb = pbuf[acc_pg[cb]]
                    if nn == 0:
                        S.op("dve", lambda e, bk=bk, acc=acc, sgv=sgv: e.scalar_tensor_tensor(out=acc, in0=sgv, scalar=1.0, in1=bk[:], op0=ALU.add, op1=ALU.mult), reads=[bkb, sgb], writes=[accb])
                    else:
                        tv = pf(tmp_pg[cb])[:, 0:512]; tvb = pbuf[tmp_pg[cb]]
                        S.op("dve", lambda e, bk=bk, tv=tv, sgv=sgv: e.scalar_tensor_tensor(out=tv, in0=sgv, scalar=1.0, in1=bk[:], op0=ALU.add, op1=ALU.mult), reads=[bkb, sgb], writes=[tvb])
                        last = (nn == 3)
                        outv = mT[:, dc, :] if last else acc
                        S.op("pool", lambda e, tv=tv, acc=acc, outv=outv: e.tensor_tensor(out=outv, in0=acc, in1=tv, op=ALU.add), reads=[tvb, accb],
                             writes=[mT_b[dc]] if last else [accb])

            for dg in range(4):
                do_gate(0, dg)
                wB, bB = ws_get(f"B0_{dg}")
                do_branch(0, dg, wB, bB)
                ws_done()
                do_gate(1, dg)
                do_gate(2, dg)
                wB, bB = ws_get(f"B12_{dg}")
                do_branch(1, dg, wB, bB)
                do_branch(2, dg, wB, bB)
                ws_done()
                do_gate(3, dg)
                wB, bB = ws_get(f"B3_{dg}")
                do_branch(3, dg, wB, bB)
                ws_done()
            for n in range(4):
                pfree(sgp[n])
            pfree(acc_pg); pfree(tmp_pg)
            pfree(ya_pg); pfree(yb_pg); pfree(yc_pg); pfree(yd_pg)
            if l == 0 and j == 0:
                tap("mT0", mT[:].rearrange("p c t -> p (c t)"), mT_b, [128, 8 * T], BF16)

            S.phase = f"L{l}T{j}_p7_wout"

            for i in range(4):
                wO, bO = ws_get(f"WO{i}")
                for cb in range(2):
                    dc = i * 2 + cb
                    bk, bkb = proj_fm(wO, bO, 8, 256, cb * 128, 128, lambda k: mT[:, k, :], mT_b)
                    S.op("dve", lambda e, bk=bk, dc=dc: e.scalar_tensor_tensor(out=xT[:, dc, :], in0=bk[:], scalar=0.5, in1=xT[:, dc, :], op0=ALU.mult, op1=ALU.add), reads=[bkb, xT_b[dc]], writes=[xT_b[dc]])
                ws_done()
            if l == 0 and j == 0:
                tap("xT1", xT[:].rearrange("p c t -> p (c t)"), xT_b, [128, 8 * T])

            S.phase = f"L{l}T{j}_p8_ffn"

            norm_x_to_h(8)
            f_pg = palloc(16)
            r_pg = palloc(2)
            for i in range(16):
                wF, bF = ws_get(f"F1_{i}")
                for cb in range(2):
                    fc = i * 2 + cb
                    bk, bkb = proj_fm(wF, bF, 8, 256, cb * 128, 128, hrhs, hT_b)
                    rv_ = pf(r_pg[fc % 2])[:, 0:512]; rb_ = pbuf[r_pg[fc % 2]]
                    S.op("act", lambda e, bk=bk, rv_=rv_: e.activation(out=rv_, in_=bk[:], func=AF.Relu), reads=[bkb], writes=[rb_])
                    eng = "pool" if fc % 2 == 0 else "dve"
                    S.op(eng, lambda e, rv_=rv_, fc=fc: e.tensor_tensor(out=pb(f_pg[fc // 2], fc % 2), in0=rv_, in1=rv_, op=ALU.mult), reads=[rb_], writes=[pbuf[f_pg[fc // 2]]])
                ws_done()
            pfree(r_pg)
            if l == 0 and j == 0:
                tap("h2T", hT[:].rearrange("p c t -> p (c t)"), hT_b, [128, 8 * T], BF16)
                tap("fT0", pb(f_pg[0], 0), [pbuf[f_pg[0]]], [128, 512], BF16)
                tap("fT31", pb(f_pg[15], 1), [pbuf[f_pg[15]]], [128, 512], BF16)
            for dc in range(8):
                wa, ba = ws_get(f"F2_{dc * 2}")
                wb2, bb2 = ws_get(f"F2_{dc * 2 + 1}")
                wa3 = km(wa, 16, 128); wb3 = km(wb2, 16, 128)
                bk, bkb = bank()
                S.mm([lambda e, bk=bk, k=k: e.matmul(bk[:], lhsT=(wa3 if k < 16 else wb3)[:, k % 16, :], rhs=pb(f_pg[k // 2], k % 2), start=(k == 0), stop=(k == 31)) for k in range(32)],
                     reads=[ba, bb2] + [pbuf[i] for i in f_pg], writes=[bkb])
                S.op("dve", lambda e, bk=bk, dc=dc: e.tensor_tensor(out=xT[:, dc, :], in0=xT[:, dc, :], in1=bk[:], op=ALU.add), reads=[bkb, xT_b[dc]], writes=[xT_b[dc]])
                ws_done()
            pfree(f_pg)
            if l == 0 and j == 0:
                tap("xT2", xT[:].rearrange("p c t -> p (c t)"), xT_b, [128, 8 * T])

            S.phase = f"L{l}T{j}_p9_ple"

            norm_x_to_h(16)
            pl_pg = palloc(4)
            pT_pg = palloc(1)
            for q in range(4):
                pv = pf(pl_pg[q])[:, 0:256]
                S.dma("sp", f"pl{q}", pv, p_d[l, t0 + q * 128:t0 + (q + 1) * 128, :], writes=[pbuf[pl_pg[q]]])
            bk, bkb = bank()
            bk2, bkb2 = bank()
            for q in range(4):
                pv = pf(pl_pg[q])[:, 0:256]
                S.mm([lambda e, bk=bk, q=q, pv=pv: e.transpose(out=bk[:, q * 128:(q + 1) * 128], in_=pv[:, 0:128], identity=identf[:])], reads=[pbuf[pl_pg[q]], identf_b], writes=[bkb])
                S.mm([lambda e, bk2=bk2, q=q, pv=pv: e.transpose(out=bk2[:, q * 128:(q + 1) * 128], in_=pv[:, 128:256], identity=identf[:])], reads=[pbuf[pl_pg[q]], identf_b], writes=[bkb2])
            S.op("act", lambda e, bk=bk: e.activation(out=pb(pT_pg[0], 0), in_=bk[:], func=AF.Copy), reads=[bkb], writes=[pbuf[pT_pg[0]]])
            S.op("dve", lambda e, bk2=bk2: e.tensor_copy(out=pb(pT_pg[0], 1), in_=bk2[:]), reads=[bkb2], writes=[pbuf[pT_pg[0]]])
            pfree(pl_pg)
            sgl_pg = palloc(8)
            for i in range(4):
                wG, bG = ws_get(f"PG{i}")
                for cb in range(2):
                    dc = i * 2 + cb
                    bk, bkb = proj_fm(wG, bG, 8, 256, cb * 128, 128, hrhs, hT_b)
                    S.op("act", lambda e, bk=bk, dc=dc: e.activation(out=pf(sgl_pg[dc])[:, 0:512], in_=bk[:], func=AF.Tanh, scale=0.5), reads=[bkb], writes=[pbuf[sgl_pg[dc]]])
                ws_done()
            wPL, bPL = ws_get("PL")
            wpl3 = km(wPL, 2, 1024)
            for dc in range(8):
                bk, bkb = bank()
                S.mm([lambda e, bk=bk, k=k, dc=dc: e.matmul(bk[:], lhsT=wpl3[:, k, dc * 128:(dc + 1) * 128], rhs=pb(pT_pg[0], k), start=(k == 0), stop=(k == 1)) for k in range(2)],
                     reads=[bPL, pbuf[pT_pg[0]]], writes=[bkb])
                sv = pf(sgl_pg[dc])[:, 0:512]
                S.op("dve", lambda e, bk=bk, sv=sv: e.scalar_tensor_tensor(out=sv, in0=sv, scalar=1.0, in1=bk[:], op0=ALU.add, op1=ALU.mult), reads=[bkb, pbuf[sgl_pg[dc]]], writes=[pbuf[sgl_pg[dc]]])
                S.op("dve", lambda e, sv=sv, dc=dc: e.scalar_tensor_tensor(out=xT[:, dc, :], in0=sv, scalar=0.5, in1=xT[:, dc, :], op0=ALU.mult, op1=ALU.add), reads=[pbuf[sgl_pg[dc]], xT_b[dc]], writes=[xT_b[dc]])
            ws_done()
            pfree(sgl_pg); pfree(pT_pg)
            if l == 0 and j == 0:
                tap("xT3", xT[:].rearrange("p c t -> p (c t)"), xT_b, [128, 8 * T])

            S.phase = f"L{l}T{j}_p10_out"

            if l < DEPTH - 1:
                S.dma("sp", "st", xs_d[j], xT[:].rearrange("p c t -> p (c t)"), reads=xT_b, writes=[xs_b[j]])
            else:
                o_pg = [palloc_c(2) for _ in range(2)]
                for q in range(4):
                    pg4 = o_pg[q % 2]
                    ov = arena[:, pg4[0] * PAGE: pg4[0] * PAGE + 2048].bitcast(F32)
                    obufs = [pbuf[i] for i in pg4]
                    bks = [bank(), bank()]
                    for g2 in range(2):
                        bk, bkb = bks[g2]
                        S.mm([lambda e, bk=bk, c=c, q=q: e.transpose(out=bk[:, (c % 4) * 128:(c % 4 + 1) * 128], in_=xT[:, c, q * 128:(q + 1) * 128], identity=identf[:]) for c in range(g2 * 4, g2 * 4 + 4)],
                             reads=xT_b + [identf_b], writes=[bkb])
                    S.op("act", lambda e, ov=ov, bk=bks[0][0]: e.activation(out=ov[:, 0:512], in_=bk[:], func=AF.Square, accum_out=sm2[:, 32:33]), reads=[bks[0][1]], writes=obufs + [sm2_b])
                    S.op("act", lambda e, ov=ov, bk=bks[1][0]: e.activation(out=ov[:, 512:1024], in_=bk[:], func=AF.Square, accum_out=sm2[:, 33:34]), reads=[bks[1][1]], writes=obufs + [sm2_b])
                    S.op("dve", lambda e: e.tensor_tensor(out=sm2[:, 34:35], in0=sm2[:, 32:33], in1=sm2[:, 33:34], op=ALU.add), reads=[sm2_b], writes=[sm2_b])
                    S.op("act", lambda e: e.activation(out=sm2[:, 34:35], in_=sm2[:, 34:35], func=AF.Sqrt, scale=1.0 / D, bias=EPS), reads=[sm2_b], writes=[sm2_b])
                    S.op("dve", lambda e: e.reciprocal(out=sm2[:, 34:35], in_=sm2[:, 34:35]), reads=[sm2_b], writes=[sm2_b])
                    for g2 in range(2):
                        bk, bkb = bks[g2]
                        S.op("dve", lambda e, bk=bk, ov=ov, g2=g2: e.scalar_tensor_tensor(out=ov[:, g2 * 512:(g2 + 1) * 512], in0=bk[:], scalar=sm2[:, 34:35], in1=gfin[:, g2 * 512:(g2 + 1) * 512], op0=ALU.mult, op1=ALU.mult),
                             reads=[bkb, sm2_b, gfin_b] + obufs, writes=obufs)
                    S.dma("sp", "st", out_d[t0 + q * 128:t0 + (q + 1) * 128, :], ov, reads=obufs)
                for pg4 in o_pg:
                    pfree(pg4)
            if stop_after is not None and (l, j) == stop_after:
                break
        if stop_after is not None:
            break

    S.prog["sp"].append(lambda e: e.wait_ge(S.sems["st"], S.cnt["st"]))
    S.run()
    es.close()
    global LAST_SCHED
    LAST_SCHED = S
    S.min_free_pages = hw[0]
    return nc


def make_consts():
    inv = (np.float32(1.0) / np.power(np.float32(10000.0), np.arange(0, 32, 2, dtype=np.float32) / np.float32(32))).astype(np.float32)
    ropec = np.zeros((32, 2), np.float32)
    ropec[0:16, 0] = inv; ropec[16:32, 0] = inv
    ropec[0:16, 1] = -1.0; ropec[16:32, 1] = 1.0
    invc = np.zeros((128, 64), np.float32)
    for g, w in enumerate((2, 4, 8, 16)):
        for t in range(16):
            invc[:, g * 16 + t] = 1.0 / min(t + 1, w)
    return ropec, invc


def make_in_maps(inputs, cores):
    packs = [pack_layer(inputs, l) for l in range(DEPTH)]
    wflat = np.ascontiguousarray(np.concatenate([p_[0] for p_ in packs], 0))
    vecs = np.ascontiguousarray(np.stack([p_[1] for p_ in packs], 0))
    rows = np.ascontiguousarray(np.concatenate([p_[2] for p_ in packs], 1))
    gfin = np.asarray(inputs["g_final"], np.float32).reshape(1, D)
    ropec, invc = make_consts()
    x = np.asarray(inputs["x"], np.float32)
    p = np.asarray(inputs["p"], np.float32)
    pos = np.asarray(inputs["positions"]).astype(np.int32)
    maps = []
    for b in cores:
        maps.append({"x": np.ascontiguousarray(x[b]), "p": np.ascontiguousarray(p[:, b]), "pos": np.ascontiguousarray(pos[b:b + 1]),
                     "wflat": wflat, "vecs": vecs, "rows": rows, "gfin": gfin, "ropec": ropec, "invcnt": invc})
    return maps


def kernel(**inputs):
    nc = build_nc()
    maps = make_in_maps(inputs, list(range(8)))
    res = run_bass_kernel_spmd(nc, maps, core_ids=list(range(8)))
    out = np.stack([np.asarray(r["out"], np.float32) for r in res.results], 0)
    return out.astype(np.float32)
```
